# Optimizing a Trainium2 kernel written in Bass

```python
import jax, jax.numpy as jnp
from jax import lax
import numpy as np

D_MODEL = 1024
BATCH = 32
SEQ = 256
DEPTH = 2
DEC_BATCH = 2
DEC_SEQ = 4096
PAST_LEN = 512

GRID_W = 64
N_RET_HEADS = 4
RET_DK = 128
RET_DV = 128
RET_CHUNK = 128
N_HEADS = 8
N_KV_HEADS = 2
HEAD_DIM = 64
Q_BLOCK = 128
ROPE_THETA = 10000.0
CONV_DIM = 512
CONV_K = 3
FFN_DIM = 2816
N_MOD = 9
EPS = 1e-6
RET_QK_W = N_RET_HEADS * RET_DK
RET_V_W = N_RET_HEADS * RET_DV
ATT_Q_W = N_HEADS * HEAD_DIM
ATT_KV_W = N_KV_HEADS * HEAD_DIM
IN_SIZES = (RET_QK_W, RET_QK_W, RET_V_W, RET_V_W, ATT_Q_W, ATT_KV_W, ATT_KV_W,
            CONV_DIM, CONV_DIM, CONV_DIM, D_MODEL, D_MODEL, D_MODEL)
IN_COLS = sum(IN_SIZES)

kernel_name = "hybrid_prefix_diffusion_trunk_step"


def rms_norm(x, g):
    xf = x.astype(jnp.float32)
    y = xf * lax.rsqrt(jnp.mean(xf * xf, axis=-1, keepdims=True) + EPS)
    return (y * g.astype(jnp.float32)).astype(x.dtype)


def modulate_norm(x, g, shift, scale):
    return rms_norm(x, g) * (1 + scale) + shift


def swiglu(x, w_in, w_out):
    gate, up = jnp.split(x @ w_in, 2, axis=-1)
    return (jax.nn.silu(gate) * up) @ w_out


def axial_rope_tables(n_tok):
    rows = n_tok // GRID_W
    t_row = jnp.repeat(jnp.arange(rows, dtype=jnp.float32), GRID_W)
    t_col = jnp.tile(jnp.arange(GRID_W, dtype=jnp.float32), rows)
    n_freq = HEAD_DIM // 4
    inv = ROPE_THETA ** (-jnp.arange(n_freq, dtype=jnp.float32) / n_freq)
    ang = jnp.concatenate([t_row[:, None] * inv, t_col[:, None] * inv], axis=-1)
    return jnp.cos(ang), jnp.sin(ang)


def apply_rope(x, cos, sin):
    half = HEAD_DIM // 2
    shape = (1, cos.shape[0]) + (1,) * (x.ndim - 3) + (half,)
    c = cos.reshape(shape)
    s = sin.reshape(shape)
    xf = x.astype(jnp.float32)
    x1, x2 = xf[..., :half], xf[..., half:]
    return jnp.concatenate([x1 * c - x2 * s, x1 * s + x2 * c], axis=-1).astype(x.dtype)


def retention_chunked(q, k, v, log_gamma, s0):
    B, T, H, _ = q.shape
    DV = v.shape[-1]
    n = T // RET_CHUNK

    def chunks(a):
        return a.astype(jnp.float32).reshape(B, n, RET_CHUNK, H, a.shape[-1]).transpose(1, 0, 3, 2, 4)

    qc, kc, vc = chunks(q), chunks(k), chunks(v)
    pos = jnp.arange(RET_CHUNK, dtype=jnp.float32)
    lg = log_gamma.astype(jnp.float32)[:, None]
    diff = pos[:, None] - pos[None, :]
    decay_in = jnp.where(diff >= 0, jnp.exp(lg[:, :, None] * jnp.maximum(diff, 0.0)), 0.0)
    decay_q = jnp.exp(lg * (pos + 1.0))[..., None]
    decay_k = jnp.exp(lg * (RET_CHUNK - 1.0 - pos))[..., None]
    decay_c = jnp.exp(lg * RET_CHUNK)[..., None]

    def step(s, inp):
        qi, ki, vi = inp
        att = jnp.einsum('bhcd,bhmd->bhcm', qi, ki) * decay_in
        o = jnp.einsum('bhcm,bhme->bhce', att, vi) + jnp.einsum('bhcd,bhde->bhce', qi, s) * decay_q
        s = s * decay_c + jnp.einsum('bhmd,bhme->bhde', ki * decay_k, vi)
        return s, o

    s_fin, o = lax.scan(step, s0.astype(jnp.float32), (qc, kc, vc))
    o = o.transpose(1, 0, 3, 2, 4).reshape(B, T, H, DV)
    return o, s_fin


def blocked_attention(q, k, v):
    B, T, KV, G, Dh = q.shape
    nb = T // Q_BLOCK
    qb = q.reshape(B, nb, Q_BLOCK, KV, G, Dh).swapaxes(0, 1)
    scale = Dh ** -0.5

    def one_block(q_blk):
        s = jnp.einsum('bqkgd,bskd->bkgqs', q_blk, k, preferred_element_type=jnp.float32) * scale
        p = jax.nn.softmax(s, axis=-1).astype(v.dtype)
        return jnp.einsum('bkgqs,bskd->bqkgd', p, v)

    o = lax.map(one_block, qb)
    return o.swapaxes(0, 1).reshape(B, T, KV, G, Dh)


def short_conv3(u, w):
    up = jnp.pad(u, ((0, 0), (1, 1), (0, 0)))
    return up[:, :-2] * w[0] + up[:, 1:-1] * w[1] + up[:, 2:] * w[2]


def token_mix(xn, w_in, decay_logit, ret_gn, q_gain, k_gain, conv_w, w_ret_o, w_att_o, w_conv_o, w_o,
              ret_s0, kv_prefix, rope):
    B, T, _ = xn.shape
    parts = jnp.split(xn @ w_in, np.cumsum(IN_SIZES)[:-1].tolist(), axis=-1)
    rq, rk, rv, rg, aq, ak, av, cb, cc, cx, g_ret, g_att, g_conv = parts

    rq = rq.reshape(B, T, N_RET_HEADS, RET_DK)
    rk = rk.reshape(B, T, N_RET_HEADS, RET_DK) * (RET_DK ** -0.5)
    rv = rv.reshape(B, T, N_RET_HEADS, RET_DV)
    log_gamma = jax.nn.log_sigmoid(decay_logit.astype(jnp.float32))
    o_f, s_f = retention_chunked(rq, rk, rv, log_gamma[0], ret_s0[:, 0])
    o_b, s_b = retention_chunked(rq[:, ::-1], rk[:, ::-1], rv[:, ::-1], log_gamma[1], ret_s0[:, 1])
    o_r = (o_f + o_b[:, ::-1]).astype(xn.dtype)
    o_r = rms_norm(o_r, ret_gn.reshape(N_RET_HEADS, RET_DV)).reshape(B, T, RET_V_W)
    y_ret = (jax.nn.silu(rg) * o_r) @ w_ret_o
    ret_state = jnp.stack([s_f, s_b], axis=1).astype(xn.dtype)

    q = rms_norm(aq.reshape(B, T, N_KV_HEADS, N_HEADS // N_KV_HEADS, HEAD_DIM), q_gain)
    k = rms_norm(ak.reshape(B, T, N_KV_HEADS, HEAD_DIM), k_gain)
    v = av.reshape(B, T, N_KV_HEADS, HEAD_DIM)
    if rope is None:
        k_all, v_all = k, v
    else:
        q = apply_rope(q, rope[0], rope[1])
        k_all = jnp.concatenate([kv_prefix[0], apply_rope(k, rope[0], rope[1])], axis=1)
        v_all = jnp.concatenate([kv_prefix[1], v], axis=1)
    y_att = blocked_attention(q, k_all, v_all).reshape(B, T, ATT_Q_W) @ w_att_o

    y_conv = (cb * short_conv3(cc * cx, conv_w)) @ w_conv_o

    merged = (jax.nn.sigmoid(g_ret) * y_ret + jax.nn.sigmoid(g_att) * y_att
              + jax.nn.sigmoid(g_conv) * y_conv)
    return merged @ w_o, ret_state, k, v


def layer(h, cond, W, l, ret_s0, kv_prefix, rope):
    (w_ada, b_ada, norm_w, w_ffn_in, w_ffn_out, w_in, ret_decay_logit, ret_gn, q_gain, k_gain,
     conv_w, w_ret_o, w_att_o, w_conv_o, w_o) = W
    mod = (jax.nn.silu(cond) @ w_ada[l] + b_ada[l])[:, None, :]
    sh1, sc1, g1, sh2, sc2, g2, sh3, sc3, g3 = jnp.split(mod, N_MOD, axis=-1)
    h = h + 0.5 * g1 * swiglu(modulate_norm(h, norm_w[l, 0], sh1, sc1), w_ffn_in[l, 0], w_ffn_out[l, 0])
    y, ret_state, k, v = token_mix(modulate_norm(h, norm_w[l, 1], sh2, sc2), w_in[l], ret_decay_logit[l],
                                   ret_gn[l], q_gain[l], k_gain[l], conv_w[l], w_ret_o[l], w_att_o[l],
                                   w_conv_o[l], w_o[l], ret_s0, kv_prefix, rope)
    h = h + g2 * y
    h = h + 0.5 * g3 * swiglu(modulate_norm(h, norm_w[l, 2], sh3, sc3), w_ffn_in[l, 1], w_ffn_out[l, 1])
    return h, ret_state, k, v


def setup_inputs(seed: int = 0) -> dict:
    key = jax.random.key(seed)
    ks = jax.random.split(key, 24)
    f32 = jnp.float32

    def nrm(k, shape, s=1.0):
        return jax.random.normal(k, shape, f32) * s

    base_logit = jnp.log(2.0 ** (5.0 + jnp.arange(N_RET_HEADS, dtype=f32)) - 1.0)
    return {
        "x_prompt": nrm(ks[0], (BATCH, SEQ, D_MODEL)),
        "x_sample": nrm(ks[1], (DEC_BATCH, DEC_SEQ, D_MODEL)),
        "c": nrm(ks[2], (DEC_BATCH, D_MODEL)),
        "state_ret": nrm(ks[3], (DEC_BATCH, DEPTH, 2, N_RET_HEADS, RET_DK, RET_DV), 0.5),
        "cache_k": nrm(ks[4], (DEC_BATCH, DEPTH, PAST_LEN, N_KV_HEADS, HEAD_DIM)),
        "cache_v": nrm(ks[5], (DEC_BATCH, DEPTH, PAST_LEN, N_KV_HEADS, HEAD_DIM)),
        "c_ctx": nrm(ks[6], (D_MODEL,)),
        "w_ada": nrm(ks[7], (DEPTH, D_MODEL, N_MOD * D_MODEL), D_MODEL ** -0.5),
        "b_ada": nrm(ks[8], (DEPTH, N_MOD * D_MODEL), 0.02),
        "norm_w": 1.0 + nrm(ks[9], (DEPTH, 3, D_MODEL), 0.02),
        "w_ffn_in": nrm(ks[10], (DEPTH, 2, D_MODEL, 2 * FFN_DIM), D_MODEL ** -0.5),
        "w_ffn_out": nrm(ks[11], (DEPTH, 2, FFN_DIM, D_MODEL), FFN_DIM ** -0.5),
        "w_in": nrm(ks[12], (DEPTH, D_MODEL, IN_COLS), D_MODEL ** -0.5),
        "ret_decay_logit": base_logit + nrm(ks[13], (DEPTH, 2, N_RET_HEADS), 0.1),
        "ret_gn": 1.0 + nrm(ks[14], (DEPTH, RET_V_W), 0.02),
        "q_gain": 1.0 + nrm(ks[15], (DEPTH, HEAD_DIM), 0.02),
        "k_gain": 1.0 + nrm(ks[16], (DEPTH, HEAD_DIM), 0.02),
        "conv_w": nrm(ks[17], (DEPTH, CONV_K, CONV_DIM), CONV_K ** -0.5),
        "w_ret_o": nrm(ks[18], (DEPTH, RET_V_W, D_MODEL), RET_V_W ** -0.5),
        "w_att_o": nrm(ks[19], (DEPTH, ATT_Q_W, D_MODEL), ATT_Q_W ** -0.5),
        "w_conv_o": nrm(ks[20], (DEPTH, CONV_DIM, D_MODEL), CONV_DIM ** -0.5),
        "w_o": nrm(ks[21], (DEPTH, D_MODEL, D_MODEL), D_MODEL ** -0.5),
    }


def reference(x_prompt, x_sample, c, state_ret, cache_k, cache_v, c_ctx, w_ada, b_ada, norm_w, w_ffn_in,
              w_ffn_out, w_in, ret_decay_logit, ret_gn, q_gain, k_gain, conv_w, w_ret_o, w_att_o, w_conv_o, w_o):
    W = (w_ada, b_ada, norm_w, w_ffn_in, w_ffn_out, w_in, ret_decay_logit, ret_gn, q_gain, k_gain,
         conv_w, w_ret_o, w_att_o, w_conv_o, w_o)

    h = x_prompt
    zero_state = jnp.zeros((x_prompt.shape[0], 2, N_RET_HEADS, RET_DK, RET_DV), jnp.float32)
    states, keys, values = [], [], []
    for l in range(DEPTH):
        h, st, k, v = layer(h, c_ctx[None, :], W, l, zero_state, None, None)
        states.append(st)
        keys.append(k)
        values.append(v)
    y_prompt = h
    new_state_ret = jnp.stack(states, axis=1)
    new_cache_k = jnp.stack(keys, axis=1)
    new_cache_v = jnp.stack(values, axis=1)

    rope = axial_rope_tables(x_sample.shape[1])
    h = x_sample
    for l in range(DEPTH):
        h, _, _, _ = layer(h, c, W, l, state_ret[:, l], (cache_k[:, l], cache_v[:, l]), rope)
    y_sample = h
    return (y_prompt, y_sample, new_state_ret, new_cache_k, new_cache_v)
```

```python
import types
import numpy as np
from contextlib import ExitStack
import concourse.bass as bass
import concourse.mybir as mybir
from concourse.bass_utils import run_bass_kernel_spmd

F32, BF16 = mybir.dt.float32, mybir.dt.bfloat16
AF = mybir.ActivationFunctionType
ALU = mybir.AluOpType

D = 1024
T = 1024
SUB = 512
NS = T // SUB
FFN = 2816
NFC = FFN // 128
DEPTH = 2
EPS = 1e-6
NCORES = 8
PAST = 512
NKS = PAST + 4096
SAME_SYNC = True

O_RQ, O_RK, O_RV, O_RG, O_AQ, O_AK, O_AV, O_CB, O_CC, O_CX, O_GR, O_GA, O_GC = (
    0, 512, 1024, 1536, 2048, 2560, 2688, 2816, 3328, 3840, 4352, 5376, 6400)


def _sw(base):
    return list(range(base + 32, base + 64)) + list(range(base, base + 32))


def build_win_cols():
    cols = []
    for h in range(4):
        for o in (O_RQ, O_RK, O_RV, O_RG):
            cols += list(range(o + h * 128, o + (h + 1) * 128))
    for j in range(4):
        cols += list(range(O_AQ + j * 64, O_AQ + (j + 1) * 64))
        cols += list(range(O_AQ + (4 + j) * 64, O_AQ + (5 + j) * 64))
    for j in range(4):
        cols += _sw(O_AQ + j * 64) + _sw(O_AQ + (4 + j) * 64)
    cols += list(range(O_AK, O_AK + 128))
    cols += _sw(O_AK) + _sw(O_AK + 64)
    cols += list(range(O_AV, O_AV + 128))
    for c in range(4):
        for o in (O_CB, O_CC, O_CX):
            cols += list(range(o + c * 128, o + (c + 1) * 128))
    cols += list(range(O_GR, O_GR + 3072))
    return np.array(cols, dtype=np.int64)


X_RET = 0
X_AQ = 2048
X_AQS = 2560
X_AK = 3072
X_AKS = 3200
X_AV = 3328
X_CONV = 3456
X_GATE = 3456 + 1536
NWX = X_GATE + 3072

V_NORM = 0
V_GN = 48
V_CONV = 56
V_QG = 80
V_QGS = 82
V_KG = 84
V_KGS = 86
V_COND = 88
NV = 112


ENGS = ['pe', 'act', 'dve', 'pool', 'sp']


class Res:
    __slots__ = ('name', 'w', 'r', 'pw', 'pr', 'excl')

    def __init__(self, name=''):
        self.name = name
        self.w = {}
        self.r = {}
        self.pw = {}
        self.pr = {}
        self.excl = False


def _freeze(fn):
    if fn.__closure__ is None:
        return fn
    cells = []
    for c in fn.__closure__:
        try:
            cells.append(types.CellType(c.cell_contents))
        except ValueError:
            cells.append(c)
    return types.FunctionType(fn.__code__, fn.__globals__, fn.__name__, fn.__defaults__, tuple(cells))


def _merge(dst, src):
    for k, v in src.items():
        if k not in dst or dst[k][2] < v[2]:
            dst[k] = v


class Prog:
    def __init__(self, nc, stack):
        self.nc = nc
        self.q = {e: [] for e in ENGS}
        self.cnt = {e: 0 for e in ENGS}
        self.seen = {e: {} for e in ENGS}
        self.esem = {e: stack.enter_context(nc.semaphore("es_" + e)) for e in ENGS if e != 'sp'}
        self.dsem = []
        self.dcnt = []
        self.dpool = {}
        self.dnext = {}
        for qn, n in (('sp', 20), ('pool', 10)):
            ids = []
            for i in range(n):
                self.dsem.append(stack.enter_context(nc.semaphore("ds_%s%d" % (qn, i))))
                self.dcnt.append(0)
                ids.append(len(self.dsem) - 1)
            self.dpool[qn] = ids
            self.dnext[qn] = 0
        self.ccsem = stack.enter_context(nc.semaphore("cc"))
        self.cccnt = 0
        self.nwait = 0

    def _need(self, eng, dep):
        kind, ident, n = dep
        if kind == 'e':
            if ident == eng and (eng == 'pe' or not SAME_SYNC):
                return
            sem = self.esem[ident]
        elif kind == 'd':
            sem = self.dsem[ident]
        else:
            sem = self.ccsem
        key = (kind, ident)
        if self.seen[eng].get(key, 0) >= n:
            return
        self.seen[eng][key] = n
        self.nwait += 1
        self.q[eng].append(lambda e, s=sem, v=n: e.wait_ge(s, v))

    def _sync(self, eng, reads, writes, selfdep, waw):
        for r in reads:
            for d in r.w.values():
                if d != selfdep:
                    self._need(eng, d)
            if r.excl:
                for d in list(r.r.values()):
                    if d != selfdep and d[1] != eng:
                        self._need(eng, d)
        for w in writes:
            if w.r:
                w.pw, w.pr = w.w, w.r
                w.w, w.r = {}, {}
            elif waw:
                for d in w.w.values():
                    if d != selfdep:
                        self._need(eng, d)
            for d in list(w.pw.values()) + list(w.pr.values()):
                if d != selfdep:
                    self._need(eng, d)

    def op(self, eng, fn, reads=(), writes=(), inc=True, waw=True):
        fn = _freeze(fn)
        n = self.cnt[eng] + 1
        dep = ('e', eng, n)
        self._sync(eng, reads, writes, dep, waw)
        if inc:
            self.cnt[eng] = n
            sem = self.esem[eng]
            self.q[eng].append(lambda e, f=fn, s=sem: f(e).then_inc(s, 1))
        else:
            self.q[eng].append(lambda e, f=fn: f(e))
        for r in reads:
            r.r[('e', eng)] = dep
        for w in writes:
            w.w[('e', eng)] = dep

    def dma(self, queue, out, in_, reads=(), writes=(), waw=False, **kw):
        pool = self.dpool[queue]
        si = pool[self.dnext[queue] % len(pool)]
        self.dnext[queue] += 1
        if self.dcnt[si] > 0:
            self._need(queue, ('d', si, self.dcnt[si]))
        self._sync(queue, reads, writes, None, waw)
        self.dcnt[si] += 16
        n = self.dcnt[si]
        sem = self.dsem[si]
        self.q[queue].append(lambda e, o=out, i=in_, s=sem, k=kw: e.dma_start(out=o, in_=i, **k).then_inc(s, 16))
        dep = ('d', si, n)
        for r in reads:
            r.r[('d', si)] = dep
        for w in writes:
            w.w[('d', si)] = dep

    def collective(self, fn, reads, writes):
        fn = _freeze(fn)
        self._sync('pool', reads, writes, None, True)
        self.cccnt += 1
        n = self.cccnt
        sem = self.ccsem
        self.q['pool'].append(lambda e, f=fn, s=sem: f(e).then_inc(s, 1))
        dep = ('c', 0, n)
        for r in reads:
            r.r[('c', 0)] = dep
        for w in writes:
            w.w[('c', 0)] = dep

    def finish(self):
        for si, c in enumerate(self.dcnt):
            if c > 0:
                self._need('sp', ('d', si, c))
        for e in ('pe', 'act', 'dve', 'pool'):
            if self.cnt[e] > 0:
                self._need('sp', ('e', e, self.cnt[e]))
        if self.cccnt:
            self._need('sp', ('c', 0, self.cccnt))

    def emit(self):
        nc = self.nc
        q = self.q
        with nc.Block() as block:
            @block.tensor
            def _(e):
                for f in q['pe']:
                    f(e)

            @block.scalar
            def _(e):
                for f in q['act']:
                    f(e)

            @block.vector
            def _(e):
                for f in q['dve']:
                    f(e)

            @block.gpsimd
            def _(e):
                for f in q['pool']:
                    f(e)

            @block.sync
            def _(e):
                for f in q['sp']:
                    f(e)


class Arena:
    def __init__(self, tensor, size):
        self.t = tensor
        self.size = size
        self.top = 0
        self.live = []
        self.dead = []
        self.peak = 0

    def alloc(self, nwords, name=''):
        s = self.top
        e = s + nwords
        assert e <= self.size, "arena overflow %s %d > %d" % (name, e, self.size)
        self.top = e
        self.peak = max(self.peak, e)
        res = Res(name)
        keep = []
        for (a, b, r) in self.dead:
            if a < e and b > s:
                for dd in (r.w, r.r, r.pw, r.pr):
                    _merge(res.pr, dd)
                if a < s or b > e:
                    keep.append((a, b, r))
            else:
                keep.append((a, b, r))
        self.dead = keep
        self.live.append((s, e, res))
        return self.t[:, s:e], res

    def f32(self, shape, name=''):
        n = int(np.prod(shape[1:]))
        ap, res = self.alloc(n, name)
        ap = _shape(ap, shape)
        if shape[0] < 128:
            ap = ap[0:shape[0]]
        return ap, res

    def bf(self, shape, name=''):
        n = int(np.prod(shape[1:]))
        ap, res = self.alloc((n + 1) // 2, name)
        ap = ap.bitcast(BF16)[:, 0:n]
        ap = _shape(ap, shape)
        if shape[0] < 128:
            ap = ap[0:shape[0]]
        return ap, res

    def mark(self):
        return self.top

    def release(self, mark):
        keep = []
        for (s, e, r) in self.live:
            if s >= mark:
                self.dead.append((s, e, r))
            else:
                keep.append((s, e, r))
        self.live = keep
        self.top = mark


def bcast_last(ap, n):
    return bass.AP(ap.tensor, ap.offset, [list(d) for d in ap.ap] + [[0, n]])


def bcast_mid(ap, reps):
    d = [list(x) for x in ap.ap]
    return bass.AP(ap.tensor, ap.offset, [d[0], [0, reps]] + d[1:])


def _shape(ap, shape):
    if len(shape) == 2:
        return ap
    if len(shape) == 3:
        return ap.rearrange("p (a b) -> p a b", a=shape[1])
    if len(shape) == 4:
        return ap.rearrange("p (a b c) -> p a b c", a=shape[1], b=shape[2])
    raise ValueError(shape)


def build_program(enable_sample=True, debug=None):
    nc = bass.Bass("TRN2", target_bir_lowering=False)
    dbg_out = {}

    def din(name, shape, dt=F32):
        return nc.dram_tensor(name, list(shape), dt, kind="ExternalInput").ap()

    def dout(name, shape, dt=F32):
        return nc.dram_tensor(name, list(shape), dt, kind="ExternalOutput").ap()

    xin = [din("xp", [T, D]), din("xs", [T, D])]
    vecs_d = din("vecs", [NV, 128])
    bada_d = din("bada", [DEPTH * 9, 128])
    decay_d = din("decay", [DEPTH * 2 * 4])
    ident_d = din("ident", [128, 128])
    posd_d = din("posd", [4, 128, 128])
    posv_d = din("posv", [128, 4])
    posr_d = din("posr", [4, 128])
    rope_d = din("rope", [2, 128, T])
    sret_d = din("sret", [DEPTH, 2, 4, 128, 128])
    ck_d = din("ck", [DEPTH, PAST, 128])
    cv_d = din("cv", [DEPTH, PAST, 128])
    w_ada = din("w_ada", [DEPTH, D, 9 * D // NCORES])
    w_ffn_in = din("w_ffn_in", [DEPTH, 2, D, 2 * FFN])
    w_ffn_out = din("w_ffn_out", [DEPTH, 2, FFN, D])
    w_inx = din("w_inx", [DEPTH, D, NWX])
    w_ret_o = din("w_ret_o", [DEPTH, 512, D])
    w_att_o = din("w_att_o", [DEPTH, 512, D])
    w_conv_o = din("w_conv_o", [DEPTH, 512, D])
    w_o = din("w_o", [DEPTH, D, D])

    y_out = [dout("yp", [T, D]), dout("ys", [T, D])]
    nstate = dout("nstate", [4, DEPTH, 2, 4, 128, 128])
    nck = dout("nck", [4, DEPTH, 256, 128])
    ncv = dout("ncv", [4, DEPTH, 256, 128])

    ctab_d = din("ctab", [64])
    NXS = 1024
    NXK = 1024 + 1024 + 8
    sndS = [nc.dram_tensor("sndS%d" % l, [NXS, 128], BF16) for l in range(DEPTH)]
    rcvS = [nc.dram_tensor("rcvS%d" % l, [NCORES * NXS, 128], BF16) for l in range(DEPTH)]
    sndK = [nc.dram_tensor("sndK%d" % l, [NXK, 128], BF16) for l in range(DEPTH)]
    rcvK = [nc.dram_tensor("rcvK%d" % l, [NCORES * NXK, 128], BF16) for l in range(DEPTH)]

    stack = ExitStack()
    with stack:
        P = Prog(nc, stack)
        ARENA_WORDS = 53200
        arena_t = stack.enter_context(nc.sbuf_tensor("arena", [128, ARENA_WORDS], F32))
        A = Arena(arena_t, ARENA_WORDS)
        banks = []
        ps_all = stack.enter_context(nc.psum_tensor("ps", [128, 8 * 512], F32))
        for i in range(8):
            banks.append((ps_all[:, i * 512:(i + 1) * 512], Res("bank%d" % i)))
            banks[-1][1].excl = True
        bstate = {'i': 0}

        def bank():
            b = banks[bstate['i'] % 6]
            bstate['i'] += 1
            return b

        ident, ident_r = A.f32([128, 128], "ident")
        P.dma('sp', ident, ident_d[:, :], writes=[ident_r])
        ones_bf, ones_r = A.bf([128, 128], "ones")
        P.op('dve', lambda e: e.memset(ones_bf, 1.0), writes=[ones_r])
        blk_bf, blk_r = A.bf([128, 128], "blk")
        P.op('dve', lambda e: e.memset(blk_bf, 0.0), writes=[blk_r])
        P.op('dve', lambda e: e.memset(blk_bf[0:64, 0:64], 1.0), writes=[blk_r])
        P.op('dve', lambda e: e.memset(blk_bf[64:128, 64:128], 1.0), writes=[blk_r])

        def transpose_rows(src_dram, nrows, name):
            dst, dst_r = A.f32([128, nrows], name)
            m = A.mark()
            st, st_r = A.f32([128, 128], name + "_st")
            P.dma('sp', st[0:nrows, :], src_dram, writes=[st_r])
            bk, bk_r = bank()
            P.op('pe', lambda e: e.transpose(bk[:, 0:nrows], st[0:nrows, :], ident[0:nrows, 0:nrows]),
                 reads=[st_r, ident_r], writes=[bk_r])
            P.op('dve', lambda e: e.tensor_copy(out=dst, in_=bk[:, 0:nrows]), reads=[bk_r], writes=[dst_r])
            A.release(m)
            return dst, dst_r

        vecs, vecs_r = transpose_rows(vecs_d[:, :], NV, "vecs")
        badap, badap_r = transpose_rows(bada_d[:, :], DEPTH * 9, "badap")

        def vcol(r):
            return vecs[:, r:r + 1]

        ct0, ct0_r = A.f32([128, 64], "ctab")
        P.dma('sp', ct0, ctab_d.partition_broadcast(128), writes=[ct0_r])

        scond, scond_r = A.bf([128, 8, 3], "scond")
        for ci in range(3):
            P.op('act', lambda e, ci=ci: e.activation(out=scond[:, :, ci], in_=vecs[:, V_COND + ci * 8:V_COND + ci * 8 + 8],
                                                      func=AF.Silu), reads=[vecs_r], writes=[scond_r], waw=False)

        lgt, lgt_r = A.f32([128, 16], "lgt")
        P.dma('sp', lgt, decay_d.partition_broadcast(128), writes=[lgt_r])
        lg, lg_r = A.f32([128, 16], "lg")
        P.op('act', lambda e: e.activation(out=lg, in_=lgt, func=AF.Exp, scale=-1.0), reads=[lgt_r], writes=[lg_r])
        P.op('act', lambda e: e.activation(out=lg, in_=lg, func=AF.Ln, bias=1.0), reads=[lg_r], writes=[lg_r])
        P.op('dve', lambda e: e.tensor_scalar(out=lg, in0=lg, scalar1=-1.0, scalar2=None, op0=ALU.mult),
             reads=[lg_r], writes=[lg_r])
        posd, posd_r = A.f32([128, 4, 128], "posd")
        P.dma('sp', posd, posd_d.rearrange("k p m -> p k m"), writes=[posd_r])
        posv, posv_r = A.f32([128, 4], "posv")
        P.dma('sp', posv, posv_d[:, :], writes=[posv_r])
        posr, posr_r = A.f32([128, 4, 128], "posr")
        P.dma('sp', posr, posr_d.partition_broadcast(128), writes=[posr_r])

        maskT, maskT_r = A.f32([128, DEPTH * 4, 128], "maskT")
        dkv, dkv_r = A.f32([128, DEPTH * 4, 2], "dkv")
        qrow, qrow_r = A.f32([128, DEPTH * 4, 2, 128], "qrow")
        dcc, dcc_r = A.f32([128, DEPTH * 4, 2], "dcc")
        KS = 128.0 ** -0.5
        m0 = A.mark()
        tmpm, tmpm_r = A.f32([128, 128], "tmpm")
        tmp2, tmp2_r = A.f32([128, 128], "tmp2")
        for l in range(DEPTH):
            for h in range(4):
                lh = l * 4 + h
                cf = lg[:, l * 8 + h:l * 8 + h + 1]
                cb = lg[:, l * 8 + 4 + h:l * 8 + 4 + h + 1]
                P.op('act', lambda e, cf=cf: e.activation(out=tmpm, in_=posd[:, 0, :], func=AF.Exp, scale=cf),
                     reads=[posd_r, lg_r], writes=[tmpm_r])
                P.op('dve', lambda e: e.tensor_tensor(out=tmpm, in0=tmpm, in1=posd[:, 2, :], op=ALU.mult),
                     reads=[tmpm_r, posd_r], writes=[tmpm_r])
                P.op('act', lambda e, cb=cb: e.activation(out=tmp2, in_=posd[:, 1, :], func=AF.Exp, scale=cb),
                     reads=[posd_r, lg_r], writes=[tmp2_r])
                P.op('dve', lambda e: e.tensor_tensor(out=tmp2, in0=tmp2, in1=posd[:, 3, :], op=ALU.mult),
                     reads=[tmp2_r, posd_r], writes=[tmp2_r])
                P.op('dve', lambda e, lh=lh: e.tensor_tensor(out=maskT[:, lh, :], in0=tmpm, in1=tmp2, op=ALU.add),
                     reads=[tmpm_r, tmp2_r], writes=[maskT_r])
                P.op('dve', lambda e, lh=lh: e.tensor_tensor(out=maskT[:, lh, :], in0=maskT[:, lh, :], in1=ident, op=ALU.add),
                     reads=[maskT_r, ident_r], writes=[maskT_r])
                P.op('act', lambda e, cf=cf, lh=lh: e.activation(out=dkv[:, lh, 0:1], in_=posv[:, 1:2], func=AF.Exp, scale=cf),
                     reads=[posv_r, lg_r], writes=[dkv_r])
                P.op('act', lambda e, cb=cb, lh=lh: e.activation(out=dkv[:, lh, 1:2], in_=posv[:, 0:1], func=AF.Exp, scale=cb),
                     reads=[posv_r, lg_r], writes=[dkv_r])
                P.op('act', lambda e, cf=cf, lh=lh: e.activation(out=qrow[:, lh, 0, :], in_=posr[:, 2, :], func=AF.Exp, scale=cf),
                     reads=[posr_r, lg_r], writes=[qrow_r])
                P.op('act', lambda e, cb=cb, lh=lh: e.activation(out=qrow[:, lh, 1, :], in_=posr[:, 3, :], func=AF.Exp, scale=cb),
                     reads=[posr_r, lg_r], writes=[qrow_r])
                P.op('act', lambda e, cf=cf, lh=lh: e.activation(out=dcc[:, lh, 0:1], in_=cf, func=AF.Exp, scale=128.0),
                     reads=[lg_r], writes=[dcc_r])
                P.op('act', lambda e, cb=cb, lh=lh: e.activation(out=dcc[:, lh, 1:2], in_=cb, func=AF.Exp, scale=128.0),
                     reads=[lg_r], writes=[dcc_r])
        P.op('dve', lambda e: e.tensor_scalar(out=dkv, in0=dkv, scalar1=KS, scalar2=None, op0=ALU.mult),
             reads=[dkv_r], writes=[dkv_r])
        A.release(m0)

        SLOT_ELEMS = 22 * 256
        NSLOT = 3
        slots = [A.bf([128, SLOT_ELEMS], "wslot%d" % i) for i in range(NSLOT)]
        wst = {'i': 0}

        stash = {}

        def prefetch_panel(key, parts):
            stash[key] = load_panel(parts)

        def load_panel(parts, key=None):
            if key is not None and key in stash:
                return stash.pop(key)
            sl, sl_r = slots[wst['i'] % NSLOT]
            wst['i'] += 1
            for (off, kc, ncol, src) in parts:
                dst = sl[:, off:off + kc * ncol].rearrange("p (k n) -> p k n", k=kc)
                P.dma('pool', dst, src, writes=[sl_r])
            return sl, sl_r

        def wsrc(w2d, kc, c0, ncol):
            return w2d.rearrange("(k p) n -> p k n", p=128)[:, :, c0:c0 + ncol]

        def pview(sl, off, kc, ncol):
            return sl[:, off:off + kc * ncol].rearrange("p (k n) -> p k n", k=kc)

        h, _ = A.f32([128, 8, T], "h")
        h_r = [Res("h%d" % c) for c in range(8)]
        modsL = [A.f32([128, 72, 3], "mods%d" % l) for l in range(DEPTH)]
        modAL = [[A.f32([128, 3, 8], "modA%d%d" % (l, ci)) for ci in range(2)] for l in range(DEPTH)]
        modGL = [[A.f32([128, 3, 8], "modG%d%d" % (l, ci)) for ci in range(2)] for l in range(DEPTH)]
        cur = {}
        base_mark = A.mark()

        def dump(name, ap, res, shape):
            d = dout("dbg_" + name, shape)
            dbg_out[name] = shape
            P.dma('sp', d, ap, reads=[res])

        sndM = nc.dram_tensor("sndM", [128, DEPTH * 27], F32)
        rcvM = nc.dram_tensor("rcvM", [NCORES * 128, DEPTH * 27], F32)

        def compute_mods_all():
            m = A.mark()
            modp, modp_r = A.f32([128, DEPTH * 9, 3], "modp")
            bk, bk_r = bank()
            for l in range(DEPTH):
                for pn in range(3):
                    sl, sl_r = load_panel([(0, 8, 384, wsrc(w_ada[l], 8, pn * 384, 384))])
                    wv = pview(sl, 0, 8, 384)
                    for j in range(3):
                        col = (l * 9 + pn * 3 + j) * 3
                        for kc in range(8):
                            P.op('pe', lambda e: e.matmul(
                                bk[:, col:col + 3], wv[:, kc, j * 128:(j + 1) * 128], scond[:, kc, :],
                                start=(kc == 0), stop=(kc == 7)),
                                reads=[sl_r, scond_r], writes=[bk_r], inc=(kc == 7))
            P.op('dve', lambda e: e.tensor_tensor(
                out=modp, in0=bk[:, 0:DEPTH * 27].rearrange("p (a b) -> p a b", b=3), in1=bcast_last(badap, 3), op=ALU.add),
                reads=[bk_r, badap_r], writes=[modp_r])
            sm_r, rm_r = Res("sndM"), Res("rcvM")
            P.dma('sp', sndM.ap(), modp.rearrange("p a b -> p (a b)"), reads=[modp_r], writes=[sm_r])
            all_gather(sndM, rcvM, sm_r, rm_r)
            rv = rcvM.ap().rearrange("(q p) (l x) -> q l p x", q=NCORES, l=DEPTH)
            for l in range(DEPTH):
                mods, mods_r = modsL[l]
                for q in range(NCORES):
                    P.dma('sp', mods[:, q * 9:(q + 1) * 9, :].rearrange("p a b -> p (a b)"), rv[q, l], reads=[rm_r], writes=[mods_r])
            A.release(m)
            for l in range(DEPTH):
                mods, mods_r = modsL[l]
                P.op('dve', lambda e: e.tensor_scalar(out=mods[:, :, 1], in0=mods[:, :, 1], scalar1=ct0[:, 0:1], scalar2=None, op0=ALU.mult),
                     reads=[mods_r, ct0_r], writes=[mods_r])
                P.op('dve', lambda e: e.scalar_tensor_tensor(out=mods[:, :, 1], in0=mods[:, :, 2], scalar=ct0[:, 1:2], in1=mods[:, :, 1],
                                                             op0=ALU.mult, op1=ALU.add), reads=[mods_r, ct0_r], writes=[mods_r])
                derive_mods(l)

        def derive_mods(l):
            mods, mods_r = modsL[l]
            for ci in range(2):
                modA, modA_r = modAL[l][ci]
                modG, modG_r = modGL[l][ci]
                for i in range(3):
                    nw = vecs[:, V_NORM + (l * 3 + i) * 8:V_NORM + (l * 3 + i) * 8 + 8]
                    sc = mods[:, (3 * i + 1) * 8:(3 * i + 1) * 8 + 8, ci]
                    gt = mods[:, (3 * i + 2) * 8:(3 * i + 2) * 8 + 8, ci]
                    P.op('dve', lambda e: e.scalar_tensor_tensor(
                        out=modA[:, i, :], in0=sc, scalar=1.0, in1=nw, op0=ALU.add, op1=ALU.mult),
                        reads=[mods_r, vecs_r], writes=[modA_r])
                    P.op('dve', lambda e: e.tensor_scalar(
                        out=modG[:, i, :], in0=gt, scalar1=(1.0 if i == 1 else 0.5), scalar2=None, op0=ALU.mult),
                        reads=[mods_r], writes=[modG_r])

        def use_mods(l, ci):
            cur['mods'], cur['mods_r'] = modsL[l]
            cur['modA'], cur['modA_r'] = modAL[l][ci]
            cur['modG'], cur['modG_r'] = modGL[l][ci]

        def load_x(g):
            m = A.mark()
            xv = xin[g].rearrange("(t p) d -> p t d", p=128)
            sts = [A.f32([128, D], "xst%d" % k) for k in range(2)]
            for tt in range(8):
                st, st_r = sts[tt % 2]
                P.dma('sp', st, xv[:, tt, :], writes=[st_r])
                for half in range(2):
                    bk, bk_r = bank()
                    for j in range(4):
                        c = half * 4 + j
                        P.op('pe', lambda e, c=c, j=j, st=st, bk=bk: e.transpose(
                            bk[:, j * 128:(j + 1) * 128], st[:, c * 128:(c + 1) * 128], ident),
                            reads=[st_r, ident_r], writes=[bk_r])
                    P.op('dve' if half else 'act',
                         (lambda e, half=half, bk=bk, tt=tt: e.tensor_copy(
                             out=h[:, half * 4:half * 4 + 4, tt * 128:(tt + 1) * 128],
                             in_=bk.rearrange("p (a b) -> p a b", a=4))) if half else
                         (lambda e, half=half, bk=bk, tt=tt: e.copy(
                             out=h[:, half * 4:half * 4 + 4, tt * 128:(tt + 1) * 128],
                             in_=bk.rearrange("p (a b) -> p a b", a=4))),
                         reads=[bk_r], writes=h_r[half * 4:half * 4 + 4], waw=False)
            A.release(m)

        def store_y(g):
            m = A.mark()
            yv = y_out[g].rearrange("(t p) d -> p t d", p=128)
            sts = [A.f32([128, D], "yst%d" % i) for i in range(2)]
            for tt in range(8):
                st, st_r = sts[tt % 2]
                for half in range(2):
                    bk, bk_r = bank()
                    for j in range(4):
                        c = half * 4 + j
                        P.op('pe', lambda e, c=c, j=j, bk=bk, tt=tt: e.transpose(
                            bk[:, j * 128:(j + 1) * 128], h[:, c, tt * 128:(tt + 1) * 128], ident),
                            reads=[h_r[c], ident_r], writes=[bk_r])
                    if half:
                        P.op('dve', lambda e, bk=bk, st=st, half=half: e.tensor_copy(out=st[:, half * 512:(half + 1) * 512], in_=bk),
                             reads=[bk_r], writes=[st_r], waw=False)
                    else:
                        P.op('act', lambda e, bk=bk, st=st, half=half: e.copy(out=st[:, half * 512:(half + 1) * 512], in_=bk),
                             reads=[bk_r], writes=[st_r], waw=False)
                P.dma('sp', yv[:, tt, :], st, reads=[st_r])
            A.release(m)

        ssq = {'pend': [], 'k': 0}

        def emit_sumsq(c, s):
            ts = slice(s * SUB, (s + 1) * SUB)
            sqa, sqr = sqp[ssq['k'] % len(sqp)]
            ssq['k'] += 1
            P.op('act', lambda e: e.activation(out=sqa, in_=h[:, c, ts], func=AF.Square), reads=[h_r[c]], writes=[sqr])
            bk, bk_r = banks[6 + s]

            def mm():
                P.op('pe', lambda e: e.matmul(bk, ones_bf, sqa, start=(c == 0), stop=(c == 7)), reads=[sqr, ones_r], writes=[bk_r])
            ssq['pend'].append(mm)

        def flush_ss(keep=0):
            while len(ssq['pend']) > keep:
                ssq['pend'].pop(0)()

        def norm_stage(i, ci, xn, xn_r, have_ss=True):
            m = A.mark()
            rstd = [A.f32([128, SUB], "rstd%d" % k) for k in range(2)]
            tmp = [A.f32([128, SUB], "ntmp%d" % k) for k in range(4)]
            if not have_ss:
                for s in range(NS):
                    for c in range(8):
                        emit_sumsq(c, s)
                        flush_ss(keep=1)
            flush_ss()
            modA, modA_r, mods, mods_r = cur['modA'], cur['modA_r'], cur['mods'], cur['mods_r']
            for s in range(NS):
                ts = slice(s * SUB, (s + 1) * SUB)
                bk, bk_r = banks[6 + s]
                rs, rs_r = rstd[s]
                t0a, t0r = tmp[2 * s]
                t1a, t1r = tmp[2 * s + 1]
                P.op('act', lambda e: e.activation(out=t0a, in_=bk, func=AF.Ln, scale=1.0 / D, bias=eps_ap),
                     reads=[bk_r, eps_r], writes=[t0r])
                P.op('act', lambda e: e.activation(out=rs, in_=t0a, func=AF.Exp, scale=-0.5), reads=[t0r], writes=[rs_r])
            for s in range(NS):
                ts = slice(s * SUB, (s + 1) * SUB)
                rs, rs_r = rstd[s]
                for c in range(8):
                    ta, tr = tmp[c % 4]
                    P.op('dve', lambda e: e.scalar_tensor_tensor(
                        out=ta, in0=h[:, c, ts], scalar=modA[:, i, c:c + 1], in1=rs, op0=ALU.mult, op1=ALU.mult),
                        reads=[h_r[c], modA_r, rs_r], writes=[tr])
                    P.op('act', lambda e: e.activation(
                        out=xn[:, c, ts], in_=ta, func=AF.Identity, bias=mods[:, 3 * i * 8 + c, ci:ci + 1], scale=1.0),
                        reads=[tr, mods_r], writes=[xn_r], waw=False)
            A.release(m)

        def h_update(bk, bk_r, i, c, ts, s):
            modG, modG_r = cur['modG'], cur['modG_r']
            P.op('dve', lambda e: e.scalar_tensor_tensor(
                out=h[:, c, ts], in0=bk, scalar=modG[:, i, c:c + 1], in1=h[:, c, ts], op0=ALU.mult, op1=ALU.add),
                reads=[bk_r, modG_r, h_r[c]], writes=[h_r[c]])
            emit_sumsq(c, s)
            flush_ss(keep=2)

        def ffn_stage(l, f, i, xn, xn_r):
            m = A.mark()
            hid, hid_r = A.bf([128, NFC, T], "hid")
            stm = [A.f32([128, SUB], "silu%d" % k) for k in range(3)]
            wi = w_ffn_in[l, f]
            k = 0
            for u in range(NFC // 2):
                sl, sl_r = load_panel([(0, 8, 256, wsrc(wi, 8, u * 256, 256)),
                                       (8 * 256, 8, 256, wsrc(wi, 8, FFN + u * 256, 256))])
                gv = pview(sl, 0, 8, 256)
                uv = pview(sl, 8 * 256, 8, 256)
                for j in range(2):
                    fc = u * 2 + j
                    for s in range(NS):
                        ts = slice(s * SUB, (s + 1) * SUB)
                        bg, bg_r = bank()
                        for kc in range(8):
                            P.op('pe', lambda e, bg=bg, gv=gv, kc=kc, j=j, ts=ts: e.matmul(
                                bg, gv[:, kc, j * 128:(j + 1) * 128], xn[:, kc, ts], start=(kc == 0), stop=(kc == 7)),
                                reads=[sl_r, xn_r], writes=[bg_r], inc=(kc == 7))
                        bu, bu_r = bank()
                        for kc in range(8):
                            P.op('pe', lambda e, bu=bu, uv=uv, kc=kc, j=j, ts=ts: e.matmul(
                                bu, uv[:, kc, j * 128:(j + 1) * 128], xn[:, kc, ts], start=(kc == 0), stop=(kc == 7)),
                                reads=[sl_r, xn_r], writes=[bu_r], inc=(kc == 7))
                        sa, sr = stm[k % 3]
                        k += 1
                        P.op('act', lambda e, sa=sa, bg=bg: e.activation(out=sa, in_=bg, func=AF.Silu),
                             reads=[bg_r], writes=[sr])
                        P.op('dve', lambda e, sa=sa, bu=bu, fc=fc, ts=ts: e.tensor_tensor(
                            out=hid[:, fc, ts], in0=bu, in1=sa, op=ALU.mult),
                            reads=[bu_r, sr], writes=[hid_r], waw=False)
            wo = w_ffn_out[l, f]
            for pn in range(4):
                sl, sl_r = load_panel([(0, NFC, 256, wsrc(wo, NFC, pn * 256, 256))])
                wv = pview(sl, 0, NFC, 256)
                for j in range(2):
                    c = pn * 2 + j
                    for s in range(NS):
                        ts = slice(s * SUB, (s + 1) * SUB)
                        bk, bk_r = bank()
                        for kc in range(NFC):
                            P.op('pe', lambda e, bk=bk, wv=wv, kc=kc, j=j, ts=ts: e.matmul(
                                bk, wv[:, kc, j * 128:(j + 1) * 128], hid[:, kc, ts], start=(kc == 0), stop=(kc == NFC - 1)),
                                reads=[sl_r, hid_r], writes=[bk_r], inc=(kc == NFC - 1))
                        h_update(bk, bk_r, i, c, ts, s)
            flush_ss()
            A.release(m)

        eps_ap, eps_r = None, None

        def setup_eps():
            nonlocal eps_ap, eps_r
            eps_ap, eps_r = A.f32([128, 1], "eps")
            P.op('dve', lambda e: e.memset(eps_ap, EPS), writes=[eps_r])

        def proj_fm(sl_r, wv, col0, xn, xn_r, s, kcn=8):
            ts = slice(s * SUB, (s + 1) * SUB)
            bk, bk_r = bank()
            for kc in range(kcn):
                P.op('pe', lambda e, bk=bk, kc=kc: e.matmul(
                    bk, wv[:, kc, col0:col0 + 128], xn[:, kc, ts], start=(kc == 0), stop=(kc == kcn - 1)),
                    reads=[sl_r, xn_r], writes=[bk_r], inc=(kc == kcn - 1))
            return bk, bk_r

        def retention(l, g, xn, xn_r, orn, orn_r, seeds=None, phase1=None):
            wi = w_inx[l]
            segs = [(sq_ * 2, 2) for sq_ in range(4)] if g == 0 else [(0, 8)]
            m = A.mark()
            nset = 1 if phase1 is not None else 2
            sets = []
            for k_ in range(nset):
                B = {}
                for nm in ("vtok", "kdf", "kdb"):
                    B[nm] = A.bf([128, 8, 128], nm + str(k_))
                B["Sbf"] = A.bf([128, 8, 2, 128], "Sbf%d" % k_)
                B["S32"] = A.f32([128, 2, 2, 128], "S32r%d" % k_)
                if phase1 is None:
                    for nm in ("qT", "kT", "sg", "qdf", "qdb"):
                        B[nm] = A.bf([128, T], nm + str(k_))
                    B["am"] = A.bf([128, 8, 128], "am%d" % k_)
                sets.append(B)
            P32, P32_r = A.f32([128, 8, 2, 128], "P32")
            if g == 0:
                stout, stout_r = A.f32([128, 4, 2, 128], "stout")
            if phase1 is None:
                osq = [A.bf([128, SUB], "osq%d" % k_) for k_ in range(2)]
                ors = [A.f32([128, SUB], "ors%d" % k_) for k_ in range(2)]
                otm = [A.f32([128, SUB], "otm%d" % k_) for k_ in range(2)]
            state = {}

            def front(hh):
                lh = l * 4 + hh
                B = sets[hh % nset]
                vtok, vtok_r = B["vtok"]
                kdf, kdf_r = B["kdf"]
                kdb, kdb_r = B["kdb"]
                Sbf, Sbf_r = B["Sbf"]
                S32, S32_r = B["S32"]
                sl, sl_r = load_panel([(0, 8, 512, wsrc(wi, 8, X_RET + hh * 512, 512))], key=("ret", l, hh, phase1 is None))
                wv = pview(sl, 0, 8, 512)
                for tp in range(4):
                    bk, bk_r = bank()
                    for q2 in range(2):
                        tt = tp * 2 + q2
                        for kc in range(8):
                            P.op('pe', lambda e: e.matmul(
                                bk[:, q2 * 256:(q2 + 1) * 256], xn[:, kc, tt * 128:(tt + 1) * 128], wv[:, kc, 128:384],
                                start=(kc == 0), stop=(kc == 7)),
                                reads=[sl_r, xn_r], writes=[bk_r], inc=(kc == 7))
                    b3 = bk.rearrange("p (a b) -> p a b", a=2)
                    P.op('act', lambda e: e.activation(
                        out=kdf[:, tp * 2:tp * 2 + 2, :], in_=b3[:, :, 0:128], func=AF.Copy, scale=dkv[:, lh, 0:1]),
                        reads=[bk_r, dkv_r], writes=[kdf_r], waw=False)
                    P.op('dve', lambda e: e.tensor_scalar(
                        out=kdb[:, tp * 2:tp * 2 + 2, :], in0=b3[:, :, 0:128], scalar1=dkv[:, lh, 1:2], scalar2=None, op0=ALU.mult),
                        reads=[bk_r, dkv_r], writes=[kdb_r], waw=False)
                    P.op('act', lambda e: e.copy(out=vtok[:, tp * 2:tp * 2 + 2, :], in_=b3[:, :, 128:256]),
                         reads=[bk_r], writes=[vtok_r], waw=False)
                if phase1 is None:
                    qT, qT_r = B["qT"]
                    kT, kT_r = B["kT"]
                    sg, sg_r = B["sg"]
                    qdf, qdf_r = B["qdf"]
                    qdb, qdb_r = B["qdb"]
                    for s in range(NS):
                        ts = slice(s * SUB, (s + 1) * SUB)
                        bk, bk_r = proj_fm(sl_r, wv, 0, xn, xn_r, s)
                        b3 = bk.rearrange("p (a b) -> p a b", a=4)
                        P.op('act', lambda e: e.copy(out=qT[:, ts], in_=bk), reads=[bk_r], writes=[qT_r], waw=False)
                        P.op('dve', lambda e: e.tensor_tensor(
                            out=qdf[:, ts].rearrange("p (a b) -> p a b", a=4), in0=b3, in1=bcast_mid(qrow[:, lh, 0, :], 4), op=ALU.mult),
                            reads=[bk_r, qrow_r], writes=[qdf_r], waw=False)
                        P.op('dve', lambda e: e.tensor_tensor(
                            out=qdb[:, ts].rearrange("p (a b) -> p a b", a=4), in0=b3, in1=bcast_mid(qrow[:, lh, 1, :], 4), op=ALU.mult),
                            reads=[bk_r, qrow_r], writes=[qdb_r], waw=False)
                        bk, bk_r = proj_fm(sl_r, wv, 128, xn, xn_r, s)
                        P.op('act', lambda e: e.activation(out=kT[:, ts], in_=bk, func=AF.Copy, scale=KS),
                             reads=[bk_r], writes=[kT_r], waw=False)
                        bk, bk_r = proj_fm(sl_r, wv, 384, xn, xn_r, s)
                        P.op('act', lambda e: e.activation(out=sg[:, ts], in_=bk, func=AF.Silu),
                             reads=[bk_r], writes=[sg_r], waw=False)
                for jp in range(4):
                    bk, bk_r = bank()
                    for q2 in range(2):
                        j = jp * 2 + q2
                        P.op('pe', lambda e: e.matmul(
                            bk[:, q2 * 256:q2 * 256 + 128], kdf[:, j, :], vtok[:, j, :], start=True, stop=True),
                            reads=[kdf_r, vtok_r], writes=[bk_r])
                        P.op('pe', lambda e: e.matmul(
                            bk[:, q2 * 256 + 128:q2 * 256 + 256], kdb[:, j, :], vtok[:, j, :], start=True, stop=True),
                            reads=[kdb_r, vtok_r], writes=[bk_r])
                    b4 = bk.rearrange("p (a b c) -> p a b c", a=2, b=2)
                    if jp % 2:
                        P.op('dve', lambda e: e.tensor_copy(out=P32[:, jp * 2:jp * 2 + 2, :, :], in_=b4),
                             reads=[bk_r], writes=[P32_r], waw=False)
                    else:
                        P.op('act', lambda e: e.copy(out=P32[:, jp * 2:jp * 2 + 2, :, :], in_=b4),
                             reads=[bk_r], writes=[P32_r], waw=False)
                if phase1 is None:
                    am, am_r = B["am"]
                    for s in range(NS):
                        ba, ba_r = bank()
                        for jj in range(4):
                            j = s * 4 + jj
                            cs = slice(j * 128, (j + 1) * 128)
                            P.op('pe', lambda e: e.matmul(
                                ba[:, jj * 128:(jj + 1) * 128], kT[:, cs], qT[:, cs], start=True, stop=True),
                                reads=[kT_r, qT_r], writes=[ba_r])
                        P.op('dve', lambda e: e.tensor_tensor(
                            out=am[:, s * 4:s * 4 + 4, :], in0=ba.rearrange("p (a b) -> p a b", a=4),
                            in1=bcast_mid(maskT[:, lh, :], 4), op=ALU.mult),
                            reads=[ba_r, maskT_r], writes=[am_r], waw=False)
                has_f = {}
                has_b = {}
                for si, (j0, n) in enumerate(segs):
                    for d_, dc in ((0, dcc[:, lh, 0:1]), (1, dcc[:, lh, 1:2])):
                        order = list(range(j0, j0 + n)) if d_ == 0 else list(range(j0 + n - 1, j0 - 1, -1))
                        has = has_f if d_ == 0 else has_b
                        seed = None if seeds is None else seeds[d_]
                        cur32 = None
                        for k_, j in enumerate(order):
                            nxt = S32[:, k_ % 2, d_, :]
                            if k_ == 0:
                                if seed is not None:
                                    sap, s_r = seed
                                    P.op('dve', lambda e: e.tensor_copy(out=nxt, in_=sap[:, hh, :]),
                                         reads=[s_r], writes=[S32_r], waw=False)
                                    has[j] = True
                                    cur32 = nxt
                                else:
                                    has[j] = False
                            else:
                                pj = order[k_ - 1]
                                if has[pj]:
                                    P.op('dve', lambda e: e.scalar_tensor_tensor(
                                        out=nxt, in0=cur32, scalar=dc, in1=P32[:, pj, d_, :],
                                        op0=ALU.mult, op1=ALU.add), reads=[S32_r, P32_r, dcc_r], writes=[S32_r], waw=False)
                                    cur32 = nxt
                                else:
                                    cur32 = P32[:, pj, d_, :]
                                has[j] = True
                            if has[j]:
                                if cur32 is nxt:
                                    P.op('dve', lambda e: e.tensor_copy(out=Sbf[:, j, d_, :], in_=cur32),
                                         reads=[S32_r], writes=[Sbf_r], waw=False)
                                else:
                                    P.op('dve', lambda e: e.tensor_copy(out=Sbf[:, j, d_, :], in_=cur32),
                                         reads=[P32_r], writes=[Sbf_r], waw=False)
                        if g == 0 or phase1 is not None:
                            lj = order[-1]
                            dst = stout[:, si, d_, :] if g == 0 else phase1[0][:, hh, d_, :]
                            dst_r = stout_r if g == 0 else phase1[1]
                            if has[lj]:
                                P.op('dve', lambda e: e.scalar_tensor_tensor(
                                    out=dst, in0=cur32, scalar=dc, in1=P32[:, lj, d_, :],
                                    op0=ALU.mult, op1=ALU.add), reads=[S32_r, P32_r, dcc_r], writes=[dst_r], waw=False)
                            else:
                                P.op('dve', lambda e: e.tensor_copy(out=dst, in_=P32[:, lj, d_, :]),
                                     reads=[P32_r], writes=[dst_r], waw=False)
                if g == 0:
                    for sq_ in range(4):
                        P.dma('sp', nstate[sq_, l, :, hh].rearrange("r d e -> d r e"), stout[:, sq_, :, :], reads=[stout_r])
                state[hh] = (has_f, has_b)

            def back(hh):
                lh = l * 4 + hh
                B = sets[hh % nset]
                vtok, vtok_r = B["vtok"]
                Sbf, Sbf_r = B["Sbf"]
                sg, sg_r = B["sg"]
                qdf, qdf_r = B["qdf"]
                qdb, qdb_r = B["qdb"]
                am, am_r = B["am"]
                has_f, has_b = state[hh]
                bos = []
                for s in range(NS):
                    bo, bo_r = bank()
                    bos.append((bo, bo_r))
                    for jj in range(4):
                        j = s * 4 + jj
                        cs = slice(j * 128, (j + 1) * 128)
                        ops = [(vtok[:, j, :], am[:, j, :], [vtok_r, am_r])]
                        if has_f[j]:
                            ops.append((Sbf[:, j, 0, :], qdf[:, cs], [Sbf_r, qdf_r]))
                        if has_b[j]:
                            ops.append((Sbf[:, j, 1, :], qdb[:, cs], [Sbf_r, qdb_r]))
                        n_ = len(ops)
                        for k_, (lt, rh, rr) in enumerate(ops):
                            P.op('pe', lambda e: e.matmul(
                                bo[:, jj * 128:(jj + 1) * 128], lt, rh, start=(k_ == 0), stop=(k_ == n_ - 1)),
                                reads=rr, writes=[bo_r], inc=(k_ == n_ - 1))
                for s in range(NS):
                    ts = slice(s * SUB, (s + 1) * SUB)
                    bo, bo_r = bos[s]
                    sqa, sqa_r = osq[s % 2]
                    rsa, rsa_r = ors[s % 2]
                    tma, tma_r = otm[s % 2]
                    P.op('act', lambda e: e.activation(out=sqa, in_=bo, func=AF.Square), reads=[bo_r], writes=[sqa_r])
                    bs, bs_r = bank()
                    P.op('pe', lambda e: e.matmul(bs, ones_bf, sqa, start=True, stop=True),
                         reads=[sqa_r, ones_r], writes=[bs_r])
                    P.op('act', lambda e: e.activation(out=tma, in_=bs, func=AF.Ln, scale=1.0 / 128, bias=eps_ap),
                         reads=[bs_r, eps_r], writes=[tma_r])
                    P.op('act', lambda e: e.activation(out=rsa, in_=tma, func=AF.Exp, scale=-0.5), reads=[tma_r], writes=[rsa_r])
                    P.op('dve', lambda e: e.scalar_tensor_tensor(
                        out=tma, in0=bo, scalar=vcol(V_GN + lh), in1=rsa, op0=ALU.mult, op1=ALU.mult),
                        reads=[bo_r, rsa_r, vecs_r], writes=[tma_r])
                    P.op('dve', lambda e: e.tensor_tensor(out=orn[:, hh, ts], in0=tma, in1=sg[:, ts], op=ALU.mult),
                         reads=[tma_r, sg_r], writes=[orn_r], waw=False)

            if phase1 is not None:
                for hh in range(4):
                    front(hh)
            else:
                front(0)
                for hh in range(4):
                    if hh + 1 < 4:
                        front(hh + 1)
                    back(hh)
            A.release(m)

        def norm_rope(bq, bq_r, bqs, bqs_r, gcol, gscol, ts, outs, tmps, rope_tabs):
            sqa, sqa_r = tmps['sq']
            rsa, rsa_r = tmps['rs']
            t1, t1_r = tmps['t1']
            P.op('act', lambda e: e.activation(out=sqa, in_=bq, func=AF.Square), reads=[bq_r], writes=[sqa_r])
            bs, bs_r = bank()
            P.op('pe', lambda e: e.matmul(bs, blk_bf, sqa, start=True, stop=True), reads=[sqa_r, blk_r], writes=[bs_r])
            P.op('act', lambda e: e.activation(out=t1, in_=bs, func=AF.Ln, scale=1.0 / 64, bias=eps_ap),
                 reads=[bs_r, eps_r], writes=[t1_r])
            P.op('act', lambda e: e.activation(out=rsa, in_=t1, func=AF.Exp, scale=-0.5), reads=[t1_r], writes=[rsa_r])
            if rope_tabs is None:
                P.op('dve', lambda e: e.scalar_tensor_tensor(out=t1, in0=bq, scalar=vcol(gcol), in1=rsa, op0=ALU.mult, op1=ALU.mult),
                     reads=[bq_r, rsa_r, vecs_r], writes=[t1_r])
            else:
                rp, rp_r = rope_tabs
                t2, t2_r = tmps['t2']
                P.op('dve', lambda e: e.scalar_tensor_tensor(out=t1, in0=bq, scalar=vcol(gcol), in1=rp[:, 0, ts], op0=ALU.mult, op1=ALU.mult),
                     reads=[bq_r, rp_r, vecs_r], writes=[t1_r])
                P.op('dve', lambda e: e.scalar_tensor_tensor(out=t2, in0=bqs, scalar=vcol(gscol), in1=rp[:, 1, ts], op0=ALU.mult, op1=ALU.mult),
                     reads=[bqs_r, rp_r, vecs_r], writes=[t2_r])
                P.op('dve', lambda e: e.tensor_tensor(out=t1, in0=t1, in1=t2, op=ALU.add), reads=[t1_r, t2_r], writes=[t1_r])
                P.op('dve', lambda e: e.tensor_tensor(out=t1, in0=t1, in1=rsa, op=ALU.mult), reads=[t1_r, rsa_r], writes=[t1_r])
            for k_, (oa, oa_r) in enumerate(outs):
                if k_ % 2 == 0:
                    P.op('act', lambda e, oa=oa: e.copy(out=oa, in_=t1), reads=[t1_r], writes=[oa_r], waw=False)
                else:
                    P.op('dve', lambda e, oa=oa: e.tensor_copy(out=oa, in_=t1), reads=[t1_r], writes=[oa_r], waw=False)

        def attention(l, g, xn, xn_r, oatt, oatt_r, rope_tabs=None, exch=None):
            wi = w_inx[l]
            m = A.mark()
            nkch = 8 if g == 0 else NKS // 128
            koff = 0 if g == 0 else PAST // 128
            QT, QT_r = A.bf([128, 4, T], "QT")
            KT, KT_r = A.bf([128, nkch * 128], "KT")
            Vx, Vx_r = A.bf([128, nkch, 2, 128], "Vx")
            P.op('dve', lambda e: e.memset(Vx[:, :, :, 64:128], 1.0), writes=[Vx_r])
            tmps = {'sq': A.bf([128, SUB], "asq"), 'rs': A.f32([128, SUB], "ars"), 't1': A.f32([128, SUB], "at1"),
                    't2': A.f32([128, SUB], "at2")}
            slA, slA_r = load_panel([(0, 8, 512, wsrc(wi, 8, X_AQ, 512))])
            wA = pview(slA, 0, 8, 512)
            if rope_tabs is not None:
                slB, slB_r = load_panel([(0, 8, 512, wsrc(wi, 8, X_AQS, 512))])
                wB = pview(slB, 0, 8, 512)
            for j in range(4):
                for s in range(NS):
                    ts = slice(s * SUB, (s + 1) * SUB)
                    bq, bq_r = proj_fm(slA_r, wA, j * 128, xn, xn_r, s)
                    bqs, bqs_r = (None, None)
                    if rope_tabs is not None:
                        bqs, bqs_r = proj_fm(slB_r, wB, j * 128, xn, xn_r, s)
                    norm_rope(bq, bq_r, bqs, bqs_r, V_QG + l, V_QGS + l, ts, [(QT[:, j, ts], QT_r)], tmps, rope_tabs)
            slC, slC_r = load_panel([(0, 8, 384, wsrc(wi, 8, X_AK, 384))])
            wC = pview(slC, 0, 8, 384)
            kf32 = None
            if g == 0:
                kf32, kf32_r = A.f32([128, T], "kf32")
                vout, vout_r = A.f32([128, 8, 128], "vout")
                kout, kout_r = A.f32([128, 8, 128], "kout")
            for s in range(NS):
                ts = slice(s * SUB, (s + 1) * SUB)
                bq, bq_r = proj_fm(slC_r, wC, 0, xn, xn_r, s)
                bqs, bqs_r = (None, None)
                if rope_tabs is not None:
                    bqs, bqs_r = proj_fm(slC_r, wC, 128, xn, xn_r, s)
                kts = slice(koff * 128 + s * SUB, koff * 128 + (s + 1) * SUB)
                outs = [(KT[:, kts], KT_r)]
                if g == 0:
                    outs.append((kf32[:, ts], kf32_r))
                norm_rope(bq, bq_r, bqs, bqs_r, V_KG + l, V_KGS + l, ts, outs, tmps, rope_tabs)
            for tp in range(2):
                bk, bk_r = bank()
                for q4 in range(4):
                    tt = tp * 4 + q4
                    for kc in range(8):
                        P.op('pe', lambda e, bk=bk, kc=kc, tt=tt, q4=q4: e.matmul(
                            bk[:, q4 * 128:(q4 + 1) * 128], xn[:, kc, tt * 128:(tt + 1) * 128], wC[:, kc, 256:384],
                            start=(kc == 0), stop=(kc == 7)),
                            reads=[slC_r, xn_r], writes=[bk_r], inc=(kc == 7))
                P.op('act', lambda e, bk=bk, tp=tp: e.copy(
                    out=Vx[:, koff + tp * 4:koff + tp * 4 + 4, :, 0:64], in_=bk.rearrange("p (a b c) -> p a b c", a=4, b=2)),
                    reads=[bk_r], writes=[Vx_r])
                if g == 0:
                    P.op('dve', lambda e, bk=bk, tp=tp: e.tensor_copy(
                        out=vout[:, tp * 4:tp * 4 + 4, :], in_=bk.rearrange("p (a b) -> p a b", a=4)),
                        reads=[bk_r], writes=[vout_r], waw=False)
            if g == 0:
                for sq_ in range(4):
                    P.dma('sp', ncv[sq_, l].rearrange("(t p) f -> p t f", p=128), vout[:, sq_ * 2:sq_ * 2 + 2, :], reads=[vout_r])
                for tp in range(2):
                    bk, bk_r = bank()
                    for q4 in range(4):
                        tt = tp * 4 + q4
                        P.op('pe', lambda e, bk=bk, tt=tt, q4=q4: e.transpose(
                            bk[:, q4 * 128:(q4 + 1) * 128], kf32[:, tt * 128:(tt + 1) * 128], ident),
                            reads=[kf32_r, ident_r], writes=[bk_r])
                    P.op('dve', lambda e, bk=bk, tp=tp: e.tensor_copy(
                        out=kout[:, tp * 4:tp * 4 + 4, :], in_=bk.rearrange("p (a b) -> p a b", a=4)),
                        reads=[bk_r], writes=[kout_r], waw=False)
                for sq_ in range(4):
                    P.dma('sp', nck[sq_, l].rearrange("(t p) f -> p t f", p=128), kout[:, sq_ * 2:sq_ * 2 + 2, :], reads=[kout_r])
            if exch is not None:
                exch(KT, KT_r, Vx, Vx_r)
            LA = 2
            pT = [A.bf([128, 2 * SUB], "pT%d" % k_) for k_ in range(LA + 2)]
            rec = [A.f32([64, SUB], "rec%d" % k_) for k_ in range(2)]
            recs = A.f32([64, SUB], "recs")
            items = []
            for qb in range(8):
                kchs = [2 * (qb // 2), 2 * (qb // 2) + 1] if g == 0 else list(range(nkch))
                for ki, kch in enumerate(kchs):
                    items.append(dict(q0=qb * 128, qb=qb, kch=kch, first=(ki == 0), last=(ki == len(kchs) - 1)))

            def front(i, it):
                kch, q0 = it['kch'], it['q0']
                pr = 2 * (i % 2)
                for gg in range(2):
                    ps_ = slice(gg * 64, (gg + 1) * 64)
                    bs, bs_r = banks[pr + gg]
                    P.op('pe', lambda e: e.matmul(bs, KT[ps_, kch * 128:(kch + 1) * 128], QT[ps_, :, q0:q0 + 128], start=True, stop=True),
                         reads=[KT_r, QT_r], writes=[bs_r])
                pa, pa_r = pT[i % (LA + 2)]
                pair = ps_all[:, pr * 512:(pr + 2) * 512]
                P.op('act', lambda e: e.activation(out=pa, in_=pair, func=AF.Exp, scale=0.125),
                     reads=[banks[pr][1], banks[pr + 1][1]], writes=[pa_r])
                it['pa'] = (pa, pa_r)

            def back(it):
                pa, pa_r = it['pa']
                kch, q0 = it['kch'], it['q0']
                first, last = it['first'], it['last']
                for gg in range(2):
                    bo, bo_r = banks[4 + 2 * (it['qb'] % 2) + gg]
                    P.op('pe', lambda e: e.matmul(bo, Vx[:, kch, gg, :], pa[:, gg * SUB:(gg + 1) * SUB], start=first, stop=last),
                         reads=[Vx_r, pa_r], writes=[bo_r])
                if last:
                    for gg in range(2):
                        ps_ = slice(gg * 64, (gg + 1) * 64)
                        bo, bo_r = banks[4 + 2 * (it['qb'] % 2) + gg]
                        ra, ra_r = rec[gg]
                        rs2, rs2_r = recs
                        P.op('act', lambda e: e.activation(out=rs2, in_=bo[64:128, :], func=AF.Ln), reads=[bo_r], writes=[rs2_r])
                        P.op('act', lambda e: e.activation(out=ra, in_=rs2, func=AF.Exp, scale=-1.0), reads=[rs2_r], writes=[ra_r])
                        P.op('dve', lambda e: e.tensor_tensor(
                            out=oatt[ps_, :, q0:q0 + 128], in0=bo[0:64, :].rearrange("p (a b) -> p a b", a=4),
                            in1=ra.rearrange("p (a b) -> p a b", a=4), op=ALU.mult),
                            reads=[bo_r, ra_r], writes=[oatt_r], waw=False)

            for i in range(len(items) + LA):
                if i < len(items):
                    front(i, items[i])
                if i >= LA:
                    back(items[i - LA])
            A.release(m)

        def conv_stage(l, g, xn, xn_r, cvo, cvo_r, halo=None, edges=None):
            wi = w_inx[l]
            nseg, L = (4, 256) if g == 0 else (1, 1024)
            for c in range(4):
                m = A.mark()
                sl, sl_r = load_panel([(0, 8, 384, wsrc(wi, 8, X_CONV + c * 384, 384))], key=("conv", l, c))
                wv = pview(sl, 0, 8, 384)
                u, u_r = A.f32([128, nseg, L + 2], "u")
                cbs, cbs_r = A.f32([128, T], "cbs")
                acc, acc_r = A.f32([128, T], "acc")
                cxs = [A.f32([128, SUB], "cxs%d" % k_) for k_ in range(2)]
                if halo is None:
                    P.op('dve', lambda e, u=u: e.memset(u[:, :, 0:1], 0.0), writes=[u_r], waw=False)
                    P.op('dve', lambda e, u=u: e.memset(u[:, :, L + 1:L + 2], 0.0), writes=[u_r], waw=False)
                else:
                    halo(c, u, u_r)
                for s in range(NS):
                    ts = slice(s * SUB, (s + 1) * SUB)
                    bcb, bcb_r = proj_fm(sl_r, wv, 0, xn, xn_r, s)
                    bcc, bcc_r = proj_fm(sl_r, wv, 128, xn, xn_r, s)
                    bcx, bcx_r = proj_fm(sl_r, wv, 256, xn, xn_r, s)
                    cxa, cxa_r = cxs[s % 2]
                    P.op('act', lambda e, bcx=bcx, cxa=cxa: e.copy(out=cxa, in_=bcx), reads=[bcx_r], writes=[cxa_r])
                    P.op('act', lambda e, bcb=bcb, ts=ts: e.copy(out=cbs[:, ts], in_=bcb), reads=[bcb_r], writes=[cbs_r], waw=False)
                    if g == 0:
                        uo = u[:, 2 * s:2 * s + 2, 1:L + 1]
                        i0 = bcc.rearrange("p (a b) -> p a b", a=2)
                        i1 = cxa.rearrange("p (a b) -> p a b", a=2)
                    else:
                        uo = u[:, 0, 1 + s * SUB:1 + (s + 1) * SUB]
                        i0 = bcc
                        i1 = cxa
                    P.op('dve', lambda e, uo=uo, i0=i0, i1=i1: e.tensor_tensor(out=uo, in0=i0, in1=i1, op=ALU.mult),
                         reads=[bcc_r, cxa_r], writes=[u_r], waw=False)
                a3 = acc.rearrange("p (a b) -> p a b", a=nseg)
                wr = V_CONV + (l * 3) * 4 + c
                P.op('dve', lambda e, u=u, a3=a3, wr=wr: e.tensor_scalar(
                    out=a3, in0=u[:, :, 1:L + 1], scalar1=vcol(wr + 4), scalar2=None, op0=ALU.mult),
                    reads=[u_r, vecs_r], writes=[acc_r])
                P.op('dve', lambda e, u=u, a3=a3, wr=wr: e.scalar_tensor_tensor(
                    out=a3, in0=u[:, :, 0:L], scalar=vcol(wr), in1=a3, op0=ALU.mult, op1=ALU.add),
                    reads=[u_r, vecs_r, acc_r], writes=[acc_r])
                P.op('dve', lambda e, u=u, a3=a3, wr=wr: e.scalar_tensor_tensor(
                    out=a3, in0=u[:, :, 2:L + 2], scalar=vcol(wr + 8), in1=a3, op0=ALU.mult, op1=ALU.add),
                    reads=[u_r, vecs_r, acc_r], writes=[acc_r])
                P.op('dve', lambda e, acc=acc, cbs=cbs, c=c: e.tensor_tensor(out=cvo[:, c, :], in0=acc, in1=cbs, op=ALU.mult),
                     reads=[acc_r, cbs_r], writes=[cvo_r], waw=False)
                if edges is not None:
                    eg, eg_r = edges
                    for k_, (src_, o0, o1) in enumerate(((u[:, 0, :], 1, L), (acc, 0, L - 1), (cbs, 0, L - 1))):
                        for e_, off in enumerate((o0, o1)):
                            P.op('act', lambda e, src_=src_, off=off, k_=k_, e_=e_, c=c: e.copy(
                                out=eg[:, k_, c, e_:e_ + 1], in_=src_[:, off:off + 1]),
                                reads=[u_r, acc_r, cbs_r], writes=[eg_r], waw=False)
                A.release(m)

        def merge_stage(l, g, xn, xn_r, srcs, oo_t):
            m = A.mark()
            mg32, mg32_r = A.f32([128, 8, T], "mg32")
            mbf, mbf_r = oo_t[0], Res("mbf")
            alias_r = [srcs[0][1], srcs[1][1]]
            sgs = [A.f32([128, SUB], "msg%d" % k_) for k_ in range(2)]
            tms = [A.f32([128, SUB], "mtm%d" % k_) for k_ in range(2)]
            wos = [w_ret_o[l], w_att_o[l], w_conv_o[l]]
            wobuf = [A.bf([128, 4, 1024], "wobuf%d" % k_) for k_ in range(2)]
            k_ = 0
            for bi in range(3):
                src, src_r = srcs[bi]
                wov, slo_r = wobuf[bi % 2]
                P.dma('pool', wov, wsrc(wos[bi], 4, 0, 1024), writes=[slo_r])
                for half in range(2):
                    slg, slg_r = load_panel([(0, 8, 512, wsrc(w_inx[l], 8, X_GATE + bi * 1024 + half * 512, 512))])
                    wgv = pview(slg, 0, 8, 512)
                    for j in range(4):
                        c = half * 4 + j
                        for s in range(NS):
                            ts = slice(s * SUB, (s + 1) * SUB)
                            by, by_r = proj_fm(slo_r, wov, c * 128, src, src_r, s, kcn=4)
                            bg, bg_r = proj_fm(slg_r, wgv, j * 128, xn, xn_r, s)
                            sa, sa_r = sgs[k_ % 2]
                            ta, ta_r = tms[k_ % 2]
                            k_ += 1
                            P.op('act', lambda e, bg=bg, sa=sa: e.activation(out=sa, in_=bg, func=AF.Sigmoid), reads=[bg_r], writes=[sa_r])
                            if bi == 0:
                                P.op('dve', lambda e, by=by, sa=sa, c=c, ts=ts: e.tensor_tensor(out=mg32[:, c, ts], in0=by, in1=sa, op=ALU.mult),
                                     reads=[by_r, sa_r], writes=[mg32_r], waw=False)
                            else:
                                P.op('dve', lambda e, by=by, sa=sa, ta=ta: e.tensor_tensor(out=ta, in0=by, in1=sa, op=ALU.mult),
                                     reads=[by_r, sa_r], writes=[ta_r])
                                if bi == 1:
                                    P.op('dve', lambda e, ta=ta, c=c, ts=ts: e.tensor_tensor(out=mg32[:, c, ts], in0=mg32[:, c, ts], in1=ta, op=ALU.add),
                                         reads=[ta_r, mg32_r], writes=[mg32_r], waw=False)
                                else:
                                    P.op('dve', lambda e, ta=ta, c=c, ts=ts: e.tensor_tensor(out=mbf[:, c, ts], in0=mg32[:, c, ts], in1=ta, op=ALU.add),
                                         reads=[ta_r, mg32_r], writes=[mbf_r] + alias_r, waw=False)
            for half in range(2):
                sl, sl_r = load_panel([(0, 8, 512, wsrc(w_o[l], 8, half * 512, 512))])
                wv = pview(sl, 0, 8, 512)
                for j in range(4):
                    c = half * 4 + j
                    for s in range(NS):
                        ts = slice(s * SUB, (s + 1) * SUB)
                        bk, bk_r = proj_fm(sl_r, wv, j * 128, mbf, mbf_r, s)
                        h_update(bk, bk_r, 1, c, ts, s)
            flush_ss()
            for r_ in (mbf_r, srcs[0][1], srcs[1][1]):
                for dd in (r_.w, r_.r, r_.pw, r_.pr):
                    _merge(oo_t[1].pr, dd)
            A.release(m)

        AG_GROUPS = [list(range(NCORES))]

        def all_gather(snd_t, rcv_t, snd_r, rcv_r):
            P.collective(lambda e: e.collective_compute(
                "AllGather", ALU.bypass, replica_groups=AG_GROUPS, ins=[snd_t.ap().opt()], outs=[rcv_t.ap().opt()]),
                reads=[snd_r], writes=[rcv_r])

        sample_c = {}

        def setup_sample():
            ct, ct_r = ct0, ct0_r
            rope, rope_r = A.f32([128, 2, T], "rope")
            P.dma('sp', rope, rope_d.rearrange("k p t -> p k t"), writes=[rope_r])
            coef, coef_r = A.f32([128, DEPTH * 4, 2, 9], "coef")
            for l in range(DEPTH):
                for hh in range(4):
                    for d_ in range(2):
                        col = l * 8 + d_ * 4 + hh
                        P.op('act', lambda e, l=l, hh=hh, d_=d_, col=col: e.activation(
                            out=coef[:, l * 4 + hh, d_, :], in_=ct[:, 2 + d_ * 9:2 + d_ * 9 + 9], func=AF.Exp, scale=lg[:, col:col + 1]),
                            reads=[ct_r, lg_r], writes=[coef_r], waw=False)
            P.op('dve', lambda e: e.tensor_tensor(
                out=coef, in0=coef, in1=bcast_mid(ct[:, 20:38], DEPTH * 4).rearrange("p a (b c) -> p a b c", b=2), op=ALU.mult),
                reads=[coef_r, ct_r], writes=[coef_r])
            sample_c.update(ct=ct, ct_r=ct_r, rope=(rope, rope_r), coef=coef, coef_r=coef_r)

        def mixer_sample(l, xn, xn_r):
            ct, ct_r = sample_c['ct'], sample_c['ct_r']
            coef, coef_r = sample_c['coef'], sample_c['coef_r']
            snd_r, rcv_r, sndk_r, rcvk_r = Res("sndS"), Res("rcvS"), Res("sndK"), Res("rcvK")
            m = A.mark()
            oo, oo_r = A.bf([128, 8, T], "oo")
            orn = (oo[:, 0:4, :], Res("orn"))
            oatt = (oo[:, 4:8, :], Res("oatt"))
            for r_ in (orn[1], oatt[1]):
                _merge(r_.pr, oo_r.pr)
            cvo = A.bf([128, 4, T], "cvo")
            edges = A.f32([128, 3, 4, 2], "edges")
            ms_ = A.mark()
            seedf = A.f32([128, 4, 128], "seedf")
            seedb = A.f32([128, 4, 128], "seedb")
            m1 = A.mark()
            sums, sums_r = A.f32([128, 4, 2, 128], "sums")
            retention(l, 1, xn, xn_r, None, None, seeds=None, phase1=(sums, sums_r))
            sbf, sbf_r = A.bf([128, 8, 128], "sumsbf")
            P.op('act', lambda e: e.copy(out=sbf, in_=sums.rearrange("p a b c -> p (a b) c")), reads=[sums_r], writes=[sbf_r])
            P.dma('sp', sndS[l].ap().rearrange("(p x) e -> p x e", p=128), sbf, reads=[sbf_r], writes=[snd_r])
            for c_ in range(3):
                prefetch_panel(("conv", l, c_), [(0, 8, 384, wsrc(w_inx[l], 8, X_CONV + c_ * 384, 384))])
            all_gather(sndS[l], rcvS[l], snd_r, rcv_r)
            conv_stage(l, 1, xn, xn_r, cvo[0], cvo[1], edges=edges)
            rS, rS_r = A.bf([128, NCORES, 8, 128], "rS")
            rv = rcvS[l].ap().rearrange("(q p x) e -> q p x e", q=NCORES, p=128)
            for q in range(NCORES):
                P.dma('sp', rS[:, q, :, :], rv[q], reads=[rcv_r], writes=[rS_r])
            s0, s0_r = A.f32([128, 2, 4, 128], "s0")
            for d_ in range(2):
                P.dma('sp', s0[:, d_, :, :], sret_d[l, d_].rearrange("h d e -> d h e"), writes=[s0_r])
            for d_, (sd, sd_r) in enumerate((seedf, seedb)):
                for hh in range(4):
                    lh = l * 4 + hh
                    P.op('dve', lambda e, d_=d_, hh=hh, lh=lh, sd=sd: e.tensor_scalar(
                        out=sd[:, hh, :], in0=s0[:, d_, hh, :], scalar1=coef[:, lh, d_, 0:1], scalar2=None, op0=ALU.mult),
                        reads=[s0_r, coef_r], writes=[sd_r], waw=False)
                    for q in range(NCORES):
                        P.op('dve', lambda e, d_=d_, hh=hh, lh=lh, sd=sd, q=q: e.scalar_tensor_tensor(
                            out=sd[:, hh, :], in0=rS[:, q, hh * 2 + d_, :], scalar=coef[:, lh, d_, 1 + q:2 + q], in1=sd[:, hh, :],
                            op0=ALU.mult, op1=ALU.add), reads=[rS_r, coef_r, sd_r], writes=[sd_r], waw=False)
            A.release(m1)
            retention(l, 1, xn, xn_r, orn[0], orn[1], seeds=[seedf, seedb])
            A.release(ms_)


            def exch(KT, KT_r, Vx, Vx_r):
                eg, eg_r = edges
                m2 = A.mark()
                sv = sndK[l].ap()
                P.dma('sp', sv[0:1024, :].rearrange("(p x) c -> p (x c)", p=128), KT[:, PAST:PAST + T], reads=[KT_r], writes=[sndk_r])
                vv = sv[1024:2048, :].rearrange("(t p) (g d) -> p t g d", p=128, g=2)
                for gg in range(2):
                    P.dma('sp', vv[:, :, gg, :], Vx[:, 4:12, gg, 0:64], reads=[Vx_r], writes=[sndk_r])
                ebf, ebf_r = A.bf([128, 8], "ebf")
                P.op('act', lambda e: e.copy(out=ebf, in_=eg[:, 0, :, :].rearrange("p a b -> p (a b)")), reads=[eg_r], writes=[ebf_r])
                P.dma('sp', sv[2048:2056, :].rearrange("a (b x) -> (a b) x", x=8), ebf, reads=[ebf_r], writes=[sndk_r])
                all_gather(sndK[l], rcvK[l], sndk_r, rcvk_r)
                rk = rcvK[l].ap().rearrange("(q r) c -> q r c", q=NCORES)
                pk, pk_r = A.f32([128, 4, 128], "pk")
                P.dma('sp', pk, ck_d[l].rearrange("(t p) f -> p t f", p=128), writes=[pk_r])
                bk, bk_r = bank()
                for tt in range(4):
                    P.op('pe', lambda e, bk=bk, tt=tt: e.transpose(bk[:, tt * 128:(tt + 1) * 128], pk[:, tt, :], ident),
                         reads=[pk_r, ident_r], writes=[bk_r])
                P.op('act', lambda e, bk=bk: e.copy(out=KT[:, 0:PAST], in_=bk), reads=[bk_r], writes=[KT_r], waw=False)
                pv, pv_r = A.f32([128, 4, 128], "pv")
                P.dma('sp', pv, cv_d[l].rearrange("(t p) f -> p t f", p=128), writes=[pv_r])
                P.op('dve', lambda e: e.tensor_copy(out=Vx[:, 0:4, :, 0:64], in_=pv.rearrange("p t (g d) -> p t g d", g=2)),
                     reads=[pv_r], writes=[Vx_r], waw=False)
                kst, kst_r = A.bf([128, 4 * T], "kst")
                vst, vst_r = A.bf([128, 32, 2, 64], "vst")
                for q in range(4):
                    P.dma('sp', KT[:, PAST + q * T:PAST + (q + 1) * T], rk[q, 0:1024, :].rearrange("(p x) c -> p (x c)", p=128),
                          reads=[rcvk_r], writes=[KT_r])
                    P.dma('sp', kst[:, q * T:(q + 1) * T], rk[4 + q, 0:1024, :].rearrange("(p x) c -> p (x c)", p=128),
                          reads=[rcvk_r], writes=[kst_r])
                    for gg in range(2):
                        P.dma('sp', Vx[:, 4 + q * 8:12 + q * 8, gg, 0:64],
                              rk[q, 1024:2048, :].rearrange("(t p) (g d) -> p t g d", p=128, g=2)[:, :, gg, :],
                              reads=[rcvk_r], writes=[Vx_r])
                        P.dma('sp', vst[:, q * 8:(q + 1) * 8, gg, :],
                              rk[4 + q, 1024:2048, :].rearrange("(t p) (g d) -> p t g d", p=128, g=2)[:, :, gg, :],
                              reads=[rcvk_r], writes=[vst_r])
                kd = KT[:, PAST:PAST + 4 * T]
                P.op('dve', lambda e: e.tensor_scalar(out=kd, in0=kd, scalar1=ct[:, 0:1], scalar2=None, op0=ALU.mult),
                     reads=[KT_r, ct_r], writes=[KT_r])
                P.op('dve', lambda e: e.scalar_tensor_tensor(out=kd, in0=kst, scalar=ct[:, 1:2], in1=kd, op0=ALU.mult, op1=ALU.add),
                     reads=[kst_r, KT_r, ct_r], writes=[KT_r])
                vd = Vx[:, 4:36, :, 0:64]
                P.op('dve', lambda e: e.tensor_scalar(out=vd, in0=vd, scalar1=ct[:, 0:1], scalar2=None, op0=ALU.mult),
                     reads=[Vx_r, ct_r], writes=[Vx_r])
                P.op('dve', lambda e: e.scalar_tensor_tensor(out=vd, in0=vst, scalar=ct[:, 1:2], in1=vd, op0=ALU.mult, op1=ALU.add),
                     reads=[vst_r, Vx_r, ct_r], writes=[Vx_r])
                re_, re_r = A.bf([128, NCORES, 8], "redge")
                for q in range(NCORES):
                    P.dma('sp', re_[:, q, :], rk[q, 2048:2056, :].rearrange("a (b x) -> (a b) x", x=8), reads=[rcvk_r], writes=[re_r])
                hl, hl_r = A.f32([128, 2, 4], "hl")
                P.op('dve', lambda e: e.memset(hl, 0.0), writes=[hl_r])
                rev = re_.rearrange("p q (c e) -> p q c e", e=2)
                for q in range(NCORES):
                    P.op('dve', lambda e, q=q: e.scalar_tensor_tensor(
                        out=hl[:, 0, :], in0=rev[:, q, :, 1], scalar=ct[:, 38 + q:39 + q], in1=hl[:, 0, :], op0=ALU.mult, op1=ALU.add),
                        reads=[re_r, ct_r, hl_r], writes=[hl_r])
                    P.op('dve', lambda e, q=q: e.scalar_tensor_tensor(
                        out=hl[:, 1, :], in0=rev[:, q, :, 0], scalar=ct[:, 46 + q:47 + q], in1=hl[:, 1, :], op0=ALU.mult, op1=ALU.add),
                        reads=[re_r, ct_r, hl_r], writes=[hl_r])
                wr = V_CONV + l * 12
                for e_, wk, tpos in ((0, 0, 0), (1, 8, T - 1)):
                    P.op('dve', lambda e, e_=e_, wk=wk: e.tensor_tensor(out=hl[:, e_, :], in0=hl[:, e_, :], in1=vecs[:, wr + wk:wr + wk + 4], op=ALU.mult),
                         reads=[hl_r, vecs_r], writes=[hl_r])
                    P.op('dve', lambda e, e_=e_: e.tensor_tensor(out=hl[:, e_, :], in0=hl[:, e_, :], in1=eg[:, 1, :, e_], op=ALU.add),
                         reads=[hl_r, eg_r], writes=[hl_r])
                    P.op('dve', lambda e, e_=e_, tpos=tpos: e.tensor_tensor(out=cvo[0][:, :, tpos], in0=hl[:, e_, :], in1=eg[:, 2, :, e_], op=ALU.mult),
                         reads=[hl_r, eg_r], writes=[cvo[1]])
                A.release(m2)

            attention(l, 1, xn, xn_r, oatt[0], oatt[1], rope_tabs=sample_c['rope'], exch=exch)
            merge_stage(l, 1, xn, xn_r, [orn, oatt, cvo], (oo, oo_r))
            A.release(m)

        def mixer_stage(l, g, xn, xn_r):
            m = A.mark()
            oo, oo_r = A.bf([128, 8, T], "oo")
            orn = (oo[:, 0:4, :], Res("orn"))
            oatt = (oo[:, 4:8, :], Res("oatt"))
            for r_ in (orn[1], oatt[1]):
                _merge(r_.pr, oo_r.pr)
            cvo = A.bf([128, 4, T], "cvo")
            fl = debug or ('ret', 'att', 'conv', 'merge')
            for t_ in (orn, oatt, cvo):
                if debug:
                    P.op('dve', lambda e, t_=t_: e.memset(t_[0], 0.0), writes=[t_[1]])
            if 'ret' in fl:
                retention(l, g, xn, xn_r, orn[0], orn[1])
            if 'att' in fl:
                attention(l, g, xn, xn_r, oatt[0], oatt[1])
            if 'conv' in fl:
                conv_stage(l, g, xn, xn_r, cvo[0], cvo[1])
            if 'merge' in fl:
                merge_stage(l, g, xn, xn_r, [orn, oatt, cvo], (oo, oo_r))
            A.release(m)

        setup_eps()
        sqp = [A.bf([128, SUB], "sqp%d" % k_) for k_ in range(3)]
        base_mark = A.mark()

        groups = [0] + ([1] if enable_sample else [])
        for g in groups:
            ci = g
            if g == 1:
                setup_sample()
            load_x(g)
            if g == 0:
                compute_mods_all()
            for l in range(DEPTH):
                use_mods(l, ci)
                m = A.mark()
                xn, xn_r = A.bf([128, 8, T], "xn")
                norm_stage(0, ci, xn, xn_r, have_ss=(l > 0))
                ffn_stage(l, 0, 0, xn, xn_r)
                norm_stage(1, ci, xn, xn_r)
                if g == 0:
                    mixer_stage(l, g, xn, xn_r)
                else:
                    mixer_sample(l, xn, xn_r)
                norm_stage(2, ci, xn, xn_r)
                ffn_stage(l, 1, 2, xn, xn_r)
                A.release(m)
            store_y(g)
        print("arena peak words", A.peak, "of", ARENA_WORDS, "waits", P.nwait)

        P.finish()
        P.emit()
    return nc, dbg_out


def _rope_tables(tok0, n):
    t = np.arange(tok0, tok0 + n)
    t_row = (t // 64).astype(np.float32)
    t_col = (t % 64).astype(np.float32)
    n_freq = 16
    inv = (np.float32(10000.0) ** (-np.arange(n_freq, dtype=np.float32) / np.float32(n_freq))).astype(np.float32)
    ang = np.concatenate([t_row[:, None] * inv, t_col[:, None] * inv], axis=-1).astype(np.float32)
    cos = np.cos(ang).astype(np.float32).T
    sin = np.sin(ang).astype(np.float32).T
    C = np.concatenate([cos, cos, cos, cos], axis=0)
    S = np.concatenate([-sin, sin, -sin, sin], axis=0)
    return np.stack([C, S], axis=0).astype(np.float32)


def _ctab(core):
    b, r = core // 4, core % 4
    t = np.zeros(64, np.float32)
    t[b] = 1.0
    for d_ in range(2):
        ex = np.zeros(9, np.float32)
        mk = np.zeros(9, np.float32)
        nprev = r if d_ == 0 else 3 - r
        ex[0] = 1024.0 * nprev
        mk[0] = 1.0
        for q in range(NCORES):
            if q // 4 != b:
                continue
            qr = q % 4
            dist = (r - 1 - qr) if d_ == 0 else (qr - r - 1)
            if dist >= 0:
                ex[1 + q] = 1024.0 * dist
                mk[1 + q] = 1.0
        t[2 + d_ * 9:2 + d_ * 9 + 9] = ex
        t[20 + d_ * 9:20 + d_ * 9 + 9] = mk
    if r > 0:
        t[38 + core - 1] = 1.0
    if r < 3:
        t[46 + core + 1] = 1.0
    return t


_CACHE = {}


def kernel(x_prompt, x_sample, c, state_ret, cache_k, cache_v, c_ctx, w_ada, b_ada, norm_w, w_ffn_in,
           w_ffn_out, w_in, ret_decay_logit, ret_gn, q_gain, k_gain, conv_w, w_ret_o, w_att_o, w_conv_o, w_o,
           _enable_sample=True, _debug=None, _ncores=NCORES):
    f32 = np.float32
    x_prompt = np.asarray(x_prompt, f32)
    x_sample = np.asarray(x_sample, f32)
    key = (_enable_sample, _debug)
    if key not in _CACHE:
        _CACHE[key] = build_program(enable_sample=_enable_sample, debug=_debug)
    nc, dbg = _CACHE[key]

    cols = build_win_cols()
    w_inx = np.ascontiguousarray(np.asarray(w_in, f32)[:, :, cols])
    rows = []
    for j in range(4):
        rows += list(range(j * 64, (j + 1) * 64)) + list(range((4 + j) * 64, (5 + j) * 64))
    w_att_o_p = np.ascontiguousarray(np.asarray(w_att_o, f32)[:, rows, :])

    qg = np.asarray(q_gain, f32)
    kg = np.asarray(k_gain, f32)
    sw = np.array(_sw(0))
    ident = np.eye(128, dtype=f32)
    ii = np.arange(128)
    mm_, cc_ = np.meshgrid(ii, ii, indexing='ij')
    posd = np.stack([np.maximum(cc_ - mm_, 0), np.maximum(mm_ - cc_, 0), (cc_ >= mm_), (cc_ < mm_)]).astype(f32)
    posv = np.stack([ii, 127 - ii, ii + 1, 128 - ii], axis=1).astype(f32)
    posr = np.stack([ii, 127 - ii, ii + 1, 128 - ii], axis=0).astype(f32)

    in_maps = []
    for core in range(_ncores):
        b, r = core // 4, core % 4
        vecs = np.zeros((NV, 128), f32)
        vecs[V_NORM:V_NORM + 48] = np.asarray(norm_w, f32).reshape(48, 128)
        vecs[V_GN:V_GN + 8] = np.asarray(ret_gn, f32).reshape(8, 128)
        vecs[V_CONV:V_CONV + 24] = np.asarray(conv_w, f32).reshape(24, 128)
        for l in range(DEPTH):
            vecs[V_QG + l] = np.tile(qg[l], 2)
            vecs[V_QGS + l] = np.tile(qg[l][sw], 2)
            vecs[V_KG + l] = np.tile(kg[l], 2)
            vecs[V_KGS + l] = np.tile(kg[l][sw], 2)
        vecs[V_COND:V_COND + 8] = np.asarray(c_ctx, f32).reshape(8, 128)
        vecs[V_COND + 8:V_COND + 16] = np.asarray(c, f32)[0].reshape(8, 128)
        vecs[V_COND + 16:V_COND + 24] = np.asarray(c, f32)[1].reshape(8, 128)
        m = {
            "xp": x_prompt[core * 4:(core + 1) * 4].reshape(T, D),
            "xs": x_sample[b, r * T:(r + 1) * T],
            "vecs": vecs,
            "bada": np.ascontiguousarray(np.asarray(b_ada, f32).reshape(DEPTH, 72, 128)[:, core * 9:(core + 1) * 9]).reshape(DEPTH * 9, 128),
            "decay": np.asarray(ret_decay_logit, f32).reshape(-1),
            "ident": ident, "posd": posd, "posv": posv, "posr": posr,
            "rope": _rope_tables(r * T, T),
            "ctab": _ctab(core),
            "sret": np.asarray(state_ret, f32)[b],
            "ck": np.asarray(cache_k, f32)[b].reshape(DEPTH, PAST, 128),
            "cv": np.asarray(cache_v, f32)[b].reshape(DEPTH, PAST, 128),
            "w_ada": np.ascontiguousarray(np.asarray(w_ada, f32)[:, :, core * 1152:(core + 1) * 1152]), "w_ffn_in": np.asarray(w_ffn_in, f32),
            "w_ffn_out": np.asarray(w_ffn_out, f32), "w_inx": w_inx,
            "w_ret_o": np.asarray(w_ret_o, f32), "w_att_o": w_att_o_p,
            "w_conv_o": np.asarray(w_conv_o, f32), "w_o": np.asarray(w_o, f32),
        }
        in_maps.append(m)

    res = run_bass_kernel_spmd(nc, in_maps, core_ids=list(range(_ncores)))
    R = res.results
    if _ncores < NCORES:
        return R
    y_prompt = np.stack([R[cix]["yp"].reshape(4, 256, D) for cix in range(NCORES)]).reshape(32, 256, D)
    y_sample = np.stack([R[cix]["ys"] for cix in range(NCORES)]).reshape(2, 4096, D)
    nstate = np.concatenate([R[cix]["nstate"] for cix in range(NCORES)], axis=0)
    nk = np.concatenate([R[cix]["nck"] for cix in range(NCORES)], axis=0).reshape(32, DEPTH, 256, 2, 64)
    nv = np.concatenate([R[cix]["ncv"] for cix in range(NCORES)], axis=0).reshape(32, DEPTH, 256, 2, 64)
    if _debug:
        kernel.dbg = [{k: R[cix]["dbg_" + k] for k in dbg} for cix in range(NCORES)]
    return (y_prompt.astype(f32), y_sample.astype(f32), nstate.astype(f32), nk.astype(f32), nv.astype(f32))
```

```python
import types
import numpy as np
from contextlib import ExitStack
import concourse.bass as bass
import concourse.mybir as mybir
from concourse.bass_utils import run_bass_kernel_spmd

F32, BF16 = mybir.dt.float32, mybir.dt.bfloat16
AF = mybir.ActivationFunctionType
ALU = mybir.AluOpType

D = 1024
T = 1024
SUB = 512
NS = T // SUB
FFN = 2816
NFC = FFN // 128
DEPTH = 2
EPS = 1e-6
NCORES = 8
PAST = 512
NKS = PAST + 4096
SAME_SYNC = True

O_RQ, O_RK, O_RV, O_RG, O_AQ, O_AK, O_AV, O_CB, O_CC, O_CX, O_GR, O_GA, O_GC = (
    0, 512, 1024, 1536, 2048, 2560, 2688, 2816, 3328, 3840, 4352, 5376, 6400)


def _sw(base):
    return list(range(base + 32, base + 64)) + list(range(base, base + 32))


def build_win_cols():
    cols = []
    for h in range(4):
        for o in (O_RQ, O_RK, O_RV, O_RG):
            cols += list(range(o + h * 128, o + (h + 1) * 128))
    for j in range(4):
        cols += list(range(O_AQ + j * 64, O_AQ + (j + 1) * 64))
        cols += list(range(O_AQ + (4 + j) * 64, O_AQ + (5 + j) * 64))
    for j in range(4):
        cols += _sw(O_AQ + j * 64) + _sw(O_AQ + (4 + j) * 64)
    cols += list(range(O_AK, O_AK + 128))
    cols += _sw(O_AK) + _sw(O_AK + 64)
    cols += list(range(O_AV, O_AV + 128))
    for c in range(4):
        for o in (O_CB, O_CC, O_CX):
            cols += list(range(o + c * 128, o + (c + 1) * 128))
    cols += list(range(O_GR, O_GR + 3072))
    return np.array(cols, dtype=np.int64)


X_RET = 0
X_AQ = 2048
X_AQS = 2560
X_AK = 3072
X_AKS = 3200
X_AV = 3328
X_CONV = 3456
X_GATE = 3456 + 1536
NWX = X_GATE + 3072

V_NORM = 0
V_GN = 48
V_CONV = 56
V_QG = 80
V_QGS = 82
V_KG = 84
V_KGS = 86
V_COND = 88
NV = 112


ENGS = ['pe', 'act', 'dve', 'pool', 'sp']


class Res:
    __slots__ = ('name', 'w', 'r', 'pw', 'pr', 'excl')

    def __init__(self, name=''):
        self.name = name
        self.w = {}
        self.r = {}
        self.pw = {}
        self.pr = {}
        self.excl = False


def _freeze(fn):
    if fn.__closure__ is None:
        return fn
    cells = []
    for c in fn.__closure__:
        try:
            cells.append(types.CellType(c.cell_contents))
        except ValueError:
            cells.append(c)
    return types.FunctionType(fn.__code__, fn.__globals__, fn.__name__, fn.__defaults__, tuple(cells))


def _merge(dst, src):
    for k, v in src.items():
        if k not in dst or dst[k][2] < v[2]:
            dst[k] = v


class Prog:
    def __init__(self, nc, stack):
        self.nc = nc
        self.q = {e: [] for e in ENGS}
        self.cnt = {e: 0 for e in ENGS}
        self.seen = {e: {} for e in ENGS}
        self.esem = {e: stack.enter_context(nc.semaphore("es_" + e)) for e in ENGS if e != 'sp'}
        self.dsem = []
        self.dcnt = []
        self.dpool = {}
        self.dnext = {}
        for qn, n in (('sp', 20), ('pool', 10)):
            ids = []
            for i in range(n):
                self.dsem.append(stack.enter_context(nc.semaphore("ds_%s%d" % (qn, i))))
                self.dcnt.append(0)
                ids.append(len(self.dsem) - 1)
            self.dpool[qn] = ids
            self.dnext[qn] = 0
        self.ccsem = stack.enter_context(nc.semaphore("cc"))
        self.cccnt = 0
        self.nwait = 0

    def _need(self, eng, dep):
        kind, ident, n = dep
        if kind == 'e':
            if ident == eng and (eng == 'pe' or not SAME_SYNC):
                return
            sem = self.esem[ident]
        elif kind == 'd':
            sem = self.dsem[ident]
        else:
            sem = self.ccsem
        key = (kind, ident)
        if self.seen[eng].get(key, 0) >= n:
            return
        self.seen[eng][key] = n
        self.nwait += 1
        self.q[eng].append(lambda e, s=sem, v=n: e.wait_ge(s, v))

    def _sync(self, eng, reads, writes, selfdep, waw):
        for r in reads:
            for d in r.w.values():
                if d != selfdep:
                    self._need(eng, d)
            if r.excl:
                for d in list(r.r.values()):
                    if d != selfdep and d[1] != eng:
                        self._need(eng, d)
        for w in writes:
            if w.r:
                w.pw, w.pr = w.w, w.r
                w.w, w.r = {}, {}
            elif waw:
                for d in w.w.values():
                    if d != selfdep:
                        self._need(eng, d)
            for d in list(w.pw.values()) + list(w.pr.values()):
                if d != selfdep:
                    self._need(eng, d)

    def op(self, eng, fn, reads=(), writes=(), inc=True, waw=True):
        fn = _freeze(fn)
        n = self.cnt[eng] + 1
        dep = ('e', eng, n)
        self._sync(eng, reads, writes, dep, waw)
        if inc:
            self.cnt[eng] = n
            sem = self.esem[eng]
            self.q[eng].append(lambda e, f=fn, s=sem: f(e).then_inc(s, 1))
        else:
            self.q[eng].append(lambda e, f=fn: f(e))
        for r in reads:
            r.r[('e', eng)] = dep
        for w in writes:
            w.w[('e', eng)] = dep

    def dma(self, queue, out, in_, reads=(), writes=(), waw=False, **kw):
        pool = self.dpool[queue]
        si = pool[self.dnext[queue] % len(pool)]
        self.dnext[queue] += 1
        if self.dcnt[si] > 0:
            self._need(queue, ('d', si, self.dcnt[si]))
        self._sync(queue, reads, writes, None, waw)
        self.dcnt[si] += 16
        n = self.dcnt[si]
        sem = self.dsem[si]
        self.q[queue].append(lambda e, o=out, i=in_, s=sem, k=kw: e.dma_start(out=o, in_=i, **k).then_inc(s, 16))
        dep = ('d', si, n)
        for r in reads:
            r.r[('d', si)] = dep
        for w in writes:
            w.w[('d', si)] = dep

    def collective(self, fn, reads, writes):
        fn = _freeze(fn)
        self._sync('pool', reads, writes, None, True)
        self.cccnt += 1
        n = self.cccnt
        sem = self.ccsem
        self.q['pool'].append(lambda e, f=fn, s=sem: f(e).then_inc(s, 1))
        dep = ('c', 0, n)
        for r in reads:
            r.r[('c', 0)] = dep
        for w in writes:
            w.w[('c', 0)] = dep

    def finish(self):
        for si, c in enumerate(self.dcnt):
            if c > 0:
                self._need('sp', ('d', si, c))
        for e in ('pe', 'act', 'dve', 'pool'):
            if self.cnt[e] > 0:
                self._need('sp', ('e', e, self.cnt[e]))
        if self.cccnt:
            self._need('sp', ('c', 0, self.cccnt))

    def emit(self):
        nc = self.nc
        q = self.q
        with nc.Block() as block:
            @block.tensor
            def _(e):
                for f in q['pe']:
                    f(e)

            @block.scalar
            def _(e):
                for f in q['act']:
                    f(e)

            @block.vector
            def _(e):
                for f in q['dve']:
                    f(e)

            @block.gpsimd
            def _(e):
                for f in q['pool']:
                    f(e)

            @block.sync
            def _(e):
                for f in q['sp']:
                    f(e)


class Arena:
    def __init__(self, tensor, size):
        self.t = tensor
        self.size = size
        self.top = 0
        self.live = []
        self.dead = []
        self.peak = 0

    def alloc(self, nwords, name=''):
        s = self.top
        e = s + nwords
        assert e <= self.size, "arena overflow %s %d > %d" % (name, e, self.size)
        self.top = e
        self.peak = max(self.peak, e)
        res = Res(name)
        keep = []
        for (a, b, r) in self.dead:
            if a < e and b > s:
                for dd in (r.w, r.r, r.pw, r.pr):
                    _merge(res.pr, dd)
                if a < s or b > e:
                    keep.append((a, b, r))
            else:
                keep.append((a, b, r))
        self.dead = keep
        self.live.append((s, e, res))
        return self.t[:, s:e], res

    def f32(self, shape, name=''):
        n = int(np.prod(shape[1:]))
        ap, res = self.alloc(n, name)
        ap = _shape(ap, shape)
        if shape[0] < 128:
            ap = ap[0:shape[0]]
        return ap, res

    def bf(self, shape, name=''):
        n = int(np.prod(shape[1:]))
        ap, res = self.alloc((n + 1) // 2, name)
        ap = ap.bitcast(BF16)[:, 0:n]
        ap = _shape(ap, shape)
        if shape[0] < 128:
            ap = ap[0:shape[0]]
        return ap, res

    def mark(self):
        return self.top

    def release(self, mark):
        keep = []
        for (s, e, r) in self.live:
            if s >= mark:
                self.dead.append((s, e, r))
            else:
                keep.append((s, e, r))
        self.live = keep
        self.top = mark


def bcast_last(ap, n):
    return bass.AP(ap.tensor, ap.offset, [list(d) for d in ap.ap] + [[0, n]])


def bcast_mid(ap, reps):
    d = [list(x) for x in ap.ap]
    return bass.AP(ap.tensor, ap.offset, [d[0], [0, reps]] + d[1:])


def _shape(ap, shape):
    if len(shape) == 2:
        return ap
    if len(shape) == 3:
        return ap.rearrange("p (a b) -> p a b", a=shape[1])
    if len(shape) == 4:
        return ap.rearrange("p (a b c) -> p a b c", a=shape[1], b=shape[2])
    raise ValueError(shape)


def build_program(enable_sample=True, debug=None):
    nc = bass.Bass("TRN2", target_bir_lowering=False)
    dbg_out = {}

    def din(name, shape, dt=F32):
        return nc.dram_tensor(name, list(shape), dt, kind="ExternalInput").ap()

    def dout(name, shape, dt=F32):
        return nc.dram_tensor(name, list(shape), dt, kind="ExternalOutput").ap()

    xin = [din("xp", [T, D]), din("xs", [T, D])]
    vecs_d = din("vecs", [NV, 128])
    bada_d = din("bada", [DEPTH * 9, 128])
    decay_d = din("decay", [DEPTH * 2 * 4])
    ident_d = din("ident", [128, 128])
    posd_d = din("posd", [4, 128, 128])
    posv_d = din("posv", [128, 4])
    posr_d = din("posr", [4, 128])
    rope_d = din("rope", [2, 128, T])
    sret_d = din("sret", [DEPTH, 2, 4, 128, 128])
    ck_d = din("ck", [DEPTH, PAST, 128])
    cv_d = din("cv", [DEPTH, PAST, 128])
    w_ada = din("w_ada", [DEPTH, D, 9 * D // NCORES])
    w_ffn_in = din("w_ffn_in", [DEPTH, 2, D, 2 * FFN])
    w_ffn_out = din("w_ffn_out", [DEPTH, 2, FFN, D])
    w_inx = din("w_inx", [DEPTH, D, NWX])
    w_ret_o = din("w_ret_o", [DEPTH, 512, D])
    w_att_o = din("w_att_o", [DEPTH, 512, D])
    w_conv_o = din("w_conv_o", [DEPTH, 512, D])
    w_o = din("w_o", [DEPTH, D, D])

    y_out = [dout("yp", [T, D]), dout("ys", [T, D])]
    nstate = dout("nstate", [4, DEPTH, 2, 4, 128, 128])
    nck = dout("nck", [4, DEPTH, 256, 128])
    ncv = dout("ncv", [4, DEPTH, 256, 128])

    ctab_d = din("ctab", [64])
    NXS = 1024
    NXK = 1024 + 1024 + 8
    sndS = [nc.dram_tensor("sndS%d" % l, [NXS, 128], BF16) for l in range(DEPTH)]
    rcvS = [nc.dram_tensor("rcvS%d" % l, [NCORES * NXS, 128], BF16) for l in range(DEPTH)]
    sndK = [nc.dram_tensor("sndK%d" % l, [NXK, 128], BF16) for l in range(DEPTH)]
    rcvK = [nc.dram_tensor("rcvK%d" % l, [NCORES * NXK, 128], BF16) for l in range(DEPTH)]

    stack = ExitStack()
    with stack:
        P = Prog(nc, stack)
        ARENA_WORDS = 53200
        arena_t = stack.enter_context(nc.sbuf_tensor("arena", [128, ARENA_WORDS], F32))
        A = Arena(arena_t, ARENA_WORDS)
        banks = []
        ps_all = stack.enter_context(nc.psum_tensor("ps", [128, 8 * 512], F32))
        for i in range(8):
            banks.append((ps_all[:, i * 512:(i + 1) * 512], Res("bank%d" % i)))
            banks[-1][1].excl = True
        bstate = {'i': 0}

        def bank():
            b = banks[bstate['i'] % 6]
            bstate['i'] += 1
            return b

        ident, ident_r = A.f32([128, 128], "ident")
        P.dma('sp', ident, ident_d[:, :], writes=[ident_r])
        ones_bf, ones_r = A.bf([128, 128], "ones")
        P.op('dve', lambda e: e.memset(ones_bf, 1.0), writes=[ones_r])
        blk_bf, blk_r = A.bf([128, 128], "blk")
        P.op('dve', lambda e: e.memset(blk_bf, 0.0), writes=[blk_r])
        P.op('dve', lambda e: e.memset(blk_bf[0:64, 0:64], 1.0), writes=[blk_r])
        P.op('dve', lambda e: e.memset(blk_bf[64:128, 64:128], 1.0), writes=[blk_r])

        def transpose_rows(src_dram, nrows, name):
            dst, dst_r = A.f32([128, nrows], name)
            m = A.mark()
            st, st_r = A.f32([128, 128], name + "_st")
            P.dma('sp', st[0:nrows, :], src_dram, writes=[st_r])
            bk, bk_r = bank()
            P.op('pe', lambda e: e.transpose(bk[:, 0:nrows], st[0:nrows, :], ident[0:nrows, 0:nrows]),
                 reads=[st_r, ident_r], writes=[bk_r])
            P.op('dve', lambda e: e.tensor_copy(out=dst, in_=bk[:, 0:nrows]), reads=[bk_r], writes=[dst_r])
            A.release(m)
            return dst, dst_r

        vecs, vecs_r = transpose_rows(vecs_d[:, :], NV, "vecs")
        badap, badap_r = transpose_rows(bada_d[:, :], DEPTH * 9, "badap")

        def vcol(r):
            return vecs[:, r:r + 1]

        ct0, ct0_r = A.f32([128, 64], "ctab")
        P.dma('sp', ct0, ctab_d.partition_broadcast(128), writes=[ct0_r])

        scond, scond_r = A.bf([128, 8, 3], "scond")
        for ci in range(3):
            P.op('act', lambda e, ci=ci: e.activation(out=scond[:, :, ci], in_=vecs[:, V_COND + ci * 8:V_COND + ci * 8 + 8],
                                                      func=AF.Silu), reads=[vecs_r], writes=[scond_r], waw=False)

        lgt, lgt_r = A.f32([128, 16], "lgt")
        P.dma('sp', lgt, decay_d.partition_broadcast(128), writes=[lgt_r])
        lg, lg_r = A.f32([128, 16], "lg")
        P.op('act', lambda e: e.activation(out=lg, in_=lgt, func=AF.Exp, scale=-1.0), reads=[lgt_r], writes=[lg_r])
        P.op('act', lambda e: e.activation(out=lg, in_=lg, func=AF.Ln, bias=1.0), reads=[lg_r], writes=[lg_r])
        P.op('dve', lambda e: e.tensor_scalar(out=lg, in0=lg, scalar1=-1.0, scalar2=None, op0=ALU.mult),
             reads=[lg_r], writes=[lg_r])
        posd, posd_r = A.f32([128, 4, 128], "posd")
        P.dma('sp', posd, posd_d.rearrange("k p m -> p k m"), writes=[posd_r])
        posv, posv_r = A.f32([128, 4], "posv")
        P.dma('sp', posv, posv_d[:, :], writes=[posv_r])
        posr, posr_r = A.f32([128, 4, 128], "posr")
        P.dma('sp', posr, posr_d.partition_broadcast(128), writes=[posr_r])

        maskT, maskT_r = A.f32([128, DEPTH * 4, 128], "maskT")
        dkv, dkv_r = A.f32([128, DEPTH * 4, 2], "dkv")
        qrow, qrow_r = A.f32([128, DEPTH * 4, 2, 128], "qrow")
        dcc, dcc_r = A.f32([128, DEPTH * 4, 2], "dcc")
        KS = 128.0 ** -0.5
        m0 = A.mark()
        tmpm, tmpm_r = A.f32([128, 128], "tmpm")
        tmp2, tmp2_r = A.f32([128, 128], "tmp2")
        for l in range(DEPTH):
            for h in range(4):
                lh = l * 4 + h
                cf = lg[:, l * 8 + h:l * 8 + h + 1]
                cb = lg[:, l * 8 + 4 + h:l * 8 + 4 + h + 1]
                P.op('act', lambda e, cf=cf: e.activation(out=tmpm, in_=posd[:, 0, :], func=AF.Exp, scale=cf),
                     reads=[posd_r, lg_r], writes=[tmpm_r])
                P.op('dve', lambda e: e.tensor_tensor(out=tmpm, in0=tmpm, in1=posd[:, 2, :], op=ALU.mult),
                     reads=[tmpm_r, posd_r], writes=[tmpm_r])
                P.op('act', lambda e, cb=cb: e.activation(out=tmp2, in_=posd[:, 1, :], func=AF.Exp, scale=cb),
                     reads=[posd_r, lg_r], writes=[tmp2_r])
                P.op('dve', lambda e: e.tensor_tensor(out=tmp2, in0=tmp2, in1=posd[:, 3, :], op=ALU.mult),
                     reads=[tmp2_r, posd_r], writes=[tmp2_r])
                P.op('dve', lambda e, lh=lh: e.tensor_tensor(out=maskT[:, lh, :], in0=tmpm, in1=tmp2, op=ALU.add),
                     reads=[tmpm_r, tmp2_r], writes=[maskT_r])
                P.op('dve', lambda e, lh=lh: e.tensor_tensor(out=maskT[:, lh, :], in0=maskT[:, lh, :], in1=ident, op=ALU.add),
                     reads=[maskT_r, ident_r], writes=[maskT_r])
                P.op('act', lambda e, cf=cf, lh=lh: e.activation(out=dkv[:, lh, 0:1], in_=posv[:, 1:2], func=AF.Exp, scale=cf),
                     reads=[posv_r, lg_r], writes=[dkv_r])
                P.op('act', lambda e, cb=cb, lh=lh: e.activation(out=dkv[:, lh, 1:2], in_=posv[:, 0:1], func=AF.Exp, scale=cb),
                     reads=[posv_r, lg_r], writes=[dkv_r])
                P.op('act', lambda e, cf=cf, lh=lh: e.activation(out=qrow[:, lh, 0, :], in_=posr[:, 2, :], func=AF.Exp, scale=cf),
                     reads=[posr_r, lg_r], writes=[qrow_r])
                P.op('act', lambda e, cb=cb, lh=lh: e.activation(out=qrow[:, lh, 1, :], in_=posr[:, 3, :], func=AF.Exp, scale=cb),
                     reads=[posr_r, lg_r], writes=[qrow_r])
                P.op('act', lambda e, cf=cf, lh=lh: e.activation(out=dcc[:, lh, 0:1], in_=cf, func=AF.Exp, scale=128.0),
                     reads=[lg_r], writes=[dcc_r])
                P.op('act', lambda e, cb=cb, lh=lh: e.activation(out=dcc[:, lh, 1:2], in_=cb, func=AF.Exp, scale=128.0),
                     reads=[lg_r], writes=[dcc_r])
        P.op('dve', lambda e: e.tensor_scalar(out=dkv, in0=dkv, scalar1=KS, scalar2=None, op0=ALU.mult),
             reads=[dkv_r], writes=[dkv_r])
        A.release(m0)

        SLOT_ELEMS = 22 * 256
        NSLOT = 3
        slots = [A.bf([128, SLOT_ELEMS], "wslot%d" % i) for i in range(NSLOT)]
        wst = {'i': 0}

        stash = {}

        def prefetch_panel(key, parts):
            stash[key] = load_panel(parts)

        def load_panel(parts, key=None):
            if key is not None and key in stash:
                return stash.pop(key)
            sl, sl_r = slots[wst['i'] % NSLOT]
            wst['i'] += 1
            for (off, kc, ncol, src) in parts:
                dst = sl[:, off:off + kc * ncol].rearrange("p (k n) -> p k n", k=kc)
                P.dma('pool', dst, src, writes=[sl_r])
            return sl, sl_r

        def wsrc(w2d, kc, c0, ncol):
            return w2d.rearrange("(k p) n -> p k n", p=128)[:, :, c0:c0 + ncol]

        def pview(sl, off, kc, ncol):
            return sl[:, off:off + kc * ncol].rearrange("p (k n) -> p k n", k=kc)

        h, _ = A.f32([128, 8, T], "h")
        h_r = [Res("h%d" % c) for c in range(8)]
        modsL = [A.f32([128, 72, 3], "mods%d" % l) for l in range(DEPTH)]
        modAL = [[A.f32([128, 3, 8], "modA%d%d" % (l, ci)) for ci in range(2)] for l in range(DEPTH)]
        modGL = [[A.f32([128, 3, 8], "modG%d%d" % (l, ci)) for ci in range(2)] for l in range(DEPTH)]
        cur = {}
        base_mark = A.mark()

        def dump(name, ap, res, shape):
            d = dout("dbg_" + name, shape)
            dbg_out[name] = shape
            P.dma('sp', d, ap, reads=[res])

        sndM = nc.dram_tensor("sndM", [128, DEPTH * 27], F32)
        rcvM = nc.dram_tensor("rcvM", [NCORES * 128, DEPTH * 27], F32)

        def compute_mods_all():
            m = A.mark()
            modp, modp_r = A.f32([128, DEPTH * 9, 3], "modp")
            bk, bk_r = bank()
            for l in range(DEPTH):
                for pn in range(3):
                    sl, sl_r = load_panel([(0, 8, 384, wsrc(w_ada[l], 8, pn * 384, 384))])
                    wv = pview(sl, 0, 8, 384)
                    for j in range(3):
                        col = (l * 9 + pn * 3 + j) * 3
                        for kc in range(8):
                            P.op('pe', lambda e: e.matmul(
                                bk[:, col:col + 3], wv[:, kc, j * 128:(j + 1) * 128], scond[:, kc, :],
                                start=(kc == 0), stop=(kc == 7)),
                                reads=[sl_r, scond_r], writes=[bk_r], inc=(kc == 7))
            P.op('dve', lambda e: e.tensor_tensor(
                out=modp, in0=bk[:, 0:DEPTH * 27].rearrange("p (a b) -> p a b", b=3), in1=bcast_last(badap, 3), op=ALU.add),
                reads=[bk_r, badap_r], writes=[modp_r])
            sm_r, rm_r = Res("sndM"), Res("rcvM")
            P.dma('sp', sndM.ap(), modp.rearrange("p a b -> p (a b)"), reads=[modp_r], writes=[sm_r])
            all_gather(sndM, rcvM, sm_r, rm_r)
            rv = rcvM.ap().rearrange("(q p) (l x) -> q l p x", q=NCORES, l=DEPTH)
            for l in range(DEPTH):
                mods, mods_r = modsL[l]
                for q in range(NCORES):
                    P.dma('sp', mods[:, q * 9:(q + 1) * 9, :].rearrange("p a b -> p (a b)"), rv[q, l], reads=[rm_r], writes=[mods_r])
            A.release(m)
            for l in range(DEPTH):
                mods, mods_r = modsL[l]
                P.op('dve', lambda e: e.tensor_scalar(out=mods[:, :, 1], in0=mods[:, :, 1], scalar1=ct0[:, 0:1], scalar2=None, op0=ALU.mult),
                     reads=[mods_r, ct0_r], writes=[mods_r])
                P.op('dve', lambda e: e.scalar_tensor_tensor(out=mods[:, :, 1], in0=mods[:, :, 2], scalar=ct0[:, 1:2], in1=mods[:, :, 1],
                                                             op0=ALU.mult, op1=ALU.add), reads=[mods_r, ct0_r], writes=[mods_r])
                derive_mods(l)

        def derive_mods(l):
            mods, mods_r = modsL[l]
            for ci in range(2):
                modA, modA_r = modAL[l][ci]
                modG, modG_r = modGL[l][ci]
                for i in range(3):
                    nw = vecs[:, V_NORM + (l * 3 + i) * 8:V_NORM + (l * 3 + i) * 8 + 8]
                    sc = mods[:, (3 * i + 1) * 8:(3 * i + 1) * 8 + 8, ci]
                    gt = mods[:, (3 * i + 2) * 8:(3 * i + 2) * 8 + 8, ci]
                    P.op('dve', lambda e: e.scalar_tensor_tensor(
                        out=modA[:, i, :], in0=sc, scalar=1.0, in1=nw, op0=ALU.add, op1=ALU.mult),
                        reads=[mods_r, vecs_r], writes=[modA_r])
                    P.op('dve', lambda e: e.tensor_scalar(
                        out=modG[:, i, :], in0=gt, scalar1=(1.0 if i == 1 else 0.5), scalar2=None, op0=ALU.mult),
                        reads=[mods_r], writes=[modG_r])

        def use_mods(l, ci):
            cur['mods'], cur['mods_r'] = modsL[l]
            cur['modA'], cur['modA_r'] = modAL[l][ci]
            cur['modG'], cur['modG_r'] = modGL[l][ci]

        def load_x(g):
            m = A.mark()
            xv = xin[g].rearrange("(t p) d -> p t d", p=128)
            sts = [A.f32([128, D], "xst%d" % k) for k in range(2)]
            for tt in range(8):
                st, st_r = sts[tt % 2]
                P.dma('sp', st, xv[:, tt, :], writes=[st_r])
                for half in range(2):
                    bk, bk_r = bank()
                    for j in range(4):
                        c = half * 4 + j
                        P.op('pe', lambda e, c=c, j=j, st=st, bk=bk: e.transpose(
                            bk[:, j * 128:(j + 1) * 128], st[:, c * 128:(c + 1) * 128], ident),
                            reads=[st_r, ident_r], writes=[bk_r])
                    P.op('dve' if half else 'act',
                         (lambda e, half=half, bk=bk, tt=tt: e.tensor_copy(
                             out=h[:, half * 4:half * 4 + 4, tt * 128:(tt + 1) * 128],
                             in_=bk.rearrange("p (a b) -> p a b", a=4))) if half else
                         (lambda e, half=half, bk=bk, tt=tt: e.copy(
                             out=h[:, half * 4:half * 4 + 4, tt * 128:(tt + 1) * 128],
                             in_=bk.rearrange("p (a b) -> p a b", a=4))),
                         reads=[bk_r], writes=h_r[half * 4:half * 4 + 4], waw=False)
            A.release(m)

        def store_y(g):
            m = A.mark()
            yv = y_out[g].rearrange("(t p) d -> p t d", p=128)
            sts = [A.f32([128, D], "yst%d" % i) for i in range(2)]
            for tt in range(8):
                st, st_r = sts[tt % 2]
                for half in range(2):
                    bk, bk_r = bank()
                    for j in range(4):
                        c = half * 4 + j
                        P.op('pe', lambda e, c=c, j=j, bk=bk, tt=tt: e.transpose(
                            bk[:, j * 128:(j + 1) * 128], h[:, c, tt * 128:(tt + 1) * 128], ident),
                            reads=[h_r[c], ident_r], writes=[bk_r])
                    if half:
                        P.op('dve', lambda e, bk=bk, st=st, half=half: e.tensor_copy(out=st[:, half * 512:(half + 1) * 512], in_=bk),
                             reads=[bk_r], writes=[st_r], waw=False)
                    else:
                        P.op('act', lambda e, bk=bk, st=st, half=half: e.copy(out=st[:, half * 512:(half + 1) * 512], in_=bk),
                             reads=[bk_r], writes=[st_r], waw=False)
                P.dma('sp', yv[:, tt, :], st, reads=[st_r])
            A.release(m)

        ssq = {'pend': [], 'k': 0}

        def emit_sumsq(c, s):
            ts = slice(s * SUB, (s + 1) * SUB)
            sqa, sqr = sqp[ssq['k'] % len(sqp)]
            ssq['k'] += 1
            P.op('act', lambda e: e.activation(out=sqa, in_=h[:, c, ts], func=AF.Square), reads=[h_r[c]], writes=[sqr])
            bk, bk_r = banks[6 + s]

            def mm():
                P.op('pe', lambda e: e.matmul(bk, ones_bf, sqa, start=(c == 0), stop=(c == 7)), reads=[sqr, ones_r], writes=[bk_r])
            ssq['pend'].append(mm)

        def flush_ss(keep=0):
            while len(ssq['pend']) > keep:
                ssq['pend'].pop(0)()

        def norm_stage(i, ci, xn, xn_r, have_ss=True):
            m = A.mark()
            rstd = [A.f32([128, SUB], "rstd%d" % k) for k in range(2)]
            tmp = [A.f32([128, SUB], "ntmp%d" % k) for k in range(4)]
            if not have_ss:
                for s in range(NS):
                    for c in range(8):
                        emit_sumsq(c, s)
                        flush_ss(keep=1)
            flush_ss()
            modA, modA_r, mods, mods_r = cur['modA'], cur['modA_r'], cur['mods'], cur['mods_r']
            for s in range(NS):
                ts = slice(s * SUB, (s + 1) * SUB)
                bk, bk_r = banks[6 + s]
                rs, rs_r = rstd[s]
                t0a, t0r = tmp[2 * s]
                t1a, t1r = tmp[2 * s + 1]
                P.op('act', lambda e: e.activation(out=t0a, in_=bk, func=AF.Ln, scale=1.0 / D, bias=eps_ap),
                     reads=[bk_r, eps_r], writes=[t0r])
                P.op('act', lambda e: e.activation(out=rs, in_=t0a, func=AF.Exp, scale=-0.5), reads=[t0r], writes=[rs_r])
            for s in range(NS):
                ts = slice(s * SUB, (s + 1) * SUB)
                rs, rs_r = rstd[s]
                for c in range(8):
                    ta, tr = tmp[c % 4]
                    P.op('dve', lambda e: e.scalar_tensor_tensor(
                        out=ta, in0=h[:, c, ts], scalar=modA[:, i, c:c + 1], in1=rs, op0=ALU.mult, op1=ALU.mult),
                        reads=[h_r[c], modA_r, rs_r], writes=[tr])
                    P.op('act', lambda e: e.activation(
                        out=xn[:, c, ts], in_=ta, func=AF.Identity, bias=mods[:, 3 * i * 8 + c, ci:ci + 1], scale=1.0),
                        reads=[tr, mods_r], writes=[xn_r], waw=False)
            A.release(m)

        def h_update(bk, bk_r, i, c, ts, s):
            modG, modG_r = cur['modG'], cur['modG_r']
            P.op('dve', lambda e: e.scalar_tensor_tensor(
                out=h[:, c, ts], in0=bk, scalar=modG[:, i, c:c + 1], in1=h[:, c, ts], op0=ALU.mult, op1=ALU.add),
                reads=[bk_r, modG_r, h_r[c]], writes=[h_r[c]])
            emit_sumsq(c, s)
            flush_ss(keep=2)

        def ffn_stage(l, f, i, xn, xn_r):
            m = A.mark()
            hid, hid_r = A.bf([128, NFC, T], "hid")
            stm = [A.f32([128, SUB], "silu%d" % k) for k in range(3)]
            wi = w_ffn_in[l, f]
            k = 0
            for u in range(NFC // 2):
                sl, sl_r = load_panel([(0, 8, 256, wsrc(wi, 8, u * 256, 256)),
                                       (8 * 256, 8, 256, wsrc(wi, 8, FFN + u * 256, 256))])
                gv = pview(sl, 0, 8, 256)
                uv = pview(sl, 8 * 256, 8, 256)
                for j in range(2):
                    fc = u * 2 + j
                    for s in range(NS):
                        ts = slice(s * SUB, (s + 1) * SUB)
                        bg, bg_r = bank()
                        for kc in range(8):
                            P.op('pe', lambda e, bg=bg, gv=gv, kc=kc, j=j, ts=ts: e.matmul(
                                bg, gv[:, kc, j * 128:(j + 1) * 128], xn[:, kc, ts], start=(kc == 0), stop=(kc == 7)),
                                reads=[sl_r, xn_r], writes=[bg_r], inc=(kc == 7))
                        bu, bu_r = bank()
                        for kc in range(8):
                            P.op('pe', lambda e, bu=bu, uv=uv, kc=kc, j=j, ts=ts: e.matmul(
                                bu, uv[:, kc, j * 128:(j + 1) * 128], xn[:, kc, ts], start=(kc == 0), stop=(kc == 7)),
                                reads=[sl_r, xn_r], writes=[bu_r], inc=(kc == 7))
                        sa, sr = stm[k % 3]
                        k += 1
                        P.op('act', lambda e, sa=sa, bg=bg: e.activation(out=sa, in_=bg, func=AF.Silu),
                             reads=[bg_r], writes=[sr])
                        P.op('dve', lambda e, sa=sa, bu=bu, fc=fc, ts=ts: e.tensor_tensor(
                            out=hid[:, fc, ts], in0=bu, in1=sa, op=ALU.mult),
                            reads=[bu_r, sr], writes=[hid_r], waw=False)
            wo = w_ffn_out[l, f]
            for pn in range(4):
                sl, sl_r = load_panel([(0, NFC, 256, wsrc(wo, NFC, pn * 256, 256))])
                wv = pview(sl, 0, NFC, 256)
                for j in range(2):
                    c = pn * 2 + j
                    for s in range(NS):
                        ts = slice(s * SUB, (s + 1) * SUB)
                        bk, bk_r = bank()
                        for kc in range(NFC):
                            P.op('pe', lambda e, bk=bk, wv=wv, kc=kc, j=j, ts=ts: e.matmul(
                                bk, wv[:, kc, j * 128:(j + 1) * 128], hid[:, kc, ts], start=(kc == 0), stop=(kc == NFC - 1)),
                                reads=[sl_r, hid_r], writes=[bk_r], inc=(kc == NFC - 1))
                        h_update(bk, bk_r, i, c, ts, s)
            flush_ss()
            A.release(m)

        eps_ap, eps_r = None, None

        def setup_eps():
            nonlocal eps_ap, eps_r
            eps_ap, eps_r = A.f32([128, 1], "eps")
            P.op('dve', lambda e: e.memset(eps_ap, EPS), writes=[eps_r])

        def proj_fm(sl_r, wv, col0, xn, xn_r, s, kcn=8):
            ts = slice(s * SUB, (s + 1) * SUB)
            bk, bk_r = bank()
            for kc in range(kcn):
                P.op('pe', lambda e, bk=bk, kc=kc: e.matmul(
                    bk, wv[:, kc, col0:col0 + 128], xn[:, kc, ts], start=(kc == 0), stop=(kc == kcn - 1)),
                    reads=[sl_r, xn_r], writes=[bk_r], inc=(kc == kcn - 1))
            return bk, bk_r

        def retention(l, g, xn, xn_r, orn, orn_r, seeds=None, phase1=None):
            wi = w_inx[l]
            segs = [(sq_ * 2, 2) for sq_ in range(4)] if g == 0 else [(0, 8)]
            m = A.mark()
            nset = 1 if phase1 is not None else 2
            sets = []
            for k_ in range(nset):
                B = {}
                for nm in ("vtok", "kdf", "kdb"):
                    B[nm] = A.bf([128, 8, 128], nm + str(k_))
                B["Sbf"] = A.bf([128, 8, 2, 128], "Sbf%d" % k_)
                B["S32"] = A.f32([128, 2, 2, 128], "S32r%d" % k_)
                if phase1 is None:
                    for nm in ("qT", "kT", "sg", "qdf", "qdb"):
                        B[nm] = A.bf([128, T], nm + str(k_))
                    B["am"] = A.bf([128, 8, 128], "am%d" % k_)
                sets.append(B)
            P32, P32_r = A.f32([128, 8, 2, 128], "P32")
            if g == 0:
                stout, stout_r = A.f32([128, 4, 2, 128], "stout")
            if phase1 is None:
                osq = [A.bf([128, SUB], "osq%d" % k_) for k_ in range(2)]
                ors = [A.f32([128, SUB], "ors%d" % k_) for k_ in range(2)]
                otm = [A.f32([128, SUB], "otm%d" % k_) for k_ in range(2)]
            state = {}

            def front(hh):
                lh = l * 4 + hh
                B = sets[hh % nset]
                vtok, vtok_r = B["vtok"]
                kdf, kdf_r = B["kdf"]
                kdb, kdb_r = B["kdb"]
                Sbf, Sbf_r = B["Sbf"]
                S32, S32_r = B["S32"]
                sl, sl_r = load_panel([(0, 8, 512, wsrc(wi, 8, X_RET + hh * 512, 512))], key=("ret", l, hh, phase1 is None))
                wv = pview(sl, 0, 8, 512)
                for tp in range(4):
                    bk, bk_r = bank()
                    for q2 in range(2):
                        tt = tp * 2 + q2
                        for kc in range(8):
                            P.op('pe', lambda e: e.matmul(
                                bk[:, q2 * 256:(q2 + 1) * 256], xn[:, kc, tt * 128:(tt + 1) * 128], wv[:, kc, 128:384],
                                start=(kc == 0), stop=(kc == 7)),
                                reads=[sl_r, xn_r], writes=[bk_r], inc=(kc == 7))
                    b3 = bk.rearrange("p (a b) -> p a b", a=2)
                    P.op('act', lambda e: e.activation(
                        out=kdf[:, tp * 2:tp * 2 + 2, :], in_=b3[:, :, 0:128], func=AF.Copy, scale=dkv[:, lh, 0:1]),
                        reads=[bk_r, dkv_r], writes=[kdf_r], waw=False)
                    P.op('dve', lambda e: e.tensor_scalar(
                        out=kdb[:, tp * 2:tp * 2 + 2, :], in0=b3[:, :, 0:128], scalar1=dkv[:, lh, 1:2], scalar2=None, op0=ALU.mult),
                        reads=[bk_r, dkv_r], writes=[kdb_r], waw=False)
                    P.op('act', lambda e: e.copy(out=vtok[:, tp * 2:tp * 2 + 2, :], in_=b3[:, :, 128:256]),
                         reads=[bk_r], writes=[vtok_r], waw=False)
                if phase1 is None:
                    qT, qT_r = B["qT"]
                    kT, kT_r = B["kT"]
                    sg, sg_r = B["sg"]
                    qdf, qdf_r = B["qdf"]
                    qdb, qdb_r = B["qdb"]
                    for s in range(NS):
                        ts = slice(s * SUB, (s + 1) * SUB)
                        bk, bk_r = proj_fm(sl_r, wv, 0, xn, xn_r, s)
                        b3 = bk.rearrange("p (a b) -> p a b", a=4)
                        P.op('act', lambda e: e.copy(out=qT[:, ts], in_=bk), reads=[bk_r], writes=[qT_r], waw=False)
                        P.op('dve', lambda e: e.tensor_tensor(
                            out=qdf[:, ts].rearrange("p (a b) -> p a b", a=4), in0=b3, in1=bcast_mid(qrow[:, lh, 0, :], 4), op=ALU.mult),
                            reads=[bk_r, qrow_r], writes=[qdf_r], waw=False)
                        P.op('dve', lambda e: e.tensor_tensor(
                            out=qdb[:, ts].rearrange("p (a b) -> p a b", a=4), in0=b3, in1=bcast_mid(qrow[:, lh, 1, :], 4), op=ALU.mult),
                            reads=[bk_r, qrow_r], writes=[qdb_r], waw=False)
                        bk, bk_r = proj_fm(sl_r, wv, 128, xn, xn_r, s)
                        P.op('act', lambda e: e.activation(out=kT[:, ts], in_=bk, func=AF.Copy, scale=KS),
                             reads=[bk_r], writes=[kT_r], waw=False)
                        bk, bk_r = proj_fm(sl_r, wv, 384, xn, xn_r, s)
                        P.op('act', lambda e: e.activation(out=sg[:, ts], in_=bk, func=AF.Silu),
                             reads=[bk_r], writes=[sg_r], waw=False)
                for jp in range(4):
                    bk, bk_r = bank()
                    for q2 in range(2):
                        j = jp * 2 + q2
                        P.op('pe', lambda e: e.matmul(
                            bk[:, q2 * 256:q2 * 256 + 128], kdf[:, j, :], vtok[:, j, :], start=True, stop=True),
                            reads=[kdf_r, vtok_r], writes=[bk_r])
                        P.op('pe', lambda e: e.matmul(
                            bk[:, q2 * 256 + 128:q2 * 256 + 256], kdb[:, j, :], vtok[:, j, :], start=True, stop=True),
                            reads=[kdb_r, vtok_r], writes=[bk_r])
                    b4 = bk.rearrange("p (a b c) -> p a b c", a=2, b=2)
                    if jp % 2:
                        P.op('dve', lambda e: e.tensor_copy(out=P32[:, jp * 2:jp * 2 + 2, :, :], in_=b4),
                             reads=[bk_r], writes=[P32_r], waw=False)
                    else:
                        P.op('act', lambda e: e.copy(out=P32[:, jp * 2:jp * 2 + 2, :, :], in_=b4),
                             reads=[bk_r], writes=[P32_r], waw=False)
                if phase1 is None:
                    am, am_r = B["am"]
                    for s in range(NS):
                        ba, ba_r = bank()
                        for jj in range(4):
                            j = s * 4 + jj
                            cs = slice(j * 128, (j + 1) * 128)
                            P.op('pe', lambda e: e.matmul(
                                ba[:, jj * 128:(jj + 1) * 128], kT[:, cs], qT[:, cs], start=True, stop=True),
                                reads=[kT_r, qT_r], writes=[ba_r])
                        P.op('dve', lambda e: e.tensor_tensor(
                            out=am[:, s * 4:s * 4 + 4, :], in0=ba.rearrange("p (a b) -> p a b", a=4),
                            in1=bcast_mid(maskT[:, lh, :], 4), op=ALU.mult),
                            reads=[ba_r, maskT_r], writes=[am_r], waw=False)
                has_f = {}
                has_b = {}
                for si, (j0, n) in enumerate(segs):
                    for d_, dc in ((0, dcc[:, lh, 0:1]), (1, dcc[:, lh, 1:2])):
                        order = list(range(j0, j0 + n)) if d_ == 0 else list(range(j0 + n - 1, j0 - 1, -1))
                        has = has_f if d_ == 0 else has_b
                        seed = None if seeds is None else seeds[d_]
                        cur32 = None
                        for k_, j in enumerate(order):
                            nxt = S32[:, k_ % 2, d_, :]
                            if k_ == 0:
                                if seed is not None:
                                    sap, s_r = seed
                                    P.op('dve', lambda e: e.tensor_copy(out=nxt, in_=sap[:, hh, :]),
                                         reads=[s_r], writes=[S32_r], waw=False)
                                    has[j] = True
                                    cur32 = nxt
                                else:
                                    has[j] = False
                            else:
                                pj = order[k_ - 1]
                                if has[pj]:
                                    P.op('dve', lambda e: e.scalar_tensor_tensor(
                                        out=nxt, in0=cur32, scalar=dc, in1=P32[:, pj, d_, :],
                                        op0=ALU.mult, op1=ALU.add), reads=[S32_r, P32_r, dcc_r], writes=[S32_r], waw=False)
                                    cur32 = nxt
                                else:
                                    cur32 = P32[:, pj, d_, :]
                                has[j] = True
                            if has[j]:
                                if cur32 is nxt:
                                    P.op('dve', lambda e: e.tensor_copy(out=Sbf[:, j, d_, :], in_=cur32),
                                         reads=[S32_r], writes=[Sbf_r], waw=False)
                                else:
                                    P.op('dve', lambda e: e.tensor_copy(out=Sbf[:, j, d_, :], in_=cur32),
                                         reads=[P32_r], writes=[Sbf_r], waw=False)
                        if g == 0 or phase1 is not None:
                            lj = order[-1]
                            dst = stout[:, si, d_, :] if g == 0 else phase1[0][:, hh, d_, :]
                            dst_r = stout_r if g == 0 else phase1[1]
                            if has[lj]:
                                P.op('dve', lambda e: e.scalar_tensor_tensor(
                                    out=dst, in0=cur32, scalar=dc, in1=P32[:, lj, d_, :],
                                    op0=ALU.mult, op1=ALU.add), reads=[S32_r, P32_r, dcc_r], writes=[dst_r], waw=False)
                            else:
                                P.op('dve', lambda e: e.tensor_copy(out=dst, in_=P32[:, lj, d_, :]),
                                     reads=[P32_r], writes=[dst_r], waw=False)
                if g == 0:
                    for sq_ in range(4):
                        P.dma('sp', nstate[sq_, l, :, hh].rearrange("r d e -> d r e"), stout[:, sq_, :, :], reads=[stout_r])
                state[hh] = (has_f, has_b)

            def back(hh):
                lh = l * 4 + hh
                B = sets[hh % nset]
                vtok, vtok_r = B["vtok"]
                Sbf, Sbf_r = B["Sbf"]
                sg, sg_r = B["sg"]
                qdf, qdf_r = B["qdf"]
                qdb, qdb_r = B["qdb"]
                am, am_r = B["am"]
                has_f, has_b = state[hh]
                bos = []
                for s in range(NS):
                    bo, bo_r = bank()
                    bos.append((bo, bo_r))
                    for jj in range(4):
                        j = s * 4 + jj
                        cs = slice(j * 128, (j + 1) * 128)
                        ops = [(vtok[:, j, :], am[:, j, :], [vtok_r, am_r])]
                        if has_f[j]:
                            ops.append((Sbf[:, j, 0, :], qdf[:, cs], [Sbf_r, qdf_r]))
                        if has_b[j]:
                            ops.append((Sbf[:, j, 1, :], qdb[:, cs], [Sbf_r, qdb_r]))
                        n_ = len(ops)
                        for k_, (lt, rh, rr) in enumerate(ops):
                            P.op('pe', lambda e: e.matmul(
                                bo[:, jj * 128:(jj + 1) * 128], lt, rh, start=(k_ == 0), stop=(k_ == n_ - 1)),
                                reads=rr, writes=[bo_r], inc=(k_ == n_ - 1))
                for s in range(NS):
                    ts = slice(s * SUB, (s + 1) * SUB)
                    bo, bo_r = bos[s]
                    sqa, sqa_r = osq[s % 2]
                    rsa, rsa_r = ors[s % 2]
                    tma, tma_r = otm[s % 2]
                    P.op('act', lambda e: e.activation(out=sqa, in_=bo, func=AF.Square), reads=[bo_r], writes=[sqa_r])
                    bs, bs_r = bank()
                    P.op('pe', lambda e: e.matmul(bs, ones_bf, sqa, start=True, stop=True),
                         reads=[sqa_r, ones_r], writes=[bs_r])
                    P.op('act', lambda e: e.activation(out=tma, in_=bs, func=AF.Ln, scale=1.0 / 128, bias=eps_ap),
                         reads=[bs_r, eps_r], writes=[tma_r])
                    P.op('act', lambda e: e.activation(out=rsa, in_=tma, func=AF.Exp, scale=-0.5), reads=[tma_r], writes=[rsa_r])
                    P.op('dve', lambda e: e.scalar_tensor_tensor(
                        out=tma, in0=bo, scalar=vcol(V_GN + lh), in1=rsa, op0=ALU.mult, op1=ALU.mult),
                        reads=[bo_r, rsa_r, vecs_r], writes=[tma_r])
                    P.op('dve', lambda e: e.tensor_tensor(out=orn[:, hh, ts], in0=tma, in1=sg[:, ts], op=ALU.mult),
                         reads=[tma_r, sg_r], writes=[orn_r], waw=False)

            if phase1 is not None:
                for hh in range(4):
                    front(hh)
            else:
                front(0)
                for hh in range(4):
                    if hh + 1 < 4:
                        front(hh + 1)
                    back(hh)
            A.release(m)

        def norm_rope(bq, bq_r, bqs, bqs_r, gcol, gscol, ts, outs, tmps, rope_tabs):
            sqa, sqa_r = tmps['sq']
            rsa, rsa_r = tmps['rs']
            t1, t1_r = tmps['t1']
            P.op('act', lambda e: e.activation(out=sqa, in_=bq, func=AF.Square), reads=[bq_r], writes=[sqa_r])
            bs, bs_r = bank()
            P.op('pe', lambda e: e.matmul(bs, blk_bf, sqa, start=True, stop=True), reads=[sqa_r, blk_r], writes=[bs_r])
            P.op('act', lambda e: e.activation(out=t1, in_=bs, func=AF.Ln, scale=1.0 / 64, bias=eps_ap),
                 reads=[bs_r, eps_r], writes=[t1_r])
            P.op('act', lambda e: e.activation(out=rsa, in_=t1, func=AF.Exp, scale=-0.5), reads=[t1_r], writes=[rsa_r])
            if rope_tabs is None:
                P.op('dve', lambda e: e.scalar_tensor_tensor(out=t1, in0=bq, scalar=vcol(gcol), in1=rsa, op0=ALU.mult, op1=ALU.mult),
                     reads=[bq_r, rsa_r, vecs_r], writes=[t1_r])
            else:
                rp, rp_r = rope_tabs
                t2, t2_r = tmps['t2']
                P.op('dve', lambda e: e.scalar_tensor_tensor(out=t1, in0=bq, scalar=vcol(gcol), in1=rp[:, 0, ts], op0=ALU.mult, op1=ALU.mult),
                     reads=[bq_r, rp_r, vecs_r], writes=[t1_r])
                P.op('dve', lambda e: e.scalar_tensor_tensor(out=t2, in0=bqs, scalar=vcol(gscol), in1=rp[:, 1, ts], op0=ALU.mult, op1=ALU.mult),
                     reads=[bqs_r, rp_r, vecs_r], writes=[t2_r])
                P.op('dve', lambda e: e.tensor_tensor(out=t1, in0=t1, in1=t2, op=ALU.add), reads=[t1_r, t2_r], writes=[t1_r])
                P.op('dve', lambda e: e.tensor_tensor(out=t1, in0=t1, in1=rsa, op=ALU.mult), reads=[t1_r, rsa_r], writes=[t1_r])
            for k_, (oa, oa_r) in enumerate(outs):
                if k_ % 2 == 0:
                    P.op('act', lambda e, oa=oa: e.copy(out=oa, in_=t1), reads=[t1_r], writes=[oa_r], waw=False)
                else:
                    P.op('dve', lambda e, oa=oa: e.tensor_copy(out=oa, in_=t1), reads=[t1_r], writes=[oa_r], waw=False)

        def attention(l, g, xn, xn_r, oatt, oatt_r, rope_tabs=None, exch=None, skip_kv=False):
            wi = w_inx[l]
            m = A.mark()
            nkch = 8 if g == 0 else NKS // 128
            koff = 0 if g == 0 else PAST // 128
            QT, QT_r = A.bf([128, 4, T], "QT")
            KT, KT_r = A.bf([128, nkch * 128], "KT")
            Vx, Vx_r = A.bf([128, nkch, 2, 128], "Vx")
            P.op('dve', lambda e: e.memset(Vx[:, :, :, 64:128], 1.0), writes=[Vx_r])
            tmps = {'sq': A.bf([128, SUB], "asq"), 'rs': A.f32([128, SUB], "ars"), 't1': A.f32([128, SUB], "at1"),
                    't2': A.f32([128, SUB], "at2")}
            slA, slA_r = load_panel([(0, 8, 512, wsrc(wi, 8, X_AQ, 512))])
            wA = pview(slA, 0, 8, 512)
            if rope_tabs is not None:
                slB, slB_r = load_panel([(0, 8, 512, wsrc(wi, 8, X_AQS, 512))])
                wB = pview(slB, 0, 8, 512)
            for j in range(4):
                for s in range(NS):
                    ts = slice(s * SUB, (s + 1) * SUB)
                    bq, bq_r = proj_fm(slA_r, wA, j * 128, xn, xn_r, s)
                    bqs, bqs_r = (None, None)
                    if rope_tabs is not None:
                        bqs, bqs_r = proj_fm(slB_r, wB, j * 128, xn, xn_r, s)
                    norm_rope(bq, bq_r, bqs, bqs_r, V_QG + l, V_QGS + l, ts, [(QT[:, j, ts], QT_r)], tmps, rope_tabs)
            if not skip_kv:
                slC, slC_r = load_panel([(0, 8, 384, wsrc(wi, 8, X_AK, 384))])
                wC = pview(slC, 0, 8, 384)
                kf32 = None
                if g == 0:
                    kf32, kf32_r = A.f32([128, T], "kf32")
                    vout, vout_r = A.f32([128, 8, 128], "vout")
                    kout, kout_r = A.f32([128, 8, 128], "kout")
                for s in range(NS):
                    ts = slice(s * SUB, (s + 1) * SUB)
                    bq, bq_r = proj_fm(slC_r, wC, 0, xn, xn_r, s)
                    bqs, bqs_r = (None, None)
                    if rope_tabs is not None:
                        bqs, bqs_r = proj_fm(slC_r, wC, 128, xn, xn_r, s)
                    kts = slice(koff * 128 + s * SUB, koff * 128 + (s + 1) * SUB)
                    outs = [(KT[:, kts], KT_r)]
                    if g == 0:
                        outs.append((kf32[:, ts], kf32_r))
                    norm_rope(bq, bq_r, bqs, bqs_r, V_KG + l, V_KGS + l, ts, outs, tmps, rope_tabs)
                for tp in range(2):
                    bk, bk_r = bank()
                    for q4 in range(4):
                        tt = tp * 4 + q4
                        for kc in range(8):
                            P.op('pe', lambda e, bk=bk, kc=kc, tt=tt, q4=q4: e.matmul(
                                bk[:, q4 * 128:(q4 + 1) * 128], xn[:, kc, tt * 128:(tt + 1) * 128], wC[:, kc, 256:384],
                                start=(kc == 0), stop=(kc == 7)),
                                reads=[slC_r, xn_r], writes=[bk_r], inc=(kc == 7))
                    P.op('act', lambda e, bk=bk, tp=tp: e.copy(
                        out=Vx[:, koff + tp * 4:koff + tp * 4 + 4, :, 0:64], in_=bk.rearrange("p (a b c) -> p a b c", a=4, b=2)),
                        reads=[bk_r], writes=[Vx_r])
                    if g == 0:
                        P.op('dve', lambda e, bk=bk, tp=tp: e.tensor_copy(
                            out=vout[:, tp * 4:tp * 4 + 4, :], in_=bk.rearrange("p (a b) -> p a b", a=4)),
                            reads=[bk_r], writes=[vout_r], waw=False)
            if g == 0:
                for sq_ in range(4):
                    P.dma('sp', ncv[sq_, l].rearrange("(t p) f -> p t f", p=128), vout[:, sq_ * 2:sq_ * 2 + 2, :], reads=[vout_r])
                for tp in range(2):
                    bk, bk_r = bank()
                    for q4 in range(4):
                        tt = tp * 4 + q4
                        P.op('pe', lambda e, bk=bk, tt=tt, q4=q4: e.transpose(
                            bk[:, q4 * 128:(q4 + 1) * 128], kf32[:, tt * 128:(tt + 1) * 128], ident),
                            reads=[kf32_r, ident_r], writes=[bk_r])
                    P.op('dve', lambda e, bk=bk, tp=tp: e.tensor_copy(
                        out=kout[:, tp * 4:tp * 4 + 4, :], in_=bk.rearrange("p (a b) -> p a b", a=4)),
                        reads=[bk_r], writes=[kout_r], waw=False)
                for sq_ in range(4):
                    P.dma('sp', nck[sq_, l].rearrange("(t p) f -> p t f", p=128), kout[:, sq_ * 2:sq_ * 2 + 2, :], reads=[kout_r])
            if exch is not None:
                exch(KT, KT_r, Vx, Vx_r)
            LA = 2
            pT = [A.bf([128, 2 * SUB], "pT%d" % k_) for k_ in range(LA + 2)]
            rec = [A.f32([64, SUB], "rec%d" % k_) for k_ in range(2)]
            recs = A.f32([64, SUB], "recs")
            items = []
            for qb in range(8):
                kchs = [2 * (qb // 2), 2 * (qb // 2) + 1] if g == 0 else list(range(nkch))
                for ki, kch in enumerate(kchs):
                    items.append(dict(q0=qb * 128, qb=qb, kch=kch, first=(ki == 0), last=(ki == len(kchs) - 1)))

            def front(i, it):
                kch, q0 = it['kch'], it['q0']
                pr = 2 * (i % 2)
                for gg in range(2):
                    ps_ = slice(gg * 64, (gg + 1) * 64)
                    bs, bs_r = banks[pr + gg]
                    P.op('pe', lambda e: e.matmul(bs, KT[ps_, kch * 128:(kch + 1) * 128], QT[ps_, :, q0:q0 + 128], start=True, stop=True),
                         reads=[KT_r, QT_r], writes=[bs_r])
                pa, pa_r = pT[i % (LA + 2)]
                pair = ps_all[:, pr * 512:(pr + 2) * 512]
                P.op('act', lambda e: e.activation(out=pa, in_=pair, func=AF.Exp, scale=0.125),
                     reads=[banks[pr][1], banks[pr + 1][1]], writes=[pa_r])
                it['pa'] = (pa, pa_r)

            def back(it):
                pa, pa_r = it['pa']
                kch, q0 = it['kch'], it['q0']
                first, last = it['first'], it['last']
                for gg in range(2):
                    bo, bo_r = banks[4 + 2 * (it['qb'] % 2) + gg]
                    P.op('pe', lambda e: e.matmul(bo, Vx[:, kch, gg, :], pa[:, gg * SUB:(gg + 1) * SUB], start=first, stop=last),
                         reads=[Vx_r, pa_r], writes=[bo_r])
                if last:
                    for gg in range(2):
                        ps_ = slice(gg * 64, (gg + 1) * 64)
                        bo, bo_r = banks[4 + 2 * (it['qb'] % 2) + gg]
                        ra, ra_r = rec[gg]
                        rs2, rs2_r = recs
                        P.op('act', lambda e: e.activation(out=rs2, in_=bo[64:128, :], func=AF.Ln), reads=[bo_r], writes=[rs2_r])
                        P.op('act', lambda e: e.activation(out=ra, in_=rs2, func=AF.Exp, scale=-1.0), reads=[rs2_r], writes=[ra_r])
                        P.op('dve', lambda e: e.tensor_tensor(
                            out=oatt[ps_, :, q0:q0 + 128], in0=bo[0:64, :].rearrange("p (a b) -> p a b", a=4),
                            in1=ra.rearrange("p (a b) -> p a b", a=4), op=ALU.mult),
                            reads=[bo_r, ra_r], writes=[oatt_r], waw=False)

            for i in range(len(items) + LA):
                if i < len(items):
                    front(i, items[i])
                if i >= LA:
                    back(items[i - LA])
            A.release(m)

        def conv_stage(l, g, xn, xn_r, cvo, cvo_r, halo=None, edges=None):
            wi = w_inx[l]
            nseg, L = (4, 256) if g == 0 else (1, 1024)
            for c in range(4):
                m = A.mark()
                sl, sl_r = load_panel([(0, 8, 384, wsrc(wi, 8, X_CONV + c * 384, 384))], key=("conv", l, c))
                wv = pview(sl, 0, 8, 384)
                u, u_r = A.f32([128, nseg, L + 2], "u")
                cbs, cbs_r = A.f32([128, T], "cbs")
                acc, acc_r = A.f32([128, T], "acc")
                cxs = [A.f32([128, SUB], "cxs%d" % k_) for k_ in range(2)]
                if halo is None:
                    P.op('dve', lambda e, u=u: e.memset(u[:, :, 0:1], 0.0), writes=[u_r], waw=False)
                    P.op('dve', lambda e, u=u: e.memset(u[:, :, L + 1:L + 2], 0.0), writes=[u_r], waw=False)
                else:
                    halo(c, u, u_r)
                for s in range(NS):
                    ts = slice(s * SUB, (s + 1) * SUB)
                    bcb, bcb_r = proj_fm(sl_r, wv, 0, xn, xn_r, s)
                    bcc, bcc_r = proj_fm(sl_r, wv, 128, xn, xn_r, s)
                    bcx, bcx_r = proj_fm(sl_r, wv, 256, xn, xn_r, s)
                    cxa, cxa_r = cxs[s % 2]
                    P.op('act', lambda e, bcx=bcx, cxa=cxa: e.copy(out=cxa, in_=bcx), reads=[bcx_r], writes=[cxa_r])
                    P.op('act', lambda e, bcb=bcb, ts=ts: e.copy(out=cbs[:, ts], in_=bcb), reads=[bcb_r], writes=[cbs_r], waw=False)
                    if g == 0:
                        uo = u[:, 2 * s:2 * s + 2, 1:L + 1]
                        i0 = bcc.rearrange("p (a b) -> p a b", a=2)
                        i1 = cxa.rearrange("p (a b) -> p a b", a=2)
                    else:
                        uo = u[:, 0, 1 + s * SUB:1 + (s + 1) * SUB]
                        i0 = bcc
                        i1 = cxa
                    P.op('dve', lambda e, uo=uo, i0=i0, i1=i1: e.tensor_tensor(out=uo, in0=i0, in1=i1, op=ALU.mult),
                         reads=[bcc_r, cxa_r], writes=[u_r], waw=False)
                a3 = acc.rearrange("p (a b) -> p a b", a=nseg)
                wr = V_CONV + (l * 3) * 4 + c
                P.op('dve', lambda e, u=u, a3=a3, wr=wr: e.tensor_scalar(
                    out=a3, in0=u[:, :, 1:L + 1], scalar1=vcol(wr + 4), scalar2=None, op0=ALU.mult),
                    reads=[u_r, vecs_r], writes=[acc_r])
                P.op('dve', lambda e, u=u, a3=a3, wr=wr: e.scalar_tensor_tensor(
                    out=a3, in0=u[:, :, 0:L], scalar=vcol(wr), in1=a3, op0=ALU.mult, op1=ALU.add),
                    reads=[u_r, vecs_r, acc_r], writes=[acc_r])
                P.op('dve', lambda e, u=u, a3=a3, wr=wr: e.scalar_tensor_tensor(
                    out=a3, in0=u[:, :, 2:L + 2], scalar=vcol(wr + 8), in1=a3, op0=ALU.mult, op1=ALU.add),
                    reads=[u_r, vecs_r, acc_r], writes=[acc_r])
                P.op('dve', lambda e, acc=acc, cbs=cbs, c=c: e.tensor_tensor(out=cvo[:, c, :], in0=acc, in1=cbs, op=ALU.mult),
                     reads=[acc_r, cbs_r], writes=[cvo_r], waw=False)
                if edges is not None:
                    eg, eg_r = edges
                    for k_, (src_, o0, o1) in enumerate(((u[:, 0, :], 1, L), (acc, 0, L - 1), (cbs, 0, L - 1))):
                        for e_, off in enumerate((o0, o1)):
                            P.op('act', lambda e, src_=src_, off=off, k_=k_, e_=e_, c=c: e.copy(
                                out=eg[:, k_, c, e_:e_ + 1], in_=src_[:, off:off + 1]),
                                reads=[u_r, acc_r, cbs_r], writes=[eg_r], waw=False)
                A.release(m)

        def merge_stage(l, g, xn, xn_r, srcs, oo_t):
            m = A.mark()
            mg32, mg32_r = A.f32([128, 8, T], "mg32")
            mbf, mbf_r = oo_t[0], Res("mbf")
            alias_r = [srcs[0][1], srcs[1][1]]
            sgs = [A.f32([128, SUB], "msg%d" % k_) for k_ in range(2)]
            tms = [A.f32([128, SUB], "mtm%d" % k_) for k_ in range(2)]
            wos = [w_ret_o[l], w_att_o[l], w_conv_o[l]]
            wobuf = [A.bf([128, 4, 1024], "wobuf%d" % k_) for k_ in range(2)]
            k_ = 0
            for bi in range(3):
                src, src_r = srcs[bi]
                wov, slo_r = wobuf[bi % 2]
                P.dma('pool', wov, wsrc(wos[bi], 4, 0, 1024), writes=[slo_r])
                for half in range(2):
                    slg, slg_r = load_panel([(0, 8, 512, wsrc(w_inx[l], 8, X_GATE + bi * 1024 + half * 512, 512))])
                    wgv = pview(slg, 0, 8, 512)
                    for j in range(4):
                        c = half * 4 + j
                        for s in range(NS):
                            ts = slice(s * SUB, (s + 1) * SUB)
                            by, by_r = proj_fm(slo_r, wov, c * 128, src, src_r, s, kcn=4)
                            bg, bg_r = proj_fm(slg_r, wgv, j * 128, xn, xn_r, s)
                            sa, sa_r = sgs[k_ % 2]
                            ta, ta_r = tms[k_ % 2]
                            k_ += 1
                            P.op('act', lambda e, bg=bg, sa=sa: e.activation(out=sa, in_=bg, func=AF.Sigmoid), reads=[bg_r], writes=[sa_r])
                            if bi == 0:
                                P.op('dve', lambda e, by=by, sa=sa, c=c, ts=ts: e.tensor_tensor(out=mg32[:, c, ts], in0=by, in1=sa, op=ALU.mult),
                                     reads=[by_r, sa_r], writes=[mg32_r], waw=False)
                            else:
                                P.op('dve', lambda e, by=by, sa=sa, ta=ta: e.tensor_tensor(out=ta, in0=by, in1=sa, op=ALU.mult),
                                     reads=[by_r, sa_r], writes=[ta_r])
                                if bi == 1:
                                    P.op('dve', lambda e, ta=ta, c=c, ts=ts: e.tensor_tensor(out=mg32[:, c, ts], in0=mg32[:, c, ts], in1=ta, op=ALU.add),
                                         reads=[ta_r, mg32_r], writes=[mg32_r], waw=False)
                                else:
                                    P.op('dve', lambda e, ta=ta, c=c, ts=ts: e.tensor_tensor(out=mbf[:, c, ts], in0=mg32[:, c, ts], in1=ta, op=ALU.add),
                                         reads=[ta_r, mg32_r], writes=[mbf_r] + alias_r, waw=False)
            for half in range(2):
                sl, sl_r = load_panel([(0, 8, 512, wsrc(w_o[l], 8, half * 512, 512))])
                wv = pview(sl, 0, 8, 512)
                for j in range(4):
                    c = half * 4 + j
                    for s in range(NS):
                        ts = slice(s * SUB, (s + 1) * SUB)
                        bk, bk_r = proj_fm(sl_r, wv, j * 128, mbf, mbf_r, s)
                        h_update(bk, bk_r, 1, c, ts, s)
            flush_ss()
            for r_ in (mbf_r, srcs[0][1], srcs[1][1]):
                for dd in (r_.w, r_.r, r_.pw, r_.pr):
                    _merge(oo_t[1].pr, dd)
            A.release(m)

        AG_GROUPS = [list(range(NCORES))]

        def all_gather(snd_t, rcv_t, snd_r, rcv_r):
            P.collective(lambda e: e.collective_compute(
                "AllGather", ALU.bypass, replica_groups=AG_GROUPS, ins=[snd_t.ap().opt()], outs=[rcv_t.ap().opt()]),
                reads=[snd_r], writes=[rcv_r])

        sample_c = {}

        def setup_sample():
            ct, ct_r = ct0, ct0_r
            rope, rope_r = A.f32([128, 2, T], "rope")
            P.dma('sp', rope, rope_d.rearrange("k p t -> p k t"), writes=[rope_r])
            coef, coef_r = A.f32([128, DEPTH * 4, 2, 9], "coef")
            for l in range(DEPTH):
                for hh in range(4):
                    for d_ in range(2):
                        col = l * 8 + d_ * 4 + hh
                        P.op('act', lambda e, l=l, hh=hh, d_=d_, col=col: e.activation(
                            out=coef[:, l * 4 + hh, d_, :], in_=ct[:, 2 + d_ * 9:2 + d_ * 9 + 9], func=AF.Exp, scale=lg[:, col:col + 1]),
                            reads=[ct_r, lg_r], writes=[coef_r], waw=False)
            P.op('dve', lambda e: e.tensor_tensor(
                out=coef, in0=coef, in1=bcast_mid(ct[:, 20:38], DEPTH * 4).rearrange("p a (b c) -> p a b c", b=2), op=ALU.mult),
                reads=[coef_r, ct_r], writes=[coef_r])
            sample_c.update(ct=ct, ct_r=ct_r, rope=(rope, rope_r), coef=coef, coef_r=coef_r)

        def kv_early(l, xn, xn_r, edges, sndk_r):
            m = A.mark()
            wi = w_inx[l]
            rope_tabs = sample_c['rope']
            ktmp, ktmp_r = A.bf([128, T], "ktmp")
            vtmp, vtmp_r = A.bf([128, 8, 2, 64], "vtmp")
            tmps = {'sq': A.bf([128, SUB], "ksq"), 'rs': A.f32([128, SUB], "krs"), 't1': A.f32([128, SUB], "kt1"),
                    't2': A.f32([128, SUB], "kt2")}
            slC, slC_r = load_panel([(0, 8, 384, wsrc(wi, 8, X_AK, 384))])
            wC = pview(slC, 0, 8, 384)
            for s in range(NS):
                ts = slice(s * SUB, (s + 1) * SUB)
                bq, bq_r = proj_fm(slC_r, wC, 0, xn, xn_r, s)
                bqs, bqs_r = proj_fm(slC_r, wC, 128, xn, xn_r, s)
                norm_rope(bq, bq_r, bqs, bqs_r, V_KG + l, V_KGS + l, ts, [(ktmp[:, ts], ktmp_r)], tmps, rope_tabs)
            for tp in range(2):
                bk, bk_r = bank()
                for q4 in range(4):
                    tt = tp * 4 + q4
                    for kc in range(8):
                        P.op('pe', lambda e: e.matmul(
                            bk[:, q4 * 128:(q4 + 1) * 128], xn[:, kc, tt * 128:(tt + 1) * 128], wC[:, kc, 256:384],
                            start=(kc == 0), stop=(kc == 7)),
                            reads=[slC_r, xn_r], writes=[bk_r], inc=(kc == 7))
                P.op('act', lambda e: e.copy(out=vtmp[:, tp * 4:tp * 4 + 4, :, :], in_=bk.rearrange("p (a b c) -> p a b c", a=4, b=2)),
                     reads=[bk_r], writes=[vtmp_r], waw=False)
            sv = sndK[l].ap()
            P.dma('sp', sv[0:1024, :].rearrange("(p x) c -> p (x c)", p=128), ktmp, reads=[ktmp_r], writes=[sndk_r])
            vv = sv[1024:2048, :].rearrange("(t p) (g d) -> p t g d", p=128, g=2)
            for gg in range(2):
                P.dma('sp', vv[:, :, gg, :], vtmp[:, :, gg, :], reads=[vtmp_r], writes=[sndk_r])
            eg, eg_r = edges
            ebf, ebf_r = A.bf([128, 8], "ebf")
            P.op('act', lambda e: e.copy(out=ebf, in_=eg[:, 0, :, :].rearrange("p a b -> p (a b)")), reads=[eg_r], writes=[ebf_r])
            P.dma('sp', sv[2048:2056, :].rearrange("a (b x) -> (a b) x", x=8), ebf, reads=[ebf_r], writes=[sndk_r])
            A.release(m)

        def mixer_sample(l, xn, xn_r):
            ct, ct_r = sample_c['ct'], sample_c['ct_r']
            coef, coef_r = sample_c['coef'], sample_c['coef_r']
            snd_r, rcv_r, sndk_r, rcvk_r = Res("sndS"), Res("rcvS"), Res("sndK"), Res("rcvK")
            m = A.mark()
            oo, oo_r = A.bf([128, 8, T], "oo")
            orn = (oo[:, 0:4, :], Res("orn"))
            oatt = (oo[:, 4:8, :], Res("oatt"))
            for r_ in (orn[1], oatt[1]):
                _merge(r_.pr, oo_r.pr)
            cvo = A.bf([128, 4, T], "cvo")
            edges = A.f32([128, 3, 4, 2], "edges")
            ms_ = A.mark()
            seedf = A.f32([128, 4, 128], "seedf")
            seedb = A.f32([128, 4, 128], "seedb")
            m1 = A.mark()
            sums, sums_r = A.f32([128, 4, 2, 128], "sums")
            retention(l, 1, xn, xn_r, None, None, seeds=None, phase1=(sums, sums_r))
            sbf, sbf_r = A.bf([128, 8, 128], "sumsbf")
            P.op('act', lambda e: e.copy(out=sbf, in_=sums.rearrange("p a b c -> p (a b) c")), reads=[sums_r], writes=[sbf_r])
            P.dma('sp', sndS[l].ap().rearrange("(p x) e -> p x e", p=128), sbf, reads=[sbf_r], writes=[snd_r])
            for c_ in range(3):
                prefetch_panel(("conv", l, c_), [(0, 8, 384, wsrc(w_inx[l], 8, X_CONV + c_ * 384, 384))])
            all_gather(sndS[l], rcvS[l], snd_r, rcv_r)
            conv_stage(l, 1, xn, xn_r, cvo[0], cvo[1], edges=edges)
            kv_early(l, xn, xn_r, edges, sndk_r)
            for hh_ in range(2):
                prefetch_panel(("ret", l, hh_, True), [(0, 8, 512, wsrc(w_inx[l], 8, X_RET + hh_ * 512, 512))])
            all_gather(sndK[l], rcvK[l], sndk_r, rcvk_r)
            rS, rS_r = A.bf([128, NCORES, 8, 128], "rS")
            rv = rcvS[l].ap().rearrange("(q p x) e -> q p x e", q=NCORES, p=128)
            for q in range(NCORES):
                P.dma('sp', rS[:, q, :, :], rv[q], reads=[rcv_r], writes=[rS_r])
            s0, s0_r = A.f32([128, 2, 4, 128], "s0")
            for d_ in range(2):
                P.dma('sp', s0[:, d_, :, :], sret_d[l, d_].rearrange("h d e -> d h e"), writes=[s0_r])
            for d_, (sd, sd_r) in enumerate((seedf, seedb)):
                for hh in range(4):
                    lh = l * 4 + hh
                    P.op('dve', lambda e, d_=d_, hh=hh, lh=lh, sd=sd: e.tensor_scalar(
                        out=sd[:, hh, :], in0=s0[:, d_, hh, :], scalar1=coef[:, lh, d_, 0:1], scalar2=None, op0=ALU.mult),
                        reads=[s0_r, coef_r], writes=[sd_r], waw=False)
                    for q in range(NCORES):
                        P.op('dve', lambda e, d_=d_, hh=hh, lh=lh, sd=sd, q=q: e.scalar_tensor_tensor(
                            out=sd[:, hh, :], in0=rS[:, q, hh * 2 + d_, :], scalar=coef[:, lh, d_, 1 + q:2 + q], in1=sd[:, hh, :],
                            op0=ALU.mult, op1=ALU.add), reads=[rS_r, coef_r, sd_r], writes=[sd_r], waw=False)
            A.release(m1)
            retention(l, 1, xn, xn_r, orn[0], orn[1], seeds=[seedf, seedb])
            A.release(ms_)


            def exch(KT, KT_r, Vx, Vx_r):
                eg, eg_r = edges
                m2 = A.mark()
                rk = rcvK[l].ap().rearrange("(q r) c -> q r c", q=NCORES)
                pk, pk_r = A.f32([128, 4, 128], "pk")
                P.dma('sp', pk, ck_d[l].rearrange("(t p) f -> p t f", p=128), writes=[pk_r])
                bk, bk_r = bank()
                for tt in range(4):
                    P.op('pe', lambda e, bk=bk, tt=tt: e.transpose(bk[:, tt * 128:(tt + 1) * 128], pk[:, tt, :], ident),
                         reads=[pk_r, ident_r], writes=[bk_r])
                P.op('act', lambda e, bk=bk: e.copy(out=KT[:, 0:PAST], in_=bk), reads=[bk_r], writes=[KT_r], waw=False)
                pv, pv_r = A.f32([128, 4, 128], "pv")
                P.dma('sp', pv, cv_d[l].rearrange("(t p) f -> p t f", p=128), writes=[pv_r])
                P.op('dve', lambda e: e.tensor_copy(out=Vx[:, 0:4, :, 0:64], in_=pv.rearrange("p t (g d) -> p t g d", g=2)),
                     reads=[pv_r], writes=[Vx_r], waw=False)
                kst, kst_r = A.bf([128, 4 * T], "kst")
                vst, vst_r = A.bf([128, 32, 2, 64], "vst")
                for q in range(4):
                    P.dma('sp', KT[:, PAST + q * T:PAST + (q + 1) * T], rk[q, 0:1024, :].rearrange("(p x) c -> p (x c)", p=128),
                          reads=[rcvk_r], writes=[KT_r])
                    P.dma('sp', kst[:, q * T:(q + 1) * T], rk[4 + q, 0:1024, :].rearrange("(p x) c -> p (x c)", p=128),
                          reads=[rcvk_r], writes=[kst_r])
                    for gg in range(2):
                        P.dma('sp', Vx[:, 4 + q * 8:12 + q * 8, gg, 0:64],
                              rk[q, 1024:2048, :].rearrange("(t p) (g d) -> p t g d", p=128, g=2)[:, :, gg, :],
                              reads=[rcvk_r], writes=[Vx_r])
                        P.dma('sp', vst[:, q * 8:(q + 1) * 8, gg, :],
                              rk[4 + q, 1024:2048, :].rearrange("(t p) (g d) -> p t g d", p=128, g=2)[:, :, gg, :],
                              reads=[rcvk_r], writes=[vst_r])
                kd = KT[:, PAST:PAST + 4 * T]
                P.op('dve', lambda e: e.tensor_scalar(out=kd, in0=kd, scalar1=ct[:, 0:1], scalar2=None, op0=ALU.mult),
                     reads=[KT_r, ct_r], writes=[KT_r])
                P.op('dve', lambda e: e.scalar_tensor_tensor(out=kd, in0=kst, scalar=ct[:, 1:2], in1=kd, op0=ALU.mult, op1=ALU.add),
                     reads=[kst_r, KT_r, ct_r], writes=[KT_r])
                vd = Vx[:, 4:36, :, 0:64]
                P.op('dve', lambda e: e.tensor_scalar(out=vd, in0=vd, scalar1=ct[:, 0:1], scalar2=None, op0=ALU.mult),
                     reads=[Vx_r, ct_r], writes=[Vx_r])
                P.op('dve', lambda e: e.scalar_tensor_tensor(out=vd, in0=vst, scalar=ct[:, 1:2], in1=vd, op0=ALU.mult, op1=ALU.add),
                     reads=[vst_r, Vx_r, ct_r], writes=[Vx_r])
                re_, re_r = A.bf([128, NCORES, 8], "redge")
                for q in range(NCORES):
                    P.dma('sp', re_[:, q, :], rk[q, 2048:2056, :].rearrange("a (b x) -> (a b) x", x=8), reads=[rcvk_r], writes=[re_r])
                hl, hl_r = A.f32([128, 2, 4], "hl")
                P.op('dve', lambda e: e.memset(hl, 0.0), writes=[hl_r])
                rev = re_.rearrange("p q (c e) -> p q c e", e=2)
                for q in range(NCORES):
                    P.op('dve', lambda e, q=q: e.scalar_tensor_tensor(
                        out=hl[:, 0, :], in0=rev[:, q, :, 1], scalar=ct[:, 38 + q:39 + q], in1=hl[:, 0, :], op0=ALU.mult, op1=ALU.add),
                        reads=[re_r, ct_r, hl_r], writes=[hl_r])
                    P.op('dve', lambda e, q=q: e.scalar_tensor_tensor(
                        out=hl[:, 1, :], in0=rev[:, q, :, 0], scalar=ct[:, 46 + q:47 + q], in1=hl[:, 1, :], op0=ALU.mult, op1=ALU.add),
                        reads=[re_r, ct_r, hl_r], writes=[hl_r])
                wr = V_CONV + l * 12
                for e_, wk, tpos in ((0, 0, 0), (1, 8, T - 1)):
                    P.op('dve', lambda e, e_=e_, wk=wk: e.tensor_tensor(out=hl[:, e_, :], in0=hl[:, e_, :], in1=vecs[:, wr + wk:wr + wk + 4], op=ALU.mult),
                         reads=[hl_r, vecs_r], writes=[hl_r])
                    P.op('dve', lambda e, e_=e_: e.tensor_tensor(out=hl[:, e_, :], in0=hl[:, e_, :], in1=eg[:, 1, :, e_], op=ALU.add),
                         reads=[hl_r, eg_r], writes=[hl_r])
                    P.op('dve', lambda e, e_=e_, tpos=tpos: e.tensor_tensor(out=cvo[0][:, :, tpos], in0=hl[:, e_, :], in1=eg[:, 2, :, e_], op=ALU.mult),
                         reads=[hl_r, eg_r], writes=[cvo[1]])
                A.release(m2)

            attention(l, 1, xn, xn_r, oatt[0], oatt[1], rope_tabs=sample_c['rope'], exch=exch, skip_kv=True)
            merge_stage(l, 1, xn, xn_r, [orn, oatt, cvo], (oo, oo_r))
            A.release(m)

        def mixer_stage(l, g, xn, xn_r):
            m = A.mark()
            oo, oo_r = A.bf([128, 8, T], "oo")
            orn = (oo[:, 0:4, :], Res("orn"))
            oatt = (oo[:, 4:8, :], Res("oatt"))
            for r_ in (orn[1], oatt[1]):
                _merge(r_.pr, oo_r.pr)
            cvo = A.bf([128, 4, T], "cvo")
            fl = debug or ('ret', 'att', 'conv', 'merge')
            for t_ in (orn, oatt, cvo):
                if debug:
                    P.op('dve', lambda e, t_=t_: e.memset(t_[0], 0.0), writes=[t_[1]])
            if 'ret' in fl:
                retention(l, g, xn, xn_r, orn[0], orn[1])
            if 'att' in fl:
                attention(l, g, xn, xn_r, oatt[0], oatt[1])
            if 'conv' in fl:
                conv_stage(l, g, xn, xn_r, cvo[0], cvo[1])
            if 'merge' in fl:
                merge_stage(l, g, xn, xn_r, [orn, oatt, cvo], (oo, oo_r))
            A.release(m)

        setup_eps()
        sqp = [A.bf([128, SUB], "sqp%d" % k_) for k_ in range(3)]
        base_mark = A.mark()

        groups = [0] + ([1] if enable_sample else [])
        for g in groups:
            ci = g
            if g == 1:
                setup_sample()
            load_x(g)
            if g == 0:
                compute_mods_all()
            for l in range(DEPTH):
                use_mods(l, ci)
                m = A.mark()
                xn, xn_r = A.bf([128, 8, T], "xn")
                norm_stage(0, ci, xn, xn_r, have_ss=(l > 0))
                ffn_stage(l, 0, 0, xn, xn_r)
                norm_stage(1, ci, xn, xn_r)
                if g == 0:
                    mixer_stage(l, g, xn, xn_r)
                else:
                    mixer_sample(l, xn, xn_r)
                norm_stage(2, ci, xn, xn_r)
                ffn_stage(l, 1, 2, xn, xn_r)
                A.release(m)
            store_y(g)
        print("arena peak words", A.peak, "of", ARENA_WORDS, "waits", P.nwait)

        P.finish()
        P.emit()
    return nc, dbg_out


def _rope_tables(tok0, n):
    t = np.arange(tok0, tok0 + n)
    t_row = (t // 64).astype(np.float32)
    t_col = (t % 64).astype(np.float32)
    n_freq = 16
    inv = (np.float32(10000.0) ** (-np.arange(n_freq, dtype=np.float32) / np.float32(n_freq))).astype(np.float32)
    ang = np.concatenate([t_row[:, None] * inv, t_col[:, None] * inv], axis=-1).astype(np.float32)
    cos = np.cos(ang).astype(np.float32).T
    sin = np.sin(ang).astype(np.float32).T
    C = np.concatenate([cos, cos, cos, cos], axis=0)
    S = np.concatenate([-sin, sin, -sin, sin], axis=0)
    return np.stack([C, S], axis=0).astype(np.float32)


def _ctab(core):
    b, r = core // 4, core % 4
    t = np.zeros(64, np.float32)
    t[b] = 1.0
    for d_ in range(2):
        ex = np.zeros(9, np.float32)
        mk = np.zeros(9, np.float32)
        nprev = r if d_ == 0 else 3 - r
        ex[0] = 1024.0 * nprev
        mk[0] = 1.0
        for q in range(NCORES):
            if q // 4 != b:
                continue
            qr = q % 4
            dist = (r - 1 - qr) if d_ == 0 else (qr - r - 1)
            if dist >= 0:
                ex[1 + q] = 1024.0 * dist
                mk[1 + q] = 1.0
        t[2 + d_ * 9:2 + d_ * 9 + 9] = ex
        t[20 + d_ * 9:20 + d_ * 9 + 9] = mk
    if r > 0:
        t[38 + core - 1] = 1.0
    if r < 3:
        t[46 + core + 1] = 1.0
    return t


_CACHE = {}


def kernel(x_prompt, x_sample, c, state_ret, cache_k, cache_v, c_ctx, w_ada, b_ada, norm_w, w_ffn_in,
           w_ffn_out, w_in, ret_decay_logit, ret_gn, q_gain, k_gain, conv_w, w_ret_o, w_att_o, w_conv_o, w_o,
           _enable_sample=True, _debug=None, _ncores=NCORES):
    f32 = np.float32
    x_prompt = np.asarray(x_prompt, f32)
    x_sample = np.asarray(x_sample, f32)
    key = (_enable_sample, _debug)
    if key not in _CACHE:
        _CACHE[key] = build_program(enable_sample=_enable_sample, debug=_debug)
    nc, dbg = _CACHE[key]

    cols = build_win_cols()
    w_inx = np.ascontiguousarray(np.asarray(w_in, f32)[:, :, cols])
    rows = []
    for j in range(4):
        rows += list(range(j * 64, (j + 1) * 64)) + list(range((4 + j) * 64, (5 + j) * 64))
    w_att_o_p = np.ascontiguousarray(np.asarray(w_att_o, f32)[:, rows, :])

    qg = np.asarray(q_gain, f32)
    kg = np.asarray(k_gain, f32)
    sw = np.array(_sw(0))
    ident = np.eye(128, dtype=f32)
    ii = np.arange(128)
    mm_, cc_ = np.meshgrid(ii, ii, indexing='ij')
    posd = np.stack([np.maximum(cc_ - mm_, 0), np.maximum(mm_ - cc_, 0), (cc_ >= mm_), (cc_ < mm_)]).astype(f32)
    posv = np.stack([ii, 127 - ii, ii + 1, 128 - ii], axis=1).astype(f32)
    posr = np.stack([ii, 127 - ii, ii + 1, 128 - ii], axis=0).astype(f32)

    in_maps = []
    for core in range(_ncores):
        b, r = core // 4, core % 4
        vecs = np.zeros((NV, 128), f32)
        vecs[V_NORM:V_NORM + 48] = np.asarray(norm_w, f32).reshape(48, 128)
        vecs[V_GN:V_GN + 8] = np.asarray(ret_gn, f32).reshape(8, 128)
        vecs[V_CONV:V_CONV + 24] = np.asarray(conv_w, f32).reshape(24, 128)
        for l in range(DEPTH):
            vecs[V_QG + l] = np.tile(qg[l], 2)
            vecs[V_QGS + l] = np.tile(qg[l][sw], 2)
            vecs[V_KG + l] = np.tile(kg[l], 2)
            vecs[V_KGS + l] = np.tile(kg[l][sw], 2)
        vecs[V_COND:V_COND + 8] = np.asarray(c_ctx, f32).reshape(8, 128)
        vecs[V_COND + 8:V_COND + 16] = np.asarray(c, f32)[0].reshape(8, 128)
        vecs[V_COND + 16:V_COND + 24] = np.asarray(c, f32)[1].reshape(8, 128)
        m = {
            "xp": x_prompt[core * 4:(core + 1) * 4].reshape(T, D),
            "xs": x_sample[b, r * T:(r + 1) * T],
            "vecs": vecs,
            "bada": np.ascontiguousarray(np.asarray(b_ada, f32).reshape(DEPTH, 72, 128)[:, core * 9:(core + 1) * 9]).reshape(DEPTH * 9, 128),
            "decay": np.asarray(ret_decay_logit, f32).reshape(-1),
            "ident": ident, "posd": posd, "posv": posv, "posr": posr,
            "rope": _rope_tables(r * T, T),
            "ctab": _ctab(core),
            "sret": np.asarray(state_ret, f32)[b],
            "ck": np.asarray(cache_k, f32)[b].reshape(DEPTH, PAST, 128),
            "cv": np.asarray(cache_v, f32)[b].reshape(DEPTH, PAST, 128),
            "w_ada": np.ascontiguousarray(np.asarray(w_ada, f32)[:, :, core * 1152:(core + 1) * 1152]), "w_ffn_in": np.asarray(w_ffn_in, f32),
            "w_ffn_out": np.asarray(w_ffn_out, f32), "w_inx": w_inx,
            "w_ret_o": np.asarray(w_ret_o, f32), "w_att_o": w_att_o_p,
            "w_conv_o": np.asarray(w_conv_o, f32), "w_o": np.asarray(w_o, f32),
        }
        in_maps.append(m)

    res = run_bass_kernel_spmd(nc, in_maps, core_ids=list(range(_ncores)))
    R = res.results
    if _ncores < NCORES:
        return R
    y_prompt = np.stack([R[cix]["yp"].reshape(4, 256, D) for cix in range(NCORES)]).reshape(32, 256, D)
    y_sample = np.stack([R[cix]["ys"] for cix in range(NCORES)]).reshape(2, 4096, D)
    nstate = np.concatenate([R[cix]["nstate"] for cix in range(NCORES)], axis=0)
    nk = np.concatenate([R[cix]["nck"] for cix in range(NCORES)], axis=0).reshape(32, DEPTH, 256, 2, 64)
    nv = np.concatenate([R[cix]["ncv"] for cix in range(NCORES)], axis=0).reshape(32, DEPTH, 256, 2, 64)
    if _debug:
        kernel.dbg = [{k: R[cix]["dbg_" + k] for k in dbg} for cix in range(NCORES)]
    return (y_prompt.astype(f32), y_sample.astype(f32), nstate.astype(f32), nk.astype(f32), nv.astype(f32))
```

```python
import types
import numpy as np
from contextlib import ExitStack
import concourse.bass as bass
import concourse.mybir as mybir
from concourse.bass_utils import run_bass_kernel_spmd

F32, BF16 = mybir.dt.float32, mybir.dt.bfloat16
AF = mybir.ActivationFunctionType
ALU = mybir.AluOpType

D = 1024
T = 1024
SUB = 512
NS = T // SUB
FFN = 2816
NFC = FFN // 128
DEPTH = 2
EPS = 1e-6
NCORES = 8
PAST = 512
NKS = PAST + 4096
SAME_SYNC = True

O_RQ, O_RK, O_RV, O_RG, O_AQ, O_AK, O_AV, O_CB, O_CC, O_CX, O_GR, O_GA, O_GC = (
    0, 512, 1024, 1536, 2048, 2560, 2688, 2816, 3328, 3840, 4352, 5376, 6400)


def _sw(base):
    return list(range(base + 32, base + 64)) + list(range(base, base + 32))


def build_win_cols():
    cols = []
    for h in range(4):
        for o in (O_RQ, O_RK, O_RV, O_RG):
            cols += list(range(o + h * 128, o + (h + 1) * 128))
    for j in range(4):
        cols += list(range(O_AQ + j * 64, O_AQ + (j + 1) * 64))
        cols += list(range(O_AQ + (4 + j) * 64, O_AQ + (5 + j) * 64))
    for j in range(4):
        cols += _sw(O_AQ + j * 64) + _sw(O_AQ + (4 + j) * 64)
    cols += list(range(O_AK, O_AK + 128))
    cols += _sw(O_AK) + _sw(O_AK + 64)
    cols += list(range(O_AV, O_AV + 128))
    for c in range(4):
        for o in (O_CB, O_CC, O_CX):
            cols += list(range(o + c * 128, o + (c + 1) * 128))
    cols += list(range(O_GR, O_GR + 3072))
    return np.array(cols, dtype=np.int64)


X_RET = 0
X_AQ = 2048
X_AQS = 2560
X_AK = 3072
X_AKS = 3200
X_AV = 3328
X_CONV = 3456
X_GATE = 3456 + 1536
NWX = X_GATE + 3072

V_NORM = 0
V_GN = 48
V_CONV = 56
V_QG = 80
V_QGS = 82
V_KG = 84
V_KGS = 86
V_COND = 88
NV = 112


ENGS = ['pe', 'act', 'dve', 'pool', 'sp']


class Res:
    __slots__ = ('name', 'w', 'r', 'pw', 'pr', 'excl')

    def __init__(self, name=''):
        self.name = name
        self.w = {}
        self.r = {}
        self.pw = {}
        self.pr = {}
        self.excl = False


def _freeze(fn):
    if fn.__closure__ is None:
        return fn
    cells = []
    for c in fn.__closure__:
        try:
            cells.append(types.CellType(c.cell_contents))
        except ValueError:
            cells.append(c)
    return types.FunctionType(fn.__code__, fn.__globals__, fn.__name__, fn.__defaults__, tuple(cells))


def _merge(dst, src):
    for k, v in src.items():
        if k not in dst or dst[k][2] < v[2]:
            dst[k] = v


class Prog:
    def __init__(self, nc, stack):
        self.nc = nc
        self.q = {e: [] for e in ENGS}
        self.cnt = {e: 0 for e in ENGS}
        self.seen = {e: {} for e in ENGS}
        self.esem = {e: stack.enter_context(nc.semaphore("es_" + e)) for e in ENGS if e != 'sp'}
        self.dsem = []
        self.dcnt = []
        self.dpool = {}
        self.dnext = {}
        for qn, n in (('sp', 20), ('pool', 10)):
            ids = []
            for i in range(n):
                self.dsem.append(stack.enter_context(nc.semaphore("ds_%s%d" % (qn, i))))
                self.dcnt.append(0)
                ids.append(len(self.dsem) - 1)
            self.dpool[qn] = ids
            self.dnext[qn] = 0
        self.ccsem = stack.enter_context(nc.semaphore("cc"))
        self.cccnt = 0
        self.nwait = 0

    def _need(self, eng, dep):
        kind, ident, n = dep
        if kind == 'e':
            if ident == eng and (eng == 'pe' or not SAME_SYNC):
                return
            sem = self.esem[ident]
        elif kind == 'd':
            sem = self.dsem[ident]
        else:
            sem = self.ccsem
        key = (kind, ident)
        if self.seen[eng].get(key, 0) >= n:
            return
        self.seen[eng][key] = n
        self.nwait += 1
        self.q[eng].append(lambda e, s=sem, v=n: e.wait_ge(s, v))

    def _sync(self, eng, reads, writes, selfdep, waw):
        for r in reads:
            for d in r.w.values():
                if d != selfdep:
                    self._need(eng, d)
            if r.excl:
                for d in list(r.r.values()):
                    if d != selfdep and d[1] != eng:
                        self._need(eng, d)
        for w in writes:
            if w.r:
                w.pw, w.pr = w.w, w.r
                w.w, w.r = {}, {}
            elif waw:
                for d in w.w.values():
                    if d != selfdep:
                        self._need(eng, d)
            for d in list(w.pw.values()) + list(w.pr.values()):
                if d != selfdep:
                    self._need(eng, d)

    def op(self, eng, fn, reads=(), writes=(), inc=True, waw=True):
        fn = _freeze(fn)
        n = self.cnt[eng] + 1
        dep = ('e', eng, n)
        self._sync(eng, reads, writes, dep, waw)
        if inc:
            self.cnt[eng] = n
            sem = self.esem[eng]
            self.q[eng].append(lambda e, f=fn, s=sem: f(e).then_inc(s, 1))
        else:
            self.q[eng].append(lambda e, f=fn: f(e))
        for r in reads:
            r.r[('e', eng)] = dep
        for w in writes:
            w.w[('e', eng)] = dep

    def dma(self, queue, out, in_, reads=(), writes=(), waw=False, **kw):
        pool = self.dpool[queue]
        si = pool[self.dnext[queue] % len(pool)]
        self.dnext[queue] += 1
        if self.dcnt[si] > 0:
            self._need(queue, ('d', si, self.dcnt[si]))
        self._sync(queue, reads, writes, None, waw)
        self.dcnt[si] += 16
        n = self.dcnt[si]
        sem = self.dsem[si]
        self.q[queue].append(lambda e, o=out, i=in_, s=sem, k=kw: e.dma_start(out=o, in_=i, **k).then_inc(s, 16))
        dep = ('d', si, n)
        for r in reads:
            r.r[('d', si)] = dep
        for w in writes:
            w.w[('d', si)] = dep

    def collective(self, fn, reads, writes):
        fn = _freeze(fn)
        self._sync('pool', reads, writes, None, True)
        self.cccnt += 1
        n = self.cccnt
        sem = self.ccsem
        self.q['pool'].append(lambda e, f=fn, s=sem: f(e).then_inc(s, 1))
        dep = ('c', 0, n)
        for r in reads:
            r.r[('c', 0)] = dep
        for w in writes:
            w.w[('c', 0)] = dep

    def finish(self):
        for si, c in enumerate(self.dcnt):
            if c > 0:
                self._need('sp', ('d', si, c))
        for e in ('pe', 'act', 'dve', 'pool'):
            if self.cnt[e] > 0:
                self._need('sp', ('e', e, self.cnt[e]))
        if self.cccnt:
            self._need('sp', ('c', 0, self.cccnt))

    def emit(self):
        nc = self.nc
        q = self.q
        with nc.Block() as block:
            @block.tensor
            def _(e):
                for f in q['pe']:
                    f(e)

            @block.scalar
            def _(e):
                for f in q['act']:
                    f(e)

            @block.vector
            def _(e):
                for f in q['dve']:
                    f(e)

            @block.gpsimd
            def _(e):
                for f in q['pool']:
                    f(e)

            @block.sync
            def _(e):
                for f in q['sp']:
                    f(e)


class Arena:
    def __init__(self, tensor, size):
        self.t = tensor
        self.size = size
        self.top = 0
        self.live = []
        self.dead = []
        self.peak = 0

    def alloc(self, nwords, name=''):
        s = self.top
        e = s + nwords
        assert e <= self.size, "arena overflow %s %d > %d" % (name, e, self.size)
        self.top = e
        self.peak = max(self.peak, e)
        res = Res(name)
        keep = []
        for (a, b, r) in self.dead:
            if a < e and b > s:
                for dd in (r.w, r.r, r.pw, r.pr):
                    _merge(res.pr, dd)
                if a < s or b > e:
                    keep.append((a, b, r))
            else:
                keep.append((a, b, r))
        self.dead = keep
        self.live.append((s, e, res))
        return self.t[:, s:e], res

    def f32(self, shape, name=''):
        n = int(np.prod(shape[1:]))
        ap, res = self.alloc(n, name)
        ap = _shape(ap, shape)
        if shape[0] < 128:
            ap = ap[0:shape[0]]
        return ap, res

    def bf(self, shape, name=''):
        n = int(np.prod(shape[1:]))
        ap, res = self.alloc((n + 1) // 2, name)
        ap = ap.bitcast(BF16)[:, 0:n]
        ap = _shape(ap, shape)
        if shape[0] < 128:
            ap = ap[0:shape[0]]
        return ap, res

    def mark(self):
        return self.top

    def release(self, mark):
        keep = []
        for (s, e, r) in self.live:
            if s >= mark:
                self.dead.append((s, e, r))
            else:
                keep.append((s, e, r))
        self.live = keep
        self.top = mark


def bcast_last(ap, n):
    return bass.AP(ap.tensor, ap.offset, [list(d) for d in ap.ap] + [[0, n]])


def bcast_mid(ap, reps):
    d = [list(x) for x in ap.ap]
    return bass.AP(ap.tensor, ap.offset, [d[0], [0, reps]] + d[1:])


def _shape(ap, shape):
    if len(shape) == 2:
        return ap
    if len(shape) == 3:
        return ap.rearrange("p (a b) -> p a b", a=shape[1])
    if len(shape) == 4:
        return ap.rearrange("p (a b c) -> p a b c", a=shape[1], b=shape[2])
    raise ValueError(shape)


def build_program(enable_sample=True, debug=None):
    nc = bass.Bass("TRN2", target_bir_lowering=False)
    dbg_out = {}

    def din(name, shape, dt=F32):
        return nc.dram_tensor(name, list(shape), dt, kind="ExternalInput").ap()

    def dout(name, shape, dt=F32):
        return nc.dram_tensor(name, list(shape), dt, kind="ExternalOutput").ap()

    xin = [din("xp", [T, D]), din("xs", [T, D])]
    vecs_d = din("vecs", [NV, 128])
    bada_d = din("bada", [DEPTH * 9, 128])
    decay_d = din("decay", [DEPTH * 2 * 4])
    ident_d = din("ident", [128, 128])
    posd_d = din("posd", [4, 128, 128])
    posv_d = din("posv", [128, 4])
    posr_d = din("posr", [4, 128])
    rope_d = din("rope", [2, 128, T])
    sret_d = din("sret", [DEPTH, 2, 4, 128, 128])
    ck_d = din("ck", [DEPTH, PAST, 128])
    cv_d = din("cv", [DEPTH, PAST, 128])
    w_ada = din("w_ada", [DEPTH, D, 9 * D // NCORES])
    w_ffn_in = din("w_ffn_in", [DEPTH, 2, D, 2 * FFN])
    w_ffn_out = din("w_ffn_out", [DEPTH, 2, FFN, D])
    w_inx = din("w_inx", [DEPTH, D, NWX])
    w_ret_o = din("w_ret_o", [DEPTH, 512, D])
    w_att_o = din("w_att_o", [DEPTH, 512, D])
    w_conv_o = din("w_conv_o", [DEPTH, 512, D])
    w_o = din("w_o", [DEPTH, D, D])

    y_out = [dout("yp", [T, D]), dout("ys", [T, D])]
    nstate = dout("nstate", [4, DEPTH, 2, 4, 128, 128])
    nck = dout("nck", [4, DEPTH, 256, 128])
    ncv = dout("ncv", [4, DEPTH, 256, 128])

    ctab_d = din("ctab", [64])
    NXS = 1024
    NXK = 1024 + 1024 + 8
    sndS = [nc.dram_tensor("sndS%d" % l, [NXS, 128], BF16) for l in range(DEPTH)]
    rcvS = [nc.dram_tensor("rcvS%d" % l, [NCORES * NXS, 128], BF16) for l in range(DEPTH)]
    sndK = [nc.dram_tensor("sndK%d" % l, [NXK, 128], BF16) for l in range(DEPTH)]
    rcvK = [nc.dram_tensor("rcvK%d" % l, [NCORES * NXK, 128], BF16) for l in range(DEPTH)]

    stack = ExitStack()
    with stack:
        P = Prog(nc, stack)
        ARENA_WORDS = 53200
        arena_t = stack.enter_context(nc.sbuf_tensor("arena", [128, ARENA_WORDS], F32))
        A = Arena(arena_t, ARENA_WORDS)
        banks = []
        ps_all = stack.enter_context(nc.psum_tensor("ps", [128, 8 * 512], F32))
        for i in range(8):
            banks.append((ps_all[:, i * 512:(i + 1) * 512], Res("bank%d" % i)))
            banks[-1][1].excl = True
        bstate = {'i': 0}

        def bank():
            b = banks[bstate['i'] % 6]
            bstate['i'] += 1
            return b

        ident, ident_r = A.f32([128, 128], "ident")
        P.dma('sp', ident, ident_d[:, :], writes=[ident_r])
        ones_bf, ones_r = A.bf([128, 128], "ones")
        P.op('dve', lambda e: e.memset(ones_bf, 1.0), writes=[ones_r])
        blk_bf, blk_r = A.bf([128, 128], "blk")
        P.op('dve', lambda e: e.memset(blk_bf, 0.0), writes=[blk_r])
        P.op('dve', lambda e: e.memset(blk_bf[0:64, 0:64], 1.0), writes=[blk_r])
        P.op('dve', lambda e: e.memset(blk_bf[64:128, 64:128], 1.0), writes=[blk_r])

        def transpose_rows(src_dram, nrows, name):
            dst, dst_r = A.f32([128, nrows], name)
            m = A.mark()
            st, st_r = A.f32([128, 128], name + "_st")
            P.dma('sp', st[0:nrows, :], src_dram, writes=[st_r])
            bk, bk_r = bank()
            P.op('pe', lambda e: e.transpose(bk[:, 0:nrows], st[0:nrows, :], ident[0:nrows, 0:nrows]),
                 reads=[st_r, ident_r], writes=[bk_r])
            P.op('dve', lambda e: e.tensor_copy(out=dst, in_=bk[:, 0:nrows]), reads=[bk_r], writes=[dst_r])
            A.release(m)
            return dst, dst_r

        vecs, vecs_r = transpose_rows(vecs_d[:, :], NV, "vecs")
        badap, badap_r = transpose_rows(bada_d[:, :], DEPTH * 9, "badap")

        def vcol(r):
            return vecs[:, r:r + 1]

        ct0, ct0_r = A.f32([128, 64], "ctab")
        P.dma('sp', ct0, ctab_d.partition_broadcast(128), writes=[ct0_r])

        scond, scond_r = A.bf([128, 8, 3], "scond")
        for ci in range(3):
            P.op('act', lambda e, ci=ci: e.activation(out=scond[:, :, ci], in_=vecs[:, V_COND + ci * 8:V_COND + ci * 8 + 8],
                                                      func=AF.Silu), reads=[vecs_r], writes=[scond_r], waw=False)

        lgt, lgt_r = A.f32([128, 16], "lgt")
        P.dma('sp', lgt, decay_d.partition_broadcast(128), writes=[lgt_r])
        lg, lg_r = A.f32([128, 16], "lg")
        P.op('act', lambda e: e.activation(out=lg, in_=lgt, func=AF.Exp, scale=-1.0), reads=[lgt_r], writes=[lg_r])
        P.op('act', lambda e: e.activation(out=lg, in_=lg, func=AF.Ln, bias=1.0), reads=[lg_r], writes=[lg_r])
        P.op('dve', lambda e: e.tensor_scalar(out=lg, in0=lg, scalar1=-1.0, scalar2=None, op0=ALU.mult),
             reads=[lg_r], writes=[lg_r])
        posd, posd_r = A.f32([128, 4, 128], "posd")
        P.dma('sp', posd, posd_d.rearrange("k p m -> p k m"), writes=[posd_r])
        posv, posv_r = A.f32([128, 4], "posv")
        P.dma('sp', posv, posv_d[:, :], writes=[posv_r])
        posr, posr_r = A.f32([128, 4, 128], "posr")
        P.dma('sp', posr, posr_d.partition_broadcast(128), writes=[posr_r])

        maskT, maskT_r = A.f32([128, DEPTH * 4, 128], "maskT")
        dkv, dkv_r = A.f32([128, DEPTH * 4, 2], "dkv")
        qrow, qrow_r = A.f32([128, DEPTH * 4, 2, 128], "qrow")
        dcc, dcc_r = A.f32([128, DEPTH * 4, 2], "dcc")
        KS = 128.0 ** -0.5
        m0 = A.mark()
        tmpm, tmpm_r = A.f32([128, 128], "tmpm")
        tmp2, tmp2_r = A.f32([128, 128], "tmp2")
        for l in range(DEPTH):
            for h in range(4):
                lh = l * 4 + h
                cf = lg[:, l * 8 + h:l * 8 + h + 1]
                cb = lg[:, l * 8 + 4 + h:l * 8 + 4 + h + 1]
                P.op('act', lambda e, cf=cf: e.activation(out=tmpm, in_=posd[:, 0, :], func=AF.Exp, scale=cf),
                     reads=[posd_r, lg_r], writes=[tmpm_r])
                P.op('dve', lambda e: e.tensor_tensor(out=tmpm, in0=tmpm, in1=posd[:, 2, :], op=ALU.mult),
                     reads=[tmpm_r, posd_r], writes=[tmpm_r])
                P.op('act', lambda e, cb=cb: e.activation(out=tmp2, in_=posd[:, 1, :], func=AF.Exp, scale=cb),
                     reads=[posd_r, lg_r], writes=[tmp2_r])
                P.op('dve', lambda e: e.tensor_tensor(out=tmp2, in0=tmp2, in1=posd[:, 3, :], op=ALU.mult),
                     reads=[tmp2_r, posd_r], writes=[tmp2_r])
                P.op('dve', lambda e, lh=lh: e.tensor_tensor(out=maskT[:, lh, :], in0=tmpm, in1=tmp2, op=ALU.add),
                     reads=[tmpm_r, tmp2_r], writes=[maskT_r])
                P.op('dve', lambda e, lh=lh: e.tensor_tensor(out=maskT[:, lh, :], in0=maskT[:, lh, :], in1=ident, op=ALU.add),
                     reads=[maskT_r, ident_r], writes=[maskT_r])
                P.op('act', lambda e, cf=cf, lh=lh: e.activation(out=dkv[:, lh, 0:1], in_=posv[:, 1:2], func=AF.Exp, scale=cf),
                     reads=[posv_r, lg_r], writes=[dkv_r])
                P.op('act', lambda e, cb=cb, lh=lh: e.activation(out=dkv[:, lh, 1:2], in_=posv[:, 0:1], func=AF.Exp, scale=cb),
                     reads=[posv_r, lg_r], writes=[dkv_r])
                P.op('act', lambda e, cf=cf, lh=lh: e.activation(out=qrow[:, lh, 0, :], in_=posr[:, 2, :], func=AF.Exp, scale=cf),
                     reads=[posr_r, lg_r], writes=[qrow_r])
                P.op('act', lambda e, cb=cb, lh=lh: e.activation(out=qrow[:, lh, 1, :], in_=posr[:, 3, :], func=AF.Exp, scale=cb),
                     reads=[posr_r, lg_r], writes=[qrow_r])
                P.op('act', lambda e, cf=cf, lh=lh: e.activation(out=dcc[:, lh, 0:1], in_=cf, func=AF.Exp, scale=128.0),
                     reads=[lg_r], writes=[dcc_r])
                P.op('act', lambda e, cb=cb, lh=lh: e.activation(out=dcc[:, lh, 1:2], in_=cb, func=AF.Exp, scale=128.0),
                     reads=[lg_r], writes=[dcc_r])
        P.op('dve', lambda e: e.tensor_scalar(out=dkv, in0=dkv, scalar1=KS, scalar2=None, op0=ALU.mult),
             reads=[dkv_r], writes=[dkv_r])
        A.release(m0)

        SLOT_ELEMS = 22 * 256
        NSLOT = 3
        slots = [A.bf([128, SLOT_ELEMS], "wslot%d" % i) for i in range(NSLOT)]
        wst = {'i': 0}

        stash = {}

        def prefetch_panel(key, parts):
            stash[key] = load_panel(parts)

        def load_panel(parts, key=None):
            if key is not None and key in stash:
                return stash.pop(key)
            sl, sl_r = slots[wst['i'] % NSLOT]
            wst['i'] += 1
            for (off, kc, ncol, src) in parts:
                dst = sl[:, off:off + kc * ncol].rearrange("p (k n) -> p k n", k=kc)
                P.dma('pool', dst, src, writes=[sl_r])
            return sl, sl_r

        def wsrc(w2d, kc, c0, ncol):
            return w2d.rearrange("(k p) n -> p k n", p=128)[:, :, c0:c0 + ncol]

        def pview(sl, off, kc, ncol):
            return sl[:, off:off + kc * ncol].rearrange("p (k n) -> p k n", k=kc)

        h, _ = A.f32([128, 8, T], "h")
        h_r = [Res("h%d" % c) for c in range(8)]
        modsL = [A.f32([128, 72, 3], "mods%d" % l) for l in range(DEPTH)]
        modAL = [[A.f32([128, 3, 8], "modA%d%d" % (l, ci)) for ci in range(2)] for l in range(DEPTH)]
        modGL = [[A.f32([128, 3, 8], "modG%d%d" % (l, ci)) for ci in range(2)] for l in range(DEPTH)]
        cur = {}
        base_mark = A.mark()

        def dump(name, ap, res, shape):
            d = dout("dbg_" + name, shape)
            dbg_out[name] = shape
            P.dma('sp', d, ap, reads=[res])

        sndM = nc.dram_tensor("sndM", [128, DEPTH * 27], F32)
        rcvM = nc.dram_tensor("rcvM", [NCORES * 128, DEPTH * 27], F32)

        def compute_mods_all():
            m = A.mark()
            modp, modp_r = A.f32([128, DEPTH * 9, 3], "modp")
            bk, bk_r = bank()
            for l in range(DEPTH):
                for pn in range(3):
                    sl, sl_r = load_panel([(0, 8, 384, wsrc(w_ada[l], 8, pn * 384, 384))])
                    wv = pview(sl, 0, 8, 384)
                    for j in range(3):
                        col = (l * 9 + pn * 3 + j) * 3
                        for kc in range(8):
                            P.op('pe', lambda e: e.matmul(
                                bk[:, col:col + 3], wv[:, kc, j * 128:(j + 1) * 128], scond[:, kc, :],
                                start=(kc == 0), stop=(kc == 7)),
                                reads=[sl_r, scond_r], writes=[bk_r], inc=(kc == 7))
            P.op('dve', lambda e: e.tensor_tensor(
                out=modp, in0=bk[:, 0:DEPTH * 27].rearrange("p (a b) -> p a b", b=3), in1=bcast_last(badap, 3), op=ALU.add),
                reads=[bk_r, badap_r], writes=[modp_r])
            sm_r, rm_r = Res("sndM"), Res("rcvM")
            P.dma('sp', sndM.ap(), modp.rearrange("p a b -> p (a b)"), reads=[modp_r], writes=[sm_r])
            all_gather(sndM, rcvM, sm_r, rm_r)
            rv = rcvM.ap().rearrange("(q p) (l x) -> q l p x", q=NCORES, l=DEPTH)
            for l in range(DEPTH):
                mods, mods_r = modsL[l]
                for q in range(NCORES):
                    P.dma('sp', mods[:, q * 9:(q + 1) * 9, :].rearrange("p a b -> p (a b)"), rv[q, l], reads=[rm_r], writes=[mods_r])
            A.release(m)
            for l in range(DEPTH):
                mods, mods_r = modsL[l]
                P.op('dve', lambda e: e.tensor_scalar(out=mods[:, :, 1], in0=mods[:, :, 1], scalar1=ct0[:, 0:1], scalar2=None, op0=ALU.mult),
                     reads=[mods_r, ct0_r], writes=[mods_r])
                P.op('dve', lambda e: e.scalar_tensor_tensor(out=mods[:, :, 1], in0=mods[:, :, 2], scalar=ct0[:, 1:2], in1=mods[:, :, 1],
                                                             op0=ALU.mult, op1=ALU.add), reads=[mods_r, ct0_r], writes=[mods_r])
                derive_mods(l)

        def derive_mods(l):
            mods, mods_r = modsL[l]
            for ci in range(2):
                modA, modA_r = modAL[l][ci]
                modG, modG_r = modGL[l][ci]
                for i in range(3):
                    nw = vecs[:, V_NORM + (l * 3 + i) * 8:V_NORM + (l * 3 + i) * 8 + 8]
                    sc = mods[:, (3 * i + 1) * 8:(3 * i + 1) * 8 + 8, ci]
                    gt = mods[:, (3 * i + 2) * 8:(3 * i + 2) * 8 + 8, ci]
                    P.op('dve', lambda e: e.scalar_tensor_tensor(
                        out=modA[:, i, :], in0=sc, scalar=1.0, in1=nw, op0=ALU.add, op1=ALU.mult),
                        reads=[mods_r, vecs_r], writes=[modA_r])
                    P.op('dve', lambda e: e.tensor_scalar(
                        out=modG[:, i, :], in0=gt, scalar1=(1.0 if i == 1 else 0.5), scalar2=None, op0=ALU.mult),
                        reads=[mods_r], writes=[modG_r])

        def use_mods(l, ci):
            cur['mods'], cur['mods_r'] = modsL[l]
            cur['modA'], cur['modA_r'] = modAL[l][ci]
            cur['modG'], cur['modG_r'] = modGL[l][ci]

        def load_x(g):
            m = A.mark()
            xv = xin[g].rearrange("(t p) d -> p t d", p=128)
            sts = [A.f32([128, D], "xst%d" % k) for k in range(2)]
            for tt in range(8):
                st, st_r = sts[tt % 2]
                P.dma('sp', st, xv[:, tt, :], writes=[st_r])
                for half in range(2):
                    bk, bk_r = bank()
                    for j in range(4):
                        c = half * 4 + j
                        P.op('pe', lambda e, c=c, j=j, st=st, bk=bk: e.transpose(
                            bk[:, j * 128:(j + 1) * 128], st[:, c * 128:(c + 1) * 128], ident),
                            reads=[st_r, ident_r], writes=[bk_r])
                    P.op('dve' if half else 'act',
                         (lambda e, half=half, bk=bk, tt=tt: e.tensor_copy(
                             out=h[:, half * 4:half * 4 + 4, tt * 128:(tt + 1) * 128],
                             in_=bk.rearrange("p (a b) -> p a b", a=4))) if half else
                         (lambda e, half=half, bk=bk, tt=tt: e.copy(
                             out=h[:, half * 4:half * 4 + 4, tt * 128:(tt + 1) * 128],
                             in_=bk.rearrange("p (a b) -> p a b", a=4))),
                         reads=[bk_r], writes=h_r[half * 4:half * 4 + 4], waw=False)
            A.release(m)

        def store_y(g):
            m = A.mark()
            yv = y_out[g].rearrange("(t p) d -> p t d", p=128)
            sts = [A.f32([128, D], "yst%d" % i) for i in range(2)]
            for tt in range(8):
                st, st_r = sts[tt % 2]
                for half in range(2):
                    bk, bk_r = bank()
                    for j in range(4):
                        c = half * 4 + j
                        P.op('pe', lambda e, c=c, j=j, bk=bk, tt=tt: e.transpose(
                            bk[:, j * 128:(j + 1) * 128], h[:, c, tt * 128:(tt + 1) * 128], ident),
                            reads=[h_r[c], ident_r], writes=[bk_r])
                    if half:
                        P.op('dve', lambda e, bk=bk, st=st, half=half: e.tensor_copy(out=st[:, half * 512:(half + 1) * 512], in_=bk),
                             reads=[bk_r], writes=[st_r], waw=False)
                    else:
                        P.op('act', lambda e, bk=bk, st=st, half=half: e.copy(out=st[:, half * 512:(half + 1) * 512], in_=bk),
                             reads=[bk_r], writes=[st_r], waw=False)
                P.dma('sp', yv[:, tt, :], st, reads=[st_r])
            A.release(m)

        ssq = {'pend': [], 'k': 0}

        def emit_sumsq(c, s):
            ts = slice(s * SUB, (s + 1) * SUB)
            sqa, sqr = sqp[ssq['k'] % len(sqp)]
            ssq['k'] += 1
            P.op('act', lambda e: e.activation(out=sqa, in_=h[:, c, ts], func=AF.Square), reads=[h_r[c]], writes=[sqr])
            bk, bk_r = banks[6 + s]

            def mm():
                P.op('pe', lambda e: e.matmul(bk, ones_bf, sqa, start=(c == 0), stop=(c == 7)), reads=[sqr, ones_r], writes=[bk_r])
            ssq['pend'].append(mm)

        def flush_ss(keep=0):
            while len(ssq['pend']) > keep:
                ssq['pend'].pop(0)()

        def norm_stage(i, ci, xn, xn_r, have_ss=True):
            m = A.mark()
            rstd = [A.f32([128, SUB], "rstd%d" % k) for k in range(2)]
            tmp = [A.f32([128, SUB], "ntmp%d" % k) for k in range(4)]
            if not have_ss:
                for s in range(NS):
                    for c in range(8):
                        emit_sumsq(c, s)
                        flush_ss(keep=1)
            flush_ss()
            modA, modA_r, mods, mods_r = cur['modA'], cur['modA_r'], cur['mods'], cur['mods_r']
            for s in range(NS):
                ts = slice(s * SUB, (s + 1) * SUB)
                bk, bk_r = banks[6 + s]
                rs, rs_r = rstd[s]
                t0a, t0r = tmp[2 * s]
                t1a, t1r = tmp[2 * s + 1]
                P.op('act', lambda e: e.activation(out=t0a, in_=bk, func=AF.Ln, scale=1.0 / D, bias=eps_ap),
                     reads=[bk_r, eps_r], writes=[t0r])
                P.op('act', lambda e: e.activation(out=rs, in_=t0a, func=AF.Exp, scale=-0.5), reads=[t0r], writes=[rs_r])
            for s in range(NS):
                ts = slice(s * SUB, (s + 1) * SUB)
                rs, rs_r = rstd[s]
                for c in range(8):
                    ta, tr = tmp[c % 4]
                    P.op('dve', lambda e: e.scalar_tensor_tensor(
                        out=ta, in0=h[:, c, ts], scalar=modA[:, i, c:c + 1], in1=rs, op0=ALU.mult, op1=ALU.mult),
                        reads=[h_r[c], modA_r, rs_r], writes=[tr])
                    P.op('act', lambda e: e.activation(
                        out=xn[:, c, ts], in_=ta, func=AF.Identity, bias=mods[:, 3 * i * 8 + c, ci:ci + 1], scale=1.0),
                        reads=[tr, mods_r], writes=[xn_r], waw=False)
            A.release(m)

        def h_update(bk, bk_r, i, c, ts, s):
            modG, modG_r = cur['modG'], cur['modG_r']
            P.op('dve', lambda e: e.scalar_tensor_tensor(
                out=h[:, c, ts], in0=bk, scalar=modG[:, i, c:c + 1], in1=h[:, c, ts], op0=ALU.mult, op1=ALU.add),
                reads=[bk_r, modG_r, h_r[c]], writes=[h_r[c]])
            emit_sumsq(c, s)
            flush_ss(keep=2)

        def ffn_stage(l, f, i, xn, xn_r):
            m = A.mark()
            hid, hid_r = A.bf([128, NFC, T], "hid")
            stm = [A.f32([128, SUB], "silu%d" % k) for k in range(3)]
            wi = w_ffn_in[l, f]
            k = 0
            for u in range(NFC // 2):
                sl, sl_r = load_panel([(0, 8, 256, wsrc(wi, 8, u * 256, 256)),
                                       (8 * 256, 8, 256, wsrc(wi, 8, FFN + u * 256, 256))])
                gv = pview(sl, 0, 8, 256)
                uv = pview(sl, 8 * 256, 8, 256)
                for j in range(2):
                    fc = u * 2 + j
                    for s in range(NS):
                        ts = slice(s * SUB, (s + 1) * SUB)
                        bg, bg_r = bank()
                        for kc in range(8):
                            P.op('pe', lambda e, bg=bg, gv=gv, kc=kc, j=j, ts=ts: e.matmul(
                                bg, gv[:, kc, j * 128:(j + 1) * 128], xn[:, kc, ts], start=(kc == 0), stop=(kc == 7)),
                                reads=[sl_r, xn_r], writes=[bg_r], inc=(kc == 7))
                        bu, bu_r = bank()
                        for kc in range(8):
                            P.op('pe', lambda e, bu=bu, uv=uv, kc=kc, j=j, ts=ts: e.matmul(
                                bu, uv[:, kc, j * 128:(j + 1) * 128], xn[:, kc, ts], start=(kc == 0), stop=(kc == 7)),
                                reads=[sl_r, xn_r], writes=[bu_r], inc=(kc == 7))
                        sa, sr = stm[k % 3]
                        k += 1
                        P.op('act', lambda e, sa=sa, bg=bg: e.activation(out=sa, in_=bg, func=AF.Silu),
                             reads=[bg_r], writes=[sr])
                        P.op('dve', lambda e, sa=sa, bu=bu, fc=fc, ts=ts: e.tensor_tensor(
                            out=hid[:, fc, ts], in0=bu, in1=sa, op=ALU.mult),
                            reads=[bu_r, sr], writes=[hid_r], waw=False)
            wo = w_ffn_out[l, f]
            for pn in range(4):
                sl, sl_r = load_panel([(0, NFC, 256, wsrc(wo, NFC, pn * 256, 256))])
                wv = pview(sl, 0, NFC, 256)
                for j in range(2):
                    c = pn * 2 + j
                    for s in range(NS):
                        ts = slice(s * SUB, (s + 1) * SUB)
                        bk, bk_r = bank()
                        for kc in range(NFC):
                            P.op('pe', lambda e, bk=bk, wv=wv, kc=kc, j=j, ts=ts: e.matmul(
                                bk, wv[:, kc, j * 128:(j + 1) * 128], hid[:, kc, ts], start=(kc == 0), stop=(kc == NFC - 1)),
                                reads=[sl_r, hid_r], writes=[bk_r], inc=(kc == NFC - 1))
                        h_update(bk, bk_r, i, c, ts, s)
            flush_ss()
            A.release(m)

        eps_ap, eps_r = None, None

        def setup_eps():
            nonlocal eps_ap, eps_r
            eps_ap, eps_r = A.f32([128, 1], "eps")
            P.op('dve', lambda e: e.memset(eps_ap, EPS), writes=[eps_r])

        def proj_fm(sl_r, wv, col0, xn, xn_r, s, kcn=8):
            ts = slice(s * SUB, (s + 1) * SUB)
            bk, bk_r = bank()
            for kc in range(kcn):
                P.op('pe', lambda e, bk=bk, kc=kc: e.matmul(
                    bk, wv[:, kc, col0:col0 + 128], xn[:, kc, ts], start=(kc == 0), stop=(kc == kcn - 1)),
                    reads=[sl_r, xn_r], writes=[bk_r], inc=(kc == kcn - 1))
            return bk, bk_r

        def retention(l, g, xn, xn_r, orn, orn_r, seeds=None, phase1=None):
            wi = w_inx[l]
            segs = [(sq_ * 2, 2) for sq_ in range(4)] if g == 0 else [(0, 8)]
            m = A.mark()
            nset = 1 if phase1 is not None else 2
            sets = []
            for k_ in range(nset):
                B = {}
                for nm in ("vtok", "kdf", "kdb"):
                    B[nm] = A.bf([128, 8, 128], nm + str(k_))
                B["Sbf"] = A.bf([128, 8, 2, 128], "Sbf%d" % k_)
                B["S32"] = A.f32([128, 2, 2, 128], "S32r%d" % k_)
                if phase1 is None:
                    for nm in ("qT", "kT", "sg", "qdf", "qdb"):
                        B[nm] = A.bf([128, T], nm + str(k_))
                    B["am"] = A.bf([128, 8, 128], "am%d" % k_)
                sets.append(B)
            P32, P32_r = A.f32([128, 8, 2, 128], "P32")
            if g == 0:
                stout, stout_r = A.f32([128, 4, 2, 128], "stout")
            if phase1 is None:
                osq = [A.bf([128, SUB], "osq%d" % k_) for k_ in range(2)]
                ors = [A.f32([128, SUB], "ors%d" % k_) for k_ in range(2)]
                otm = [A.f32([128, SUB], "otm%d" % k_) for k_ in range(2)]
            state = {}

            def front(hh):
                lh = l * 4 + hh
                B = sets[hh % nset]
                vtok, vtok_r = B["vtok"]
                kdf, kdf_r = B["kdf"]
                kdb, kdb_r = B["kdb"]
                Sbf, Sbf_r = B["Sbf"]
                S32, S32_r = B["S32"]
                sl, sl_r = load_panel([(0, 8, 512, wsrc(wi, 8, X_RET + hh * 512, 512))], key=("ret", l, hh, phase1 is None))
                wv = pview(sl, 0, 8, 512)
                for tp in range(4):
                    bk, bk_r = bank()
                    for q2 in range(2):
                        tt = tp * 2 + q2
                        for kc in range(8):
                            P.op('pe', lambda e: e.matmul(
                                bk[:, q2 * 256:(q2 + 1) * 256], xn[:, kc, tt * 128:(tt + 1) * 128], wv[:, kc, 128:384],
                                start=(kc == 0), stop=(kc == 7)),
                                reads=[sl_r, xn_r], writes=[bk_r], inc=(kc == 7))
                    b3 = bk.rearrange("p (a b) -> p a b", a=2)
                    P.op('act', lambda e: e.activation(
                        out=kdf[:, tp * 2:tp * 2 + 2, :], in_=b3[:, :, 0:128], func=AF.Copy, scale=dkv[:, lh, 0:1]),
                        reads=[bk_r, dkv_r], writes=[kdf_r], waw=False)
                    P.op('dve', lambda e: e.tensor_scalar(
                        out=kdb[:, tp * 2:tp * 2 + 2, :], in0=b3[:, :, 0:128], scalar1=dkv[:, lh, 1:2], scalar2=None, op0=ALU.mult),
                        reads=[bk_r, dkv_r], writes=[kdb_r], waw=False)
                    P.op('act', lambda e: e.copy(out=vtok[:, tp * 2:tp * 2 + 2, :], in_=b3[:, :, 128:256]),
                         reads=[bk_r], writes=[vtok_r], waw=False)
                if phase1 is None:
                    qT, qT_r = B["qT"]
                    kT, kT_r = B["kT"]
                    sg, sg_r = B["sg"]
                    qdf, qdf_r = B["qdf"]
                    qdb, qdb_r = B["qdb"]
                    for s in range(NS):
                        ts = slice(s * SUB, (s + 1) * SUB)
                        bk, bk_r = proj_fm(sl_r, wv, 0, xn, xn_r, s)
                        b3 = bk.rearrange("p (a b) -> p a b", a=4)
                        P.op('act', lambda e: e.copy(out=qT[:, ts], in_=bk), reads=[bk_r], writes=[qT_r], waw=False)
                        P.op('dve', lambda e: e.tensor_tensor(
                            out=qdf[:, ts].rearrange("p (a b) -> p a b", a=4), in0=b3, in1=bcast_mid(qrow[:, lh, 0, :], 4), op=ALU.mult),
                            reads=[bk_r, qrow_r], writes=[qdf_r], waw=False)
                        P.op('dve', lambda e: e.tensor_tensor(
                            out=qdb[:, ts].rearrange("p (a b) -> p a b", a=4), in0=b3, in1=bcast_mid(qrow[:, lh, 1, :], 4), op=ALU.mult),
                            reads=[bk_r, qrow_r], writes=[qdb_r], waw=False)
                        bk, bk_r = proj_fm(sl_r, wv, 128, xn, xn_r, s)
                        P.op('act', lambda e: e.activation(out=kT[:, ts], in_=bk, func=AF.Copy, scale=KS),
                             reads=[bk_r], writes=[kT_r], waw=False)
                        bk, bk_r = proj_fm(sl_r, wv, 384, xn, xn_r, s)
                        P.op('act', lambda e: e.activation(out=sg[:, ts], in_=bk, func=AF.Silu),
                             reads=[bk_r], writes=[sg_r], waw=False)
                for jp in range(4):
                    bk, bk_r = bank()
                    for q2 in range(2):
                        j = jp * 2 + q2
                        P.op('pe', lambda e: e.matmul(
                            bk[:, q2 * 256:q2 * 256 + 128], kdf[:, j, :], vtok[:, j, :], start=True, stop=True),
                            reads=[kdf_r, vtok_r], writes=[bk_r])
                        P.op('pe', lambda e: e.matmul(
                            bk[:, q2 * 256 + 128:q2 * 256 + 256], kdb[:, j, :], vtok[:, j, :], start=True, stop=True),
                            reads=[kdb_r, vtok_r], writes=[bk_r])
                    b4 = bk.rearrange("p (a b c) -> p a b c", a=2, b=2)
                    if jp % 2:
                        P.op('dve', lambda e: e.tensor_copy(out=P32[:, jp * 2:jp * 2 + 2, :, :], in_=b4),
                             reads=[bk_r], writes=[P32_r], waw=False)
                    else:
                        P.op('act', lambda e: e.copy(out=P32[:, jp * 2:jp * 2 + 2, :, :], in_=b4),
                             reads=[bk_r], writes=[P32_r], waw=False)
                if phase1 is None:
                    am, am_r = B["am"]
                    for s in range(NS):
                        ba, ba_r = bank()
                        for jj in range(4):
                            j = s * 4 + jj
                            cs = slice(j * 128, (j + 1) * 128)
                            P.op('pe', lambda e: e.matmul(
                                ba[:, jj * 128:(jj + 1) * 128], kT[:, cs], qT[:, cs], start=True, stop=True),
                                reads=[kT_r, qT_r], writes=[ba_r])
                        P.op('dve', lambda e: e.tensor_tensor(
                            out=am[:, s * 4:s * 4 + 4, :], in0=ba.rearrange("p (a b) -> p a b", a=4),
                            in1=bcast_mid(maskT[:, lh, :], 4), op=ALU.mult),
                            reads=[ba_r, maskT_r], writes=[am_r], waw=False)
                has_f = {}
                has_b = {}
                for si, (j0, n) in enumerate(segs):
                    for d_, dc in ((0, dcc[:, lh, 0:1]), (1, dcc[:, lh, 1:2])):
                        order = list(range(j0, j0 + n)) if d_ == 0 else list(range(j0 + n - 1, j0 - 1, -1))
                        has = has_f if d_ == 0 else has_b
                        seed = None if seeds is None else seeds[d_]
                        cur32 = None
                        for k_, j in enumerate(order):
                            nxt = S32[:, k_ % 2, d_, :]
                            if k_ == 0:
                                if seed is not None:
                                    sap, s_r = seed
                                    P.op('dve', lambda e: e.tensor_copy(out=nxt, in_=sap[:, hh, :]),
                                         reads=[s_r], writes=[S32_r], waw=False)
                                    has[j] = True
                                    cur32 = nxt
                                else:
                                    has[j] = False
                            else:
                                pj = order[k_ - 1]
                                if has[pj]:
                                    P.op('dve', lambda e: e.scalar_tensor_tensor(
                                        out=nxt, in0=cur32, scalar=dc, in1=P32[:, pj, d_, :],
                                        op0=ALU.mult, op1=ALU.add), reads=[S32_r, P32_r, dcc_r], writes=[S32_r], waw=False)
                                    cur32 = nxt
                                else:
                                    cur32 = P32[:, pj, d_, :]
                                has[j] = True
                            if has[j]:
                                if cur32 is nxt:
                                    P.op('dve', lambda e: e.tensor_copy(out=Sbf[:, j, d_, :], in_=cur32),
                                         reads=[S32_r], writes=[Sbf_r], waw=False)
                                else:
                                    P.op('dve', lambda e: e.tensor_copy(out=Sbf[:, j, d_, :], in_=cur32),
                                         reads=[P32_r], writes=[Sbf_r], waw=False)
                        if g == 0 or phase1 is not None:
                            lj = order[-1]
                            dst = stout[:, si, d_, :] if g == 0 else phase1[0][:, hh, d_, :]
                            dst_r = stout_r if g == 0 else phase1[1]
                            if has[lj]:
                                P.op('dve', lambda e: e.scalar_tensor_tensor(
                                    out=dst, in0=cur32, scalar=dc, in1=P32[:, lj, d_, :],
                                    op0=ALU.mult, op1=ALU.add), reads=[S32_r, P32_r, dcc_r], writes=[dst_r], waw=False)
                            else:
                                P.op('dve', lambda e: e.tensor_copy(out=dst, in_=P32[:, lj, d_, :]),
                                     reads=[P32_r], writes=[dst_r], waw=False)
                if g == 0:
                    for sq_ in range(4):
                        P.dma('sp', nstate[sq_, l, :, hh].rearrange("r d e -> d r e"), stout[:, sq_, :, :], reads=[stout_r])
                state[hh] = (has_f, has_b)

            def back(hh):
                lh = l * 4 + hh
                B = sets[hh % nset]
                vtok, vtok_r = B["vtok"]
                Sbf, Sbf_r = B["Sbf"]
                sg, sg_r = B["sg"]
                qdf, qdf_r = B["qdf"]
                qdb, qdb_r = B["qdb"]
                am, am_r = B["am"]
                has_f, has_b = state[hh]
                bos = []
                for s in range(NS):
                    bo, bo_r = bank()
                    bos.append((bo, bo_r))
                    for jj in range(4):
                        j = s * 4 + jj
                        cs = slice(j * 128, (j + 1) * 128)
                        ops = [(vtok[:, j, :], am[:, j, :], [vtok_r, am_r])]
                        if has_f[j]:
                            ops.append((Sbf[:, j, 0, :], qdf[:, cs], [Sbf_r, qdf_r]))
                        if has_b[j]:
                            ops.append((Sbf[:, j, 1, :], qdb[:, cs], [Sbf_r, qdb_r]))
                        n_ = len(ops)
                        for k_, (lt, rh, rr) in enumerate(ops):
                            P.op('pe', lambda e: e.matmul(
                                bo[:, jj * 128:(jj + 1) * 128], lt, rh, start=(k_ == 0), stop=(k_ == n_ - 1)),
                                reads=rr, writes=[bo_r], inc=(k_ == n_ - 1))
                for s in range(NS):
                    ts = slice(s * SUB, (s + 1) * SUB)
                    bo, bo_r = bos[s]
                    sqa, sqa_r = osq[s % 2]
                    rsa, rsa_r = ors[s % 2]
                    tma, tma_r = otm[s % 2]
                    P.op('act', lambda e: e.activation(out=sqa, in_=bo, func=AF.Square), reads=[bo_r], writes=[sqa_r])
                    bs, bs_r = bank()
                    P.op('pe', lambda e: e.matmul(bs, ones_bf, sqa, start=True, stop=True),
                         reads=[sqa_r, ones_r], writes=[bs_r])
                    P.op('act', lambda e: e.activation(out=tma, in_=bs, func=AF.Ln, scale=1.0 / 128, bias=eps_ap),
                         reads=[bs_r, eps_r], writes=[tma_r])
                    P.op('act', lambda e: e.activation(out=rsa, in_=tma, func=AF.Exp, scale=-0.5), reads=[tma_r], writes=[rsa_r])
                    P.op('dve', lambda e: e.scalar_tensor_tensor(
                        out=tma, in0=bo, scalar=vcol(V_GN + lh), in1=rsa, op0=ALU.mult, op1=ALU.mult),
                        reads=[bo_r, rsa_r, vecs_r], writes=[tma_r])
                    P.op('dve', lambda e: e.tensor_tensor(out=orn[:, hh, ts], in0=tma, in1=sg[:, ts], op=ALU.mult),
                         reads=[tma_r, sg_r], writes=[orn_r], waw=False)

            if phase1 is not None:
                for hh in range(4):
                    front(hh)
            else:
                front(0)
                for hh in range(4):
                    if hh + 1 < 4:
                        front(hh + 1)
                    back(hh)
            A.release(m)

        def norm_rope(bq, bq_r, bqs, bqs_r, gcol, gscol, ts, outs, tmps, rope_tabs):
            sqa, sqa_r = tmps['sq']
            rsa, rsa_r = tmps['rs']
            t1, t1_r = tmps['t1']
            P.op('act', lambda e: e.activation(out=sqa, in_=bq, func=AF.Square), reads=[bq_r], writes=[sqa_r])
            bs, bs_r = bank()
            P.op('pe', lambda e: e.matmul(bs, blk_bf, sqa, start=True, stop=True), reads=[sqa_r, blk_r], writes=[bs_r])
            P.op('act', lambda e: e.activation(out=t1, in_=bs, func=AF.Ln, scale=1.0 / 64, bias=eps_ap),
                 reads=[bs_r, eps_r], writes=[t1_r])
            P.op('act', lambda e: e.activation(out=rsa, in_=t1, func=AF.Exp, scale=-0.5), reads=[t1_r], writes=[rsa_r])
            if rope_tabs is None:
                P.op('dve', lambda e: e.scalar_tensor_tensor(out=t1, in0=bq, scalar=vcol(gcol), in1=rsa, op0=ALU.mult, op1=ALU.mult),
                     reads=[bq_r, rsa_r, vecs_r], writes=[t1_r])
            else:
                rp, rp_r = rope_tabs
                t2, t2_r = tmps['t2']
                P.op('dve', lambda e: e.scalar_tensor_tensor(out=t1, in0=bq, scalar=vcol(gcol), in1=rp[:, 0, ts], op0=ALU.mult, op1=ALU.mult),
                     reads=[bq_r, rp_r, vecs_r], writes=[t1_r])
                P.op('dve', lambda e: e.scalar_tensor_tensor(out=t2, in0=bqs, scalar=vcol(gscol), in1=rp[:, 1, ts], op0=ALU.mult, op1=ALU.mult),
                     reads=[bqs_r, rp_r, vecs_r], writes=[t2_r])
                P.op('dve', lambda e: e.tensor_tensor(out=t1, in0=t1, in1=t2, op=ALU.add), reads=[t1_r, t2_r], writes=[t1_r])
                P.op('dve', lambda e: e.tensor_tensor(out=t1, in0=t1, in1=rsa, op=ALU.mult), reads=[t1_r, rsa_r], writes=[t1_r])
            for k_, (oa, oa_r) in enumerate(outs):
                if k_ % 2 == 0:
                    P.op('act', lambda e, oa=oa: e.copy(out=oa, in_=t1), reads=[t1_r], writes=[oa_r], waw=False)
                else:
                    P.op('dve', lambda e, oa=oa: e.tensor_copy(out=oa, in_=t1), reads=[t1_r], writes=[oa_r], waw=False)

        def attention(l, g, xn, xn_r, oatt, oatt_r, rope_tabs=None, exch=None, skip_kv=False):
            wi = w_inx[l]
            m = A.mark()
            nkch = 8 if g == 0 else NKS // 128
            koff = 0 if g == 0 else PAST // 128
            QT, QT_r = A.bf([128, 4, T], "QT")
            KT, KT_r = A.bf([128, nkch * 128], "KT")
            Vx, Vx_r = A.bf([128, nkch, 2, 128], "Vx")
            P.op('dve', lambda e: e.memset(Vx[:, :, :, 64:128], 1.0), writes=[Vx_r])
            tmps = {'sq': A.bf([128, SUB], "asq"), 'rs': A.f32([128, SUB], "ars"), 't1': A.f32([128, SUB], "at1"),
                    't2': A.f32([128, SUB], "at2")}
            slA, slA_r = load_panel([(0, 8, 512, wsrc(wi, 8, X_AQ, 512))])
            wA = pview(slA, 0, 8, 512)
            if rope_tabs is not None:
                slB, slB_r = load_panel([(0, 8, 512, wsrc(wi, 8, X_AQS, 512))])
                wB = pview(slB, 0, 8, 512)
            for j in range(4):
                for s in range(NS):
                    ts = slice(s * SUB, (s + 1) * SUB)
                    bq, bq_r = proj_fm(slA_r, wA, j * 128, xn, xn_r, s)
                    bqs, bqs_r = (None, None)
                    if rope_tabs is not None:
                        bqs, bqs_r = proj_fm(slB_r, wB, j * 128, xn, xn_r, s)
                    norm_rope(bq, bq_r, bqs, bqs_r, V_QG + l, V_QGS + l, ts, [(QT[:, j, ts], QT_r)], tmps, rope_tabs)
            if not skip_kv:
                slC, slC_r = load_panel([(0, 8, 384, wsrc(wi, 8, X_AK, 384))])
                wC = pview(slC, 0, 8, 384)
                kf32 = None
                if g == 0:
                    kf32, kf32_r = A.f32([128, T], "kf32")
                    vout, vout_r = A.f32([128, 8, 128], "vout")
                    kout, kout_r = A.f32([128, 8, 128], "kout")
                for s in range(NS):
                    ts = slice(s * SUB, (s + 1) * SUB)
                    bq, bq_r = proj_fm(slC_r, wC, 0, xn, xn_r, s)
                    bqs, bqs_r = (None, None)
                    if rope_tabs is not None:
                        bqs, bqs_r = proj_fm(slC_r, wC, 128, xn, xn_r, s)
                    kts = slice(koff * 128 + s * SUB, koff * 128 + (s + 1) * SUB)
                    outs = [(KT[:, kts], KT_r)]
                    if g == 0:
                        outs.append((kf32[:, ts], kf32_r))
                    norm_rope(bq, bq_r, bqs, bqs_r, V_KG + l, V_KGS + l, ts, outs, tmps, rope_tabs)
                for tp in range(2):
                    bk, bk_r = bank()
                    for q4 in range(4):
                        tt = tp * 4 + q4
                        for kc in range(8):
                            P.op('pe', lambda e, bk=bk, kc=kc, tt=tt, q4=q4: e.matmul(
                                bk[:, q4 * 128:(q4 + 1) * 128], xn[:, kc, tt * 128:(tt + 1) * 128], wC[:, kc, 256:384],
                                start=(kc == 0), stop=(kc == 7)),
                                reads=[slC_r, xn_r], writes=[bk_r], inc=(kc == 7))
                    P.op('act', lambda e, bk=bk, tp=tp: e.copy(
                        out=Vx[:, koff + tp * 4:koff + tp * 4 + 4, :, 0:64], in_=bk.rearrange("p (a b c) -> p a b c", a=4, b=2)),
                        reads=[bk_r], writes=[Vx_r])
                    if g == 0:
                        P.op('dve', lambda e, bk=bk, tp=tp: e.tensor_copy(
                            out=vout[:, tp * 4:tp * 4 + 4, :], in_=bk.rearrange("p (a b) -> p a b", a=4)),
                            reads=[bk_r], writes=[vout_r], waw=False)
            if g == 0:
                for sq_ in range(4):
                    P.dma('sp', ncv[sq_, l].rearrange("(t p) f -> p t f", p=128), vout[:, sq_ * 2:sq_ * 2 + 2, :], reads=[vout_r])
                for tp in range(2):
                    bk, bk_r = bank()
                    for q4 in range(4):
                        tt = tp * 4 + q4
                        P.op('pe', lambda e, bk=bk, tt=tt, q4=q4: e.transpose(
                            bk[:, q4 * 128:(q4 + 1) * 128], kf32[:, tt * 128:(tt + 1) * 128], ident),
                            reads=[kf32_r, ident_r], writes=[bk_r])
                    P.op('dve', lambda e, bk=bk, tp=tp: e.tensor_copy(
                        out=kout[:, tp * 4:tp * 4 + 4, :], in_=bk.rearrange("p (a b) -> p a b", a=4)),
                        reads=[bk_r], writes=[kout_r], waw=False)
                for sq_ in range(4):
                    P.dma('sp', nck[sq_, l].rearrange("(t p) f -> p t f", p=128), kout[:, sq_ * 2:sq_ * 2 + 2, :], reads=[kout_r])
            if exch is not None:
                exch(KT, KT_r, Vx, Vx_r)
            LA = 2
            pT = [A.bf([128, 2 * SUB], "pT%d" % k_) for k_ in range(LA + 2)]
            rec = [A.f32([64, SUB], "rec%d" % k_) for k_ in range(2)]
            recs = A.f32([64, SUB], "recs")
            items = []
            for qb in range(8):
                kchs = [2 * (qb // 2), 2 * (qb // 2) + 1] if g == 0 else list(range(nkch))
                for ki, kch in enumerate(kchs):
                    items.append(dict(q0=qb * 128, qb=qb, kch=kch, first=(ki == 0), last=(ki == len(kchs) - 1)))

            def front(i, it):
                kch, q0 = it['kch'], it['q0']
                pr = 2 * (i % 2)
                for gg in range(2):
                    ps_ = slice(gg * 64, (gg + 1) * 64)
                    bs, bs_r = banks[pr + gg]
                    P.op('pe', lambda e: e.matmul(bs, KT[ps_, kch * 128:(kch + 1) * 128], QT[ps_, :, q0:q0 + 128], start=True, stop=True),
                         reads=[KT_r, QT_r], writes=[bs_r])
                pa, pa_r = pT[i % (LA + 2)]
                pair = ps_all[:, pr * 512:(pr + 2) * 512]
                P.op('act', lambda e: e.activation(out=pa, in_=pair, func=AF.Exp, scale=0.125),
                     reads=[banks[pr][1], banks[pr + 1][1]], writes=[pa_r])
                it['pa'] = (pa, pa_r)

            def back(it):
                pa, pa_r = it['pa']
                kch, q0 = it['kch'], it['q0']
                first, last = it['first'], it['last']
                for gg in range(2):
                    bo, bo_r = banks[4 + 2 * (it['qb'] % 2) + gg]
                    P.op('pe', lambda e: e.matmul(bo, Vx[:, kch, gg, :], pa[:, gg * SUB:(gg + 1) * SUB], start=first, stop=last),
                         reads=[Vx_r, pa_r], writes=[bo_r])
                if last:
                    for gg in range(2):
                        ps_ = slice(gg * 64, (gg + 1) * 64)
                        bo, bo_r = banks[4 + 2 * (it['qb'] % 2) + gg]
                        ra, ra_r = rec[gg]
                        rs2, rs2_r = recs
                        P.op('dve', lambda e: e.reciprocal(out=ra, in_=bo[64:128, :]), reads=[bo_r], writes=[ra_r])
                        P.op('dve', lambda e: e.tensor_tensor(
                            out=oatt[ps_, :, q0:q0 + 128], in0=bo[0:64, :].rearrange("p (a b) -> p a b", a=4),
                            in1=ra.rearrange("p (a b) -> p a b", a=4), op=ALU.mult),
                            reads=[bo_r, ra_r], writes=[oatt_r], waw=False)

            for i in range(len(items) + LA):
                if i < len(items):
                    front(i, items[i])
                if i >= LA:
                    back(items[i - LA])
            A.release(m)

        def conv_stage(l, g, xn, xn_r, cvo, cvo_r, halo=None, edges=None):
            wi = w_inx[l]
            nseg, L = (4, 256) if g == 0 else (1, 1024)
            for c in range(4):
                m = A.mark()
                sl, sl_r = load_panel([(0, 8, 384, wsrc(wi, 8, X_CONV + c * 384, 384))], key=("conv", l, c))
                wv = pview(sl, 0, 8, 384)
                u, u_r = A.f32([128, nseg, L + 2], "u")
                cbs, cbs_r = A.f32([128, T], "cbs")
                acc, acc_r = A.f32([128, T], "acc")
                cxs = [A.f32([128, SUB], "cxs%d" % k_) for k_ in range(2)]
                if halo is None:
                    P.op('dve', lambda e, u=u: e.memset(u[:, :, 0:1], 0.0), writes=[u_r], waw=False)
                    P.op('dve', lambda e, u=u: e.memset(u[:, :, L + 1:L + 2], 0.0), writes=[u_r], waw=False)
                else:
                    halo(c, u, u_r)
                for s in range(NS):
                    ts = slice(s * SUB, (s + 1) * SUB)
                    bcb, bcb_r = proj_fm(sl_r, wv, 0, xn, xn_r, s)
                    bcc, bcc_r = proj_fm(sl_r, wv, 128, xn, xn_r, s)
                    bcx, bcx_r = proj_fm(sl_r, wv, 256, xn, xn_r, s)
                    cxa, cxa_r = cxs[s % 2]
                    P.op('act', lambda e, bcx=bcx, cxa=cxa: e.copy(out=cxa, in_=bcx), reads=[bcx_r], writes=[cxa_r])
                    P.op('act', lambda e, bcb=bcb, ts=ts: e.copy(out=cbs[:, ts], in_=bcb), reads=[bcb_r], writes=[cbs_r], waw=False)
                    if g == 0:
                        uo = u[:, 2 * s:2 * s + 2, 1:L + 1]
                        i0 = bcc.rearrange("p (a b) -> p a b", a=2)
                        i1 = cxa.rearrange("p (a b) -> p a b", a=2)
                    else:
                        uo = u[:, 0, 1 + s * SUB:1 + (s + 1) * SUB]
                        i0 = bcc
                        i1 = cxa
                    P.op('dve', lambda e, uo=uo, i0=i0, i1=i1: e.tensor_tensor(out=uo, in0=i0, in1=i1, op=ALU.mult),
                         reads=[bcc_r, cxa_r], writes=[u_r], waw=False)
                a3 = acc.rearrange("p (a b) -> p a b", a=nseg)
                wr = V_CONV + (l * 3) * 4 + c
                P.op('dve', lambda e, u=u, a3=a3, wr=wr: e.tensor_scalar(
                    out=a3, in0=u[:, :, 1:L + 1], scalar1=vcol(wr + 4), scalar2=None, op0=ALU.mult),
                    reads=[u_r, vecs_r], writes=[acc_r])
                P.op('dve', lambda e, u=u, a3=a3, wr=wr: e.scalar_tensor_tensor(
                    out=a3, in0=u[:, :, 0:L], scalar=vcol(wr), in1=a3, op0=ALU.mult, op1=ALU.add),
                    reads=[u_r, vecs_r, acc_r], writes=[acc_r])
                P.op('dve', lambda e, u=u, a3=a3, wr=wr: e.scalar_tensor_tensor(
                    out=a3, in0=u[:, :, 2:L + 2], scalar=vcol(wr + 8), in1=a3, op0=ALU.mult, op1=ALU.add),
                    reads=[u_r, vecs_r, acc_r], writes=[acc_r])
                P.op('dve', lambda e, acc=acc, cbs=cbs, c=c: e.tensor_tensor(out=cvo[:, c, :], in0=acc, in1=cbs, op=ALU.mult),
                     reads=[acc_r, cbs_r], writes=[cvo_r], waw=False)
                if edges is not None:
                    eg, eg_r = edges
                    for k_, (src_, o0, o1) in enumerate(((u[:, 0, :], 1, L), (acc, 0, L - 1), (cbs, 0, L - 1))):
                        for e_, off in enumerate((o0, o1)):
                            P.op('act', lambda e, src_=src_, off=off, k_=k_, e_=e_, c=c: e.copy(
                                out=eg[:, k_, c, e_:e_ + 1], in_=src_[:, off:off + 1]),
                                reads=[u_r, acc_r, cbs_r], writes=[eg_r], waw=False)
                A.release(m)

        def merge_stage(l, g, xn, xn_r, srcs, oo_t):
            m = A.mark()
            mg32, mg32_r = A.f32([128, 8, T], "mg32")
            mbf, mbf_r = oo_t[0], Res("mbf")
            alias_r = [srcs[0][1], srcs[1][1]]
            sgs = [A.f32([128, SUB], "msg%d" % k_) for k_ in range(2)]
            tms = [A.f32([128, SUB], "mtm%d" % k_) for k_ in range(2)]
            wos = [w_ret_o[l], w_att_o[l], w_conv_o[l]]
            wobuf = [A.bf([128, 4, 1024], "wobuf%d" % k_) for k_ in range(2)]
            k_ = 0
            for bi in range(3):
                src, src_r = srcs[bi]
                wov, slo_r = wobuf[bi % 2]
                P.dma('pool', wov, wsrc(wos[bi], 4, 0, 1024), writes=[slo_r])
                for half in range(2):
                    slg, slg_r = load_panel([(0, 8, 512, wsrc(w_inx[l], 8, X_GATE + bi * 1024 + half * 512, 512))])
                    wgv = pview(slg, 0, 8, 512)
                    for j in range(4):
                        c = half * 4 + j
                        for s in range(NS):
                            ts = slice(s * SUB, (s + 1) * SUB)
                            by, by_r = proj_fm(slo_r, wov, c * 128, src, src_r, s, kcn=4)
                            bg, bg_r = proj_fm(slg_r, wgv, j * 128, xn, xn_r, s)
                            sa, sa_r = sgs[k_ % 2]
                            ta, ta_r = tms[k_ % 2]
                            k_ += 1
                            P.op('act', lambda e, bg=bg, sa=sa: e.activation(out=sa, in_=bg, func=AF.Sigmoid), reads=[bg_r], writes=[sa_r])
                            if bi == 0:
                                P.op('dve', lambda e, by=by, sa=sa, c=c, ts=ts: e.tensor_tensor(out=mg32[:, c, ts], in0=by, in1=sa, op=ALU.mult),
                                     reads=[by_r, sa_r], writes=[mg32_r], waw=False)
                            else:
                                P.op('dve', lambda e, by=by, sa=sa, ta=ta: e.tensor_tensor(out=ta, in0=by, in1=sa, op=ALU.mult),
                                     reads=[by_r, sa_r], writes=[ta_r])
                                if bi == 1:
                                    P.op('dve', lambda e, ta=ta, c=c, ts=ts: e.tensor_tensor(out=mg32[:, c, ts], in0=mg32[:, c, ts], in1=ta, op=ALU.add),
                                         reads=[ta_r, mg32_r], writes=[mg32_r], waw=False)
                                else:
                                    P.op('dve', lambda e, ta=ta, c=c, ts=ts: e.tensor_tensor(out=mbf[:, c, ts], in0=mg32[:, c, ts], in1=ta, op=ALU.add),
                                         reads=[ta_r, mg32_r], writes=[mbf_r] + alias_r, waw=False)
            for half in range(2):
                sl, sl_r = load_panel([(0, 8, 512, wsrc(w_o[l], 8, half * 512, 512))])
                wv = pview(sl, 0, 8, 512)
                for j in range(4):
                    c = half * 4 + j
                    for s in range(NS):
                        ts = slice(s * SUB, (s + 1) * SUB)
                        bk, bk_r = proj_fm(sl_r, wv, j * 128, mbf, mbf_r, s)
                        h_update(bk, bk_r, 1, c, ts, s)
            flush_ss()
            for r_ in (mbf_r, srcs[0][1], srcs[1][1]):
                for dd in (r_.w, r_.r, r_.pw, r_.pr):
                    _merge(oo_t[1].pr, dd)
            A.release(m)

        AG_GROUPS = [list(range(NCORES))]

        def all_gather(snd_t, rcv_t, snd_r, rcv_r):
            P.collective(lambda e: e.collective_compute(
                "AllGather", ALU.bypass, replica_groups=AG_GROUPS, ins=[snd_t.ap().opt()], outs=[rcv_t.ap().opt()]),
                reads=[snd_r], writes=[rcv_r])

        sample_c = {}

        def setup_sample():
            ct, ct_r = ct0, ct0_r
            rope, rope_r = A.f32([128, 2, T], "rope")
            P.dma('sp', rope, rope_d.rearrange("k p t -> p k t"), writes=[rope_r])
            coef, coef_r = A.f32([128, DEPTH * 4, 2, 9], "coef")
            for l in range(DEPTH):
                for hh in range(4):
                    for d_ in range(2):
                        col = l * 8 + d_ * 4 + hh
                        P.op('act', lambda e, l=l, hh=hh, d_=d_, col=col: e.activation(
                            out=coef[:, l * 4 + hh, d_, :], in_=ct[:, 2 + d_ * 9:2 + d_ * 9 + 9], func=AF.Exp, scale=lg[:, col:col + 1]),
                            reads=[ct_r, lg_r], writes=[coef_r], waw=False)
            P.op('dve', lambda e: e.tensor_tensor(
                out=coef, in0=coef, in1=bcast_mid(ct[:, 20:38], DEPTH * 4).rearrange("p a (b c) -> p a b c", b=2), op=ALU.mult),
                reads=[coef_r, ct_r], writes=[coef_r])
            sample_c.update(ct=ct, ct_r=ct_r, rope=(rope, rope_r), coef=coef, coef_r=coef_r)

        def kv_early(l, xn, xn_r, edges, sndk_r):
            m = A.mark()
            wi = w_inx[l]
            rope_tabs = sample_c['rope']
            ktmp, ktmp_r = A.bf([128, T], "ktmp")
            vtmp, vtmp_r = A.bf([128, 8, 2, 64], "vtmp")
            tmps = {'sq': A.bf([128, SUB], "ksq"), 'rs': A.f32([128, SUB], "krs"), 't1': A.f32([128, SUB], "kt1"),
                    't2': A.f32([128, SUB], "kt2")}
            slC, slC_r = load_panel([(0, 8, 384, wsrc(wi, 8, X_AK, 384))])
            wC = pview(slC, 0, 8, 384)
            for s in range(NS):
                ts = slice(s * SUB, (s + 1) * SUB)
                bq, bq_r = proj_fm(slC_r, wC, 0, xn, xn_r, s)
                bqs, bqs_r = proj_fm(slC_r, wC, 128, xn, xn_r, s)
                norm_rope(bq, bq_r, bqs, bqs_r, V_KG + l, V_KGS + l, ts, [(ktmp[:, ts], ktmp_r)], tmps, rope_tabs)
            for tp in range(2):
                bk, bk_r = bank()
                for q4 in range(4):
                    tt = tp * 4 + q4
                    for kc in range(8):
                        P.op('pe', lambda e: e.matmul(
                            bk[:, q4 * 128:(q4 + 1) * 128], xn[:, kc, tt * 128:(tt + 1) * 128], wC[:, kc, 256:384],
                            start=(kc == 0), stop=(kc == 7)),
                            reads=[slC_r, xn_r], writes=[bk_r], inc=(kc == 7))
                P.op('act', lambda e: e.copy(out=vtmp[:, tp * 4:tp * 4 + 4, :, :], in_=bk.rearrange("p (a b c) -> p a b c", a=4, b=2)),
                     reads=[bk_r], writes=[vtmp_r], waw=False)
            sv = sndK[l].ap()
            P.dma('sp', sv[0:1024, :].rearrange("(p x) c -> p (x c)", p=128), ktmp, reads=[ktmp_r], writes=[sndk_r])
            vv = sv[1024:2048, :].rearrange("(t p) (g d) -> p t g d", p=128, g=2)
            for gg in range(2):
                P.dma('sp', vv[:, :, gg, :], vtmp[:, :, gg, :], reads=[vtmp_r], writes=[sndk_r])
            eg, eg_r = edges
            ebf, ebf_r = A.bf([128, 8], "ebf")
            P.op('act', lambda e: e.copy(out=ebf, in_=eg[:, 0, :, :].rearrange("p a b -> p (a b)")), reads=[eg_r], writes=[ebf_r])
            P.dma('sp', sv[2048:2056, :].rearrange("a (b x) -> (a b) x", x=8), ebf, reads=[ebf_r], writes=[sndk_r])
            A.release(m)

        def mixer_sample(l, xn, xn_r):
            ct, ct_r = sample_c['ct'], sample_c['ct_r']
            coef, coef_r = sample_c['coef'], sample_c['coef_r']
            snd_r, rcv_r, sndk_r, rcvk_r = Res("sndS"), Res("rcvS"), Res("sndK"), Res("rcvK")
            m = A.mark()
            oo, oo_r = A.bf([128, 8, T], "oo")
            orn = (oo[:, 0:4, :], Res("orn"))
            oatt = (oo[:, 4:8, :], Res("oatt"))
            for r_ in (orn[1], oatt[1]):
                _merge(r_.pr, oo_r.pr)
            cvo = A.bf([128, 4, T], "cvo")
            edges = A.f32([128, 3, 4, 2], "edges")
            ms_ = A.mark()
            seedf = A.f32([128, 4, 128], "seedf")
            seedb = A.f32([128, 4, 128], "seedb")
            m1 = A.mark()
            sums, sums_r = A.f32([128, 4, 2, 128], "sums")
            retention(l, 1, xn, xn_r, None, None, seeds=None, phase1=(sums, sums_r))
            sbf, sbf_r = A.bf([128, 8, 128], "sumsbf")
            P.op('act', lambda e: e.copy(out=sbf, in_=sums.rearrange("p a b c -> p (a b) c")), reads=[sums_r], writes=[sbf_r])
            P.dma('sp', sndS[l].ap().rearrange("(p x) e -> p x e", p=128), sbf, reads=[sbf_r], writes=[snd_r])
            for c_ in range(3):
                prefetch_panel(("conv", l, c_), [(0, 8, 384, wsrc(w_inx[l], 8, X_CONV + c_ * 384, 384))])
            all_gather(sndS[l], rcvS[l], snd_r, rcv_r)
            conv_stage(l, 1, xn, xn_r, cvo[0], cvo[1], edges=edges)
            kv_early(l, xn, xn_r, edges, sndk_r)
            for hh_ in range(2):
                prefetch_panel(("ret", l, hh_, True), [(0, 8, 512, wsrc(w_inx[l], 8, X_RET + hh_ * 512, 512))])
            all_gather(sndK[l], rcvK[l], sndk_r, rcvk_r)
            rS, rS_r = A.bf([128, NCORES, 8, 128], "rS")
            rv = rcvS[l].ap().rearrange("(q p x) e -> q p x e", q=NCORES, p=128)
            for q in range(NCORES):
                P.dma('sp', rS[:, q, :, :], rv[q], reads=[rcv_r], writes=[rS_r])
            s0, s0_r = A.f32([128, 2, 4, 128], "s0")
            for d_ in range(2):
                P.dma('sp', s0[:, d_, :, :], sret_d[l, d_].rearrange("h d e -> d h e"), writes=[s0_r])
            for d_, (sd, sd_r) in enumerate((seedf, seedb)):
                for hh in range(4):
                    lh = l * 4 + hh
                    P.op('dve', lambda e, d_=d_, hh=hh, lh=lh, sd=sd: e.tensor_scalar(
                        out=sd[:, hh, :], in0=s0[:, d_, hh, :], scalar1=coef[:, lh, d_, 0:1], scalar2=None, op0=ALU.mult),
                        reads=[s0_r, coef_r], writes=[sd_r], waw=False)
                    for q in range(NCORES):
                        P.op('dve', lambda e, d_=d_, hh=hh, lh=lh, sd=sd, q=q: e.scalar_tensor_tensor(
                            out=sd[:, hh, :], in0=rS[:, q, hh * 2 + d_, :], scalar=coef[:, lh, d_, 1 + q:2 + q], in1=sd[:, hh, :],
                            op0=ALU.mult, op1=ALU.add), reads=[rS_r, coef_r, sd_r], writes=[sd_r], waw=False)
            A.release(m1)
            retention(l, 1, xn, xn_r, orn[0], orn[1], seeds=[seedf, seedb])
            A.release(ms_)


            def exch(KT, KT_r, Vx, Vx_r):
                eg, eg_r = edges
                m2 = A.mark()
                rk = rcvK[l].ap().rearrange("(q r) c -> q r c", q=NCORES)
                pk, pk_r = A.f32([128, 4, 128], "pk")
                P.dma('sp', pk, ck_d[l].rearrange("(t p) f -> p t f", p=128), writes=[pk_r])
                bk, bk_r = bank()
                for tt in range(4):
                    P.op('pe', lambda e, bk=bk, tt=tt: e.transpose(bk[:, tt * 128:(tt + 1) * 128], pk[:, tt, :], ident),
                         reads=[pk_r, ident_r], writes=[bk_r])
                P.op('act', lambda e, bk=bk: e.copy(out=KT[:, 0:PAST], in_=bk), reads=[bk_r], writes=[KT_r], waw=False)
                pv, pv_r = A.f32([128, 4, 128], "pv")
                P.dma('sp', pv, cv_d[l].rearrange("(t p) f -> p t f", p=128), writes=[pv_r])
                P.op('dve', lambda e: e.tensor_copy(out=Vx[:, 0:4, :, 0:64], in_=pv.rearrange("p t (g d) -> p t g d", g=2)),
                     reads=[pv_r], writes=[Vx_r], waw=False)
                kst, kst_r = A.bf([128, 4 * T], "kst")
                vst, vst_r = A.bf([128, 32, 2, 64], "vst")
                for q in range(4):
                    P.dma('sp', KT[:, PAST + q * T:PAST + (q + 1) * T], rk[q, 0:1024, :].rearrange("(p x) c -> p (x c)", p=128),
                          reads=[rcvk_r], writes=[KT_r])
                    P.dma('sp', kst[:, q * T:(q + 1) * T], rk[4 + q, 0:1024, :].rearrange("(p x) c -> p (x c)", p=128),
                          reads=[rcvk_r], writes=[kst_r])
                    for gg in range(2):
                        P.dma('sp', Vx[:, 4 + q * 8:12 + q * 8, gg, 0:64],
                              rk[q, 1024:2048, :].rearrange("(t p) (g d) -> p t g d", p=128, g=2)[:, :, gg, :],
                              reads=[rcvk_r], writes=[Vx_r])
                        P.dma('sp', vst[:, q * 8:(q + 1) * 8, gg, :],
                              rk[4 + q, 1024:2048, :].rearrange("(t p) (g d) -> p t g d", p=128, g=2)[:, :, gg, :],
                              reads=[rcvk_r], writes=[vst_r])
                kd = KT[:, PAST:PAST + 4 * T]
                P.op('dve', lambda e: e.tensor_scalar(out=kd, in0=kd, scalar1=ct[:, 0:1], scalar2=None, op0=ALU.mult),
                     reads=[KT_r, ct_r], writes=[KT_r])
                P.op('dve', lambda e: e.scalar_tensor_tensor(out=kd, in0=kst, scalar=ct[:, 1:2], in1=kd, op0=ALU.mult, op1=ALU.add),
                     reads=[kst_r, KT_r, ct_r], writes=[KT_r])
                vd = Vx[:, 4:36, :, 0:64]
                P.op('dve', lambda e: e.tensor_scalar(out=vd, in0=vd, scalar1=ct[:, 0:1], scalar2=None, op0=ALU.mult),
                     reads=[Vx_r, ct_r], writes=[Vx_r])
                P.op('dve', lambda e: e.scalar_tensor_tensor(out=vd, in0=vst, scalar=ct[:, 1:2], in1=vd, op0=ALU.mult, op1=ALU.add),
                     reads=[vst_r, Vx_r, ct_r], writes=[Vx_r])
                re_, re_r = A.bf([128, NCORES, 8], "redge")
                for q in range(NCORES):
                    P.dma('sp', re_[:, q, :], rk[q, 2048:2056, :].rearrange("a (b x) -> (a b) x", x=8), reads=[rcvk_r], writes=[re_r])
                hl, hl_r = A.f32([128, 2, 4], "hl")
                P.op('dve', lambda e: e.memset(hl, 0.0), writes=[hl_r])
                rev = re_.rearrange("p q (c e) -> p q c e", e=2)
                for q in range(NCORES):
                    P.op('dve', lambda e, q=q: e.scalar_tensor_tensor(
                        out=hl[:, 0, :], in0=rev[:, q, :, 1], scalar=ct[:, 38 + q:39 + q], in1=hl[:, 0, :], op0=ALU.mult, op1=ALU.add),
                        reads=[re_r, ct_r, hl_r], writes=[hl_r])
                    P.op('dve', lambda e, q=q: e.scalar_tensor_tensor(
                        out=hl[:, 1, :], in0=rev[:, q, :, 0], scalar=ct[:, 46 + q:47 + q], in1=hl[:, 1, :], op0=ALU.mult, op1=ALU.add),
                        reads=[re_r, ct_r, hl_r], writes=[hl_r])
                wr = V_CONV + l * 12
                for e_, wk, tpos in ((0, 0, 0), (1, 8, T - 1)):
                    P.op('dve', lambda e, e_=e_, wk=wk: e.tensor_tensor(out=hl[:, e_, :], in0=hl[:, e_, :], in1=vecs[:, wr + wk:wr + wk + 4], op=ALU.mult),
                         reads=[hl_r, vecs_r], writes=[hl_r])
                    P.op('dve', lambda e, e_=e_: e.tensor_tensor(out=hl[:, e_, :], in0=hl[:, e_, :], in1=eg[:, 1, :, e_], op=ALU.add),
                         reads=[hl_r, eg_r], writes=[hl_r])
                    P.op('dve', lambda e, e_=e_, tpos=tpos: e.tensor_tensor(out=cvo[0][:, :, tpos], in0=hl[:, e_, :], in1=eg[:, 2, :, e_], op=ALU.mult),
                         reads=[hl_r, eg_r], writes=[cvo[1]])
                A.release(m2)

            attention(l, 1, xn, xn_r, oatt[0], oatt[1], rope_tabs=sample_c['rope'], exch=exch, skip_kv=True)
            merge_stage(l, 1, xn, xn_r, [orn, oatt, cvo], (oo, oo_r))
            A.release(m)

        def mixer_stage(l, g, xn, xn_r):
            m = A.mark()
            oo, oo_r = A.bf([128, 8, T], "oo")
            orn = (oo[:, 0:4, :], Res("orn"))
            oatt = (oo[:, 4:8, :], Res("oatt"))
            for r_ in (orn[1], oatt[1]):
                _merge(r_.pr, oo_r.pr)
            cvo = A.bf([128, 4, T], "cvo")
            fl = debug or ('ret', 'att', 'conv', 'merge')
            for t_ in (orn, oatt, cvo):
                if debug:
                    P.op('dve', lambda e, t_=t_: e.memset(t_[0], 0.0), writes=[t_[1]])
            if 'ret' in fl:
                retention(l, g, xn, xn_r, orn[0], orn[1])
            if 'att' in fl:
                attention(l, g, xn, xn_r, oatt[0], oatt[1])
            if 'conv' in fl:
                conv_stage(l, g, xn, xn_r, cvo[0], cvo[1])
            if 'merge' in fl:
                merge_stage(l, g, xn, xn_r, [orn, oatt, cvo], (oo, oo_r))
            A.release(m)

        setup_eps()
        sqp = [A.bf([128, SUB], "sqp%d" % k_) for k_ in range(3)]
        base_mark = A.mark()

        groups = [0] + ([1] if enable_sample else [])
        for g in groups:
            ci = g
            if g == 1:
                setup_sample()
            load_x(g)
            if g == 0:
                compute_mods_all()
            for l in range(DEPTH):
                use_mods(l, ci)
                m = A.mark()
                xn, xn_r = A.bf([128, 8, T], "xn")
                norm_stage(0, ci, xn, xn_r, have_ss=(l > 0))
                ffn_stage(l, 0, 0, xn, xn_r)
                norm_stage(1, ci, xn, xn_r)
                if g == 0:
                    mixer_stage(l, g, xn, xn_r)
                else:
                    mixer_sample(l, xn, xn_r)
                norm_stage(2, ci, xn, xn_r)
                ffn_stage(l, 1, 2, xn, xn_r)
                A.release(m)
            store_y(g)
        print("arena peak words", A.peak, "of", ARENA_WORDS, "waits", P.nwait)

        P.finish()
        P.emit()
    return nc, dbg_out


def _rope_tables(tok0, n):
    t = np.arange(tok0, tok0 + n)
    t_row = (t // 64).astype(np.float32)
    t_col = (t % 64).astype(np.float32)
    n_freq = 16
    inv = (np.float32(10000.0) ** (-np.arange(n_freq, dtype=np.float32) / np.float32(n_freq))).astype(np.float32)
    ang = np.concatenate([t_row[:, None] * inv, t_col[:, None] * inv], axis=-1).astype(np.float32)
    cos = np.cos(ang).astype(np.float32).T
    sin = np.sin(ang).astype(np.float32).T
    C = np.concatenate([cos, cos, cos, cos], axis=0)
    S = np.concatenate([-sin, sin, -sin, sin], axis=0)
    return np.stack([C, S], axis=0).astype(np.float32)


def _ctab(core):
    b, r = core // 4, core % 4
    t = np.zeros(64, np.float32)
    t[b] = 1.0
    for d_ in range(2):
        ex = np.zeros(9, np.float32)
        mk = np.zeros(9, np.float32)
        nprev = r if d_ == 0 else 3 - r
        ex[0] = 1024.0 * nprev
        mk[0] = 1.0
        for q in range(NCORES):
            if q // 4 != b:
                continue
            qr = q % 4
            dist = (r - 1 - qr) if d_ == 0 else (qr - r - 1)
            if dist >= 0:
                ex[1 + q] = 1024.0 * dist
                mk[1 + q] = 1.0
        t[2 + d_ * 9:2 + d_ * 9 + 9] = ex
        t[20 + d_ * 9:20 + d_ * 9 + 9] = mk
    if r > 0:
        t[38 + core - 1] = 1.0
    if r < 3:
        t[46 + core + 1] = 1.0
    return t


_CACHE = {}


def kernel(x_prompt, x_sample, c, state_ret, cache_k, cache_v, c_ctx, w_ada, b_ada, norm_w, w_ffn_in,
           w_ffn_out, w_in, ret_decay_logit, ret_gn, q_gain, k_gain, conv_w, w_ret_o, w_att_o, w_conv_o, w_o,
           _enable_sample=True, _debug=None, _ncores=NCORES):
    f32 = np.float32
    x_prompt = np.asarray(x_prompt, f32)
    x_sample = np.asarray(x_sample, f32)
    key = (_enable_sample, _debug)
    if key not in _CACHE:
        _CACHE[key] = build_program(enable_sample=_enable_sample, debug=_debug)
    nc, dbg = _CACHE[key]

    cols = build_win_cols()
    w_inx = np.ascontiguousarray(np.asarray(w_in, f32)[:, :, cols])
    rows = []
    for j in range(4):
        rows += list(range(j * 64, (j + 1) * 64)) + list(range((4 + j) * 64, (5 + j) * 64))
    w_att_o_p = np.ascontiguousarray(np.asarray(w_att_o, f32)[:, rows, :])

    qg = np.asarray(q_gain, f32)
    kg = np.asarray(k_gain, f32)
    sw = np.array(_sw(0))
    ident = np.eye(128, dtype=f32)
    ii = np.arange(128)
    mm_, cc_ = np.meshgrid(ii, ii, indexing='ij')
    posd = np.stack([np.maximum(cc_ - mm_, 0), np.maximum(mm_ - cc_, 0), (cc_ >= mm_), (cc_ < mm_)]).astype(f32)
    posv = np.stack([ii, 127 - ii, ii + 1, 128 - ii], axis=1).astype(f32)
    posr = np.stack([ii, 127 - ii, ii + 1, 128 - ii], axis=0).astype(f32)

    in_maps = []
    for core in range(_ncores):
        b, r = core // 4, core % 4
        vecs = np.zeros((NV, 128), f32)
        vecs[V_NORM:V_NORM + 48] = np.asarray(norm_w, f32).reshape(48, 128)
        vecs[V_GN:V_GN + 8] = np.asarray(ret_gn, f32).reshape(8, 128)
        vecs[V_CONV:V_CONV + 24] = np.asarray(conv_w, f32).reshape(24, 128)
        for l in range(DEPTH):
            vecs[V_QG + l] = np.tile(qg[l], 2)
            vecs[V_QGS + l] = np.tile(qg[l][sw], 2)
            vecs[V_KG + l] = np.tile(kg[l], 2)
            vecs[V_KGS + l] = np.tile(kg[l][sw], 2)
        vecs[V_COND:V_COND + 8] = np.asarray(c_ctx, f32).reshape(8, 128)
        vecs[V_COND + 8:V_COND + 16] = np.asarray(c, f32)[0].reshape(8, 128)
        vecs[V_COND + 16:V_COND + 24] = np.asarray(c, f32)[1].reshape(8, 128)
        m = {
            "xp": x_prompt[core * 4:(core + 1) * 4].reshape(T, D),
            "xs": x_sample[b, r * T:(r + 1) * T],
            "vecs": vecs,
            "bada": np.ascontiguousarray(np.asarray(b_ada, f32).reshape(DEPTH, 72, 128)[:, core * 9:(core + 1) * 9]).reshape(DEPTH * 9, 128),
            "decay": np.asarray(ret_decay_logit, f32).reshape(-1),
            "ident": ident, "posd": posd, "posv": posv, "posr": posr,
            "rope": _rope_tables(r * T, T),
            "ctab": _ctab(core),
            "sret": np.asarray(state_ret, f32)[b],
            "ck": np.asarray(cache_k, f32)[b].reshape(DEPTH, PAST, 128),
            "cv": np.asarray(cache_v, f32)[b].reshape(DEPTH, PAST, 128),
            "w_ada": np.ascontiguousarray(np.asarray(w_ada, f32)[:, :, core * 1152:(core + 1) * 1152]), "w_ffn_in": np.asarray(w_ffn_in, f32),
            "w_ffn_out": np.asarray(w_ffn_out, f32), "w_inx": w_inx,
            "w_ret_o": np.asarray(w_ret_o, f32), "w_att_o": w_att_o_p,
            "w_conv_o": np.asarray(w_conv_o, f32), "w_o": np.asarray(w_o, f32),
        }
        in_maps.append(m)

    res = run_bass_kernel_spmd(nc, in_maps, core_ids=list(range(_ncores)))
    R = res.results
    if _ncores < NCORES:
        return R
    y_prompt = np.stack([R[cix]["yp"].reshape(4, 256, D) for cix in range(NCORES)]).reshape(32, 256, D)
    y_sample = np.stack([R[cix]["ys"] for cix in range(NCORES)]).reshape(2, 4096, D)
    nstate = np.concatenate([R[cix]["nstate"] for cix in range(NCORES)], axis=0)
    nk = np.concatenate([R[cix]["nck"] for cix in range(NCORES)], axis=0).reshape(32, DEPTH, 256, 2, 64)
    nv = np.concatenate([R[cix]["ncv"] for cix in range(NCORES)], axis=0).reshape(32, DEPTH, 256, 2, 64)
    if _debug:
        kernel.dbg = [{k: R[cix]["dbg_" + k] for k in dbg} for cix in range(NCORES)]
    return (y_prompt.astype(f32), y_sample.astype(f32), nstate.astype(f32), nk.astype(f32), nv.astype(f32))
```
